# Optimizing a Trainium2 kernel written in Bass

```python
import jax, jax.numpy as jnp
from jax import lax
import numpy as np

D_MODEL = 1024
BATCH = 4
SEQ = 4096
DEPTH = 2
DEC_BATCH = 128
DEC_SEQ = 4
PAST_LEN = 8192
PAGE_SIZE = 128

A_WIDTH = D_MODEL // 4
CONV_W = 3
HEAD_DIM = 64
N_Q = (D_MODEL // 2) // HEAD_DIM
N_KV = 2
Q_PER_KV = N_Q // N_KV
B_WIDTH = N_Q * HEAD_DIM
KV_WIDTH = N_KV * HEAD_DIM
WINDOW = 128
CHUNK = 128
C_GROUPS = 4
C_WIDTH = D_MODEL // 4
C_GDIM = C_WIDTH // C_GROUPS
N_BRANCH = 3
D_FF = 2816
FFN_CONV_W = 3
EPS = 1e-6
NEG = -1e30

SPLITS = (A_WIDTH, A_WIDTH, A_WIDTH, B_WIDTH, KV_WIDTH, KV_WIDTH, C_WIDTH, C_WIDTH, N_BRANCH * D_MODEL)
PROJ_WIDTH = sum(SPLITS)
SPLIT_IDX = tuple(int(s) for s in np.cumsum(SPLITS)[:-1])

kernel_name = 'hybrid_gated_conv_swa_chunkmlp_step'


def rmsnorm(x, g):
    xf = x.astype(jnp.float32)
    y = xf * lax.rsqrt(jnp.mean(xf * xf, axis=-1, keepdims=True) + EPS)
    return (y * g.astype(jnp.float32)).astype(x.dtype)


def causal_dwconv(z, prev, w, b=None):
    width = w.shape[0]
    t = z.shape[1]
    full = jnp.concatenate([prev.astype(z.dtype), z], axis=1)
    y = sum(w[i] * full[:, i:i + t] for i in range(width))
    if b is not None:
        y = y + b
    return y, full[:, full.shape[1] - (width - 1):]


def sink_softmax(s, sinks):
    sk = sinks.astype(jnp.float32)[..., None, None]
    m = jnp.maximum(jnp.max(s, axis=-1, keepdims=True), sk)
    p = jnp.exp(s - m)
    return p / (jnp.sum(p, axis=-1, keepdims=True) + jnp.exp(sk - m))


def window_attn_prompt(q, k, v, sinks):
    bsz, s_len = q.shape[0], q.shape[1]
    nb = s_len // WINDOW
    qb = q.reshape(bsz, nb, WINDOW, N_KV, Q_PER_KV, HEAD_DIM)
    kb = k.reshape(bsz, nb, WINDOW, N_KV, HEAD_DIM)
    vb = v.reshape(bsz, nb, WINDOW, N_KV, HEAD_DIM)
    kk = jnp.concatenate([jnp.concatenate([jnp.zeros_like(kb[:, :1]), kb[:, :-1]], axis=1), kb], axis=2)
    vv = jnp.concatenate([jnp.concatenate([jnp.zeros_like(vb[:, :1]), vb[:, :-1]], axis=1), vb], axis=2)
    s = jnp.einsum('bnqhgd,bnkhd->bnhgqk', qb, kk, preferred_element_type=jnp.float32) * (HEAD_DIM ** -0.5)
    i = jnp.arange(WINDOW)[:, None]
    j = jnp.arange(2 * WINDOW)[None, :]
    diff = i + WINDOW - j
    band = (diff >= 0) & (diff < WINDOW)
    has_prev = jnp.arange(nb)[:, None, None] > 0
    mask = band[None] & (has_prev | (j >= WINDOW)[None])
    p = sink_softmax(jnp.where(mask[None, :, None, None], s, NEG), sinks)
    o = jnp.einsum('bnhgqk,bnkhd->bnqhgd', p.astype(vv.dtype), vv)
    rows = min(WINDOW, s_len)
    return o.reshape(bsz, s_len, B_WIDTH), k[:, s_len - rows:], v[:, s_len - rows:]


def window_attn_sample(q, k, v, ck, cv, sinks):
    bsz, t = q.shape[0], q.shape[1]
    r = ck.shape[1]
    kk = jnp.concatenate([ck.astype(k.dtype), k], axis=1)
    vv = jnp.concatenate([cv.astype(v.dtype), v], axis=1)
    qg = q.reshape(bsz, t, N_KV, Q_PER_KV, HEAD_DIM)
    s = jnp.einsum('bqhgd,bkhd->bhgqk', qg, kk, preferred_element_type=jnp.float32) * (HEAD_DIM ** -0.5)
    diff = jnp.arange(t)[:, None] + r - jnp.arange(r + t)[None, :]
    mask = (diff >= 0) & (diff < WINDOW)
    p = sink_softmax(jnp.where(mask, s, NEG), sinks)
    o = jnp.einsum('bhgqk,bkhd->bqhgd', p.astype(vv.dtype), vv)
    return o.reshape(bsz, t, B_WIDTH), kk[:, t:], vv[:, t:]


def chunk_spatial_gate(u, v, w_s, b_s):
    L = v.shape[2]
    tri = jnp.tril(jnp.ones((L, L), dtype=bool))
    w = jnp.where(tri, w_s[:, :L, :L], jnp.zeros((), w_s.dtype))
    mixed = jnp.einsum('gts,bnsgc->bntgc', w, v) + b_s[:, :L].T[None, None, :, :, None]
    return u * mixed


def hybrid_layer(x, conv_a_prev, win_k, win_v, ffn_prev,
                 w_in, conv_a_w, sinks, w_s, b_s, g_v, w_ba, w_bb, w_bc, w_o,
                 g_pre, g_post, g_pre_f, g_post_f, w_up, conv_f_w, conv_f_b, w_down):
    bsz, t, _ = x.shape
    h = rmsnorm(x, g_pre)
    p = h @ w_in
    a_in, a_b, a_c, q, k, v, c_u, c_v, gates = jnp.split(p, SPLIT_IDX, axis=-1)
    za, conv_a_new = causal_dwconv(a_c * a_in, conv_a_prev, conv_a_w)
    ya = a_b * za
    q = q.reshape(bsz, t, N_Q, HEAD_DIM)
    k = k.reshape(bsz, t, N_KV, HEAD_DIM)
    v = v.reshape(bsz, t, N_KV, HEAD_DIM)
    if win_k is None:
        yb, k_new, v_new = window_attn_prompt(q, k, v, sinks)
    else:
        yb, k_new, v_new = window_attn_sample(q, k, v, win_k, win_v, sinks)
    L = min(t, CHUNK)
    n = t // L
    vn = rmsnorm(c_v, g_v)
    yc = chunk_spatial_gate(c_u.reshape(bsz, n, L, C_GROUPS, C_GDIM),
                            vn.reshape(bsz, n, L, C_GROUPS, C_GDIM), w_s, b_s).reshape(bsz, t, C_WIDTH)
    g = jax.nn.sigmoid(gates.astype(jnp.float32)).astype(x.dtype).reshape(bsz, t, N_BRANCH, D_MODEL)
    merged = g[:, :, 0] * (ya @ w_ba) + g[:, :, 1] * (yb @ w_bb) + g[:, :, 2] * (yc @ w_bc)
    x = x + rmsnorm(merged @ w_o, g_post)
    up = rmsnorm(x, g_pre_f) @ w_up
    upc, ffn_new = causal_dwconv(up, ffn_prev, conv_f_w, conv_f_b)
    fa, fb = jnp.split(upc, 2, axis=-1)
    x = x + rmsnorm((jax.nn.silu(fa) * fb) @ w_down, g_post_f)
    return x, conv_a_new, k_new, v_new, ffn_new, vn


def setup_inputs(seed: int = 0) -> dict:
    key = jax.random.key(seed)
    ks = iter(jax.random.split(key, 32))
    f32 = jnp.float32
    nrm = lambda shape, scale: jax.random.normal(next(ks), shape, f32) * scale
    win_rows = min(WINDOW, PAST_LEN)
    return {
        'x_prompt': nrm((BATCH, SEQ, D_MODEL), 1.0),
        'x_sample': nrm((DEC_BATCH, DEC_SEQ, D_MODEL), 1.0),
        'state_conv_a': nrm((DEPTH, DEC_BATCH, CONV_W - 1, A_WIDTH), 1.0),
        'cache_win_k': nrm((DEPTH, DEC_BATCH, win_rows, N_KV, HEAD_DIM), 1.0),
        'cache_win_v': nrm((DEPTH, DEC_BATCH, win_rows, N_KV, HEAD_DIM), 1.0),
        'state_ffn_conv': nrm((DEPTH, DEC_BATCH, FFN_CONV_W - 1, 2 * D_FF), 1.0),
        'w_in': nrm((DEPTH, D_MODEL, PROJ_WIDTH), D_MODEL ** -0.5),
        'conv_a_w': nrm((DEPTH, CONV_W, A_WIDTH), CONV_W ** -0.5),
        'attn_sinks': nrm((DEPTH, N_KV, Q_PER_KV), 1.0),
        'spatial_w': nrm((DEPTH, C_GROUPS, CHUNK, CHUNK), CHUNK ** -0.5),
        'spatial_b': 1.0 + nrm((DEPTH, C_GROUPS, CHUNK), 0.1),
        'g_v': 1.0 + nrm((DEPTH, C_WIDTH), 0.05),
        'w_branch_a': nrm((DEPTH, A_WIDTH, D_MODEL), A_WIDTH ** -0.5),
        'w_branch_b': nrm((DEPTH, B_WIDTH, D_MODEL), B_WIDTH ** -0.5),
        'w_branch_c': nrm((DEPTH, C_WIDTH, D_MODEL), C_WIDTH ** -0.5),
        'w_out': nrm((DEPTH, D_MODEL, D_MODEL), D_MODEL ** -0.5),
        'g_pre_mix': 1.0 + nrm((DEPTH, D_MODEL), 0.05),
        'g_post_mix': 1.0 + nrm((DEPTH, D_MODEL), 0.05),
        'g_pre_ffn': 1.0 + nrm((DEPTH, D_MODEL), 0.05),
        'g_post_ffn': 1.0 + nrm((DEPTH, D_MODEL), 0.05),
        'w_up': nrm((DEPTH, D_MODEL, 2 * D_FF), D_MODEL ** -0.5),
        'conv_ffn_w': nrm((DEPTH, FFN_CONV_W, 2 * D_FF), FFN_CONV_W ** -0.5),
        'conv_ffn_b': nrm((DEPTH, 2 * D_FF), 0.01),
        'w_down': nrm((DEPTH, D_FF, D_MODEL), D_FF ** -0.5),
    }


def reference(x_prompt, x_sample, state_conv_a, cache_win_k, cache_win_v, state_ffn_conv,
              w_in, conv_a_w, attn_sinks, spatial_w, spatial_b, g_v, w_branch_a, w_branch_b,
              w_branch_c, w_out, g_pre_mix, g_post_mix, g_pre_ffn, g_post_ffn, w_up,
              conv_ffn_w, conv_ffn_b, w_down):
    yp, ys = x_prompt, x_sample
    bp = x_prompt.shape[0]
    ca_p, wk_p, wv_p, ff_p = [], [], [], []
    ca_s, wk_s, wv_s, ff_s, cv_s = [], [], [], [], []
    for l in range(DEPTH):
        wts = (w_in[l], conv_a_w[l], attn_sinks[l], spatial_w[l], spatial_b[l], g_v[l],
               w_branch_a[l], w_branch_b[l], w_branch_c[l], w_out[l], g_pre_mix[l], g_post_mix[l],
               g_pre_ffn[l], g_post_ffn[l], w_up[l], conv_ffn_w[l], conv_ffn_b[l], w_down[l])
        zeros_a = jnp.zeros((bp, CONV_W - 1, A_WIDTH), yp.dtype)
        zeros_f = jnp.zeros((bp, FFN_CONV_W - 1, 2 * D_FF), yp.dtype)
        yp, ca, k_new, v_new, ff, _ = hybrid_layer(yp, zeros_a, None, None, zeros_f, *wts)
        ca_p.append(ca); wk_p.append(k_new); wv_p.append(v_new); ff_p.append(ff)
        ys, ca, k_new, v_new, ff, vn = hybrid_layer(ys, state_conv_a[l], cache_win_k[l], cache_win_v[l],
                                                    state_ffn_conv[l], *wts)
        ca_s.append(ca); wk_s.append(k_new); wv_s.append(v_new); ff_s.append(ff); cv_s.append(vn)
    return (yp, ys,
            jnp.stack(ca_p), jnp.stack(wk_p), jnp.stack(wv_p), jnp.stack(ff_p),
            jnp.stack(ca_s), jnp.stack(wk_s), jnp.stack(wv_s), jnp.stack(ff_s), jnp.stack(cv_s))
```

```python
import numpy as np
from contextlib import ExitStack
import concourse.bass as bass
import concourse.mybir as mybir
from concourse.bass_utils import run_bass_kernel_spmd

F32 = mybir.dt.float32
BF16 = mybir.dt.bfloat16
I32 = mybir.dt.int32
ALU = mybir.AluOpType
AF = mybir.ActivationFunctionType
AX = mybir.AxisListType

ENGS = ("pe", "act", "dve", "pool", "sp")
N_LANES = 24
N_PLANES = 4
SAME_ENGINE_SYNC = True
FUSE_WAIT = True

L = 2
D = 1024
NBLK = 18
NS = 64
NSEQ = 16
TS_ = 4
DFF = 2816
NG = 32
NSLOT = 4
EPS = 1e-6
NEG = -1e30
TILES = [(0, 4), (4, 4), (8, 4), (12, 4), (16, 2)]


class Op:
    __slots__ = ("eng", "fn", "deps", "is_dma", "sem", "val", "milestone")

    def __init__(self, eng, fn, deps, is_dma=False):
        self.eng = eng
        self.fn = fn
        self.deps = deps
        self.is_dma = is_dma
        self.sem = None
        self.val = None
        self.milestone = False


class Res:
    __slots__ = ("w", "r")

    def __init__(self):
        self.w = None
        self.r = []


class Sched:
    def __init__(self, nc):
        self.nc = nc
        self.ops = {e: [] for e in ENGS}
        self.esem = {e: nc.alloc_semaphore("s_" + e) for e in ENGS}
        self.lanes = [nc.alloc_semaphore("s_dma%d" % i) for i in range(N_LANES + N_PLANES)]
        self.lane_cnt = [0] * (N_LANES + N_PLANES)
        self.lane_last = [None] * (N_LANES + N_PLANES)
        self.lane_rr = 0
        self.plane_rr = 0
        self.res = {}

    def _res(self, t):
        r = self.res.get(t)
        if r is None:
            r = self.res[t] = Res()
        return r

    def _collect(self, reads, writes):
        deps = []
        for t in reads:
            r = self._res(t)
            if r.w is not None:
                deps.append(r.w)
        for t in writes:
            r = self._res(t)
            if r.w is not None:
                deps.append(r.w)
            deps.extend(r.r)
        return deps

    def _update(self, op, reads, writes):
        for t in reads:
            self._res(t).r.append(op)
        for t in writes:
            r = self._res(t)
            r.w = op
            r.r = []

    @staticmethod
    def _excl(reads, writes):
        ex = [t for t in reads if isinstance(t, tuple) and t[0] == "ps"]
        if ex:
            reads = [t for t in reads if not (isinstance(t, tuple) and t[0] == "ps")]
            writes = list(writes) + ex
        return reads, writes

    def add(self, eng, fn, reads=(), writes=()):
        reads, writes = self._excl(reads, writes)
        deps = self._collect(reads, writes)
        op = Op(eng, fn, deps)
        self.ops[eng].append(op)
        self._update(op, reads, writes)
        return op

    def dma(self, out, in_, reads=(), writes=(), q="sp", **kw):
        reads, writes = self._excl(reads, writes)
        deps = self._collect(reads, writes)
        if q == "pool":
            lane = N_LANES + self.plane_rr
            self.plane_rr = (self.plane_rr + 1) % N_PLANES
        else:
            lane = self.lane_rr
            self.lane_rr = (self.lane_rr + 1) % N_LANES
        if self.lane_last[lane] is not None:
            deps.append(self.lane_last[lane])
        self.lane_cnt[lane] += 16

        def fn(e, out=out, in_=in_, kw=kw):
            return e.dma_start(out=out, in_=in_, **kw)

        op = Op(q, fn, deps, is_dma=True)
        op.sem = self.lanes[lane]
        op.val = self.lane_cnt[lane]
        self.lane_last[lane] = op
        self.ops[q].append(op)
        self._update(op, reads, writes)
        return op

    def barrier(self):
        deps = [op for op in self.lane_last if op is not None]
        for e in ENGS:
            for op in reversed(self.ops[e]):
                if not op.is_dma and op.fn is not None:
                    deps.append(op)
                    break
        for e in ENGS:
            self.ops[e].append(Op(e, None, list(deps)))

    def finalize(self, final_eng="sp"):
        nc = self.nc
        last_deps = [op for op in self.lane_last if op is not None]
        for e in ENGS:
            if e != final_eng:
                for op in reversed(self.ops[e]):
                    if not op.is_dma and op.fn is not None:
                        last_deps.append(op)
                        break
        self.ops[final_eng].append(Op(final_eng, None, last_deps))
        for e in ENGS:
            for op in self.ops[e]:
                for d in op.deps:
                    if d.is_dma:
                        continue
                    if d.eng != op.eng or (SAME_ENGINE_SYNC and d.eng != "pe"):
                        d.milestone = True
        for e in ENGS:
            c = 0
            for op in self.ops[e]:
                if op.is_dma:
                    continue
                if op.milestone:
                    c += 1
                    op.val = c
                    op.sem = self.esem[e]
        stats = {}
        with nc.Block() as block:
            def emit(e, eo):
                seen = {}
                nw = 0
                for op in self.ops[e]:
                    need = {}
                    for d in op.deps:
                        if not d.is_dma and d.eng == e and (not SAME_ENGINE_SYNC or e == "pe"):
                            continue
                        k = id(d.sem)
                        if seen.get(k, 0) >= d.val:
                            continue
                        if k not in need or need[k][1] < d.val:
                            need[k] = (d.sem, d.val)
                    waits = list(need.values())
                    for k, (sem, val) in need.items():
                        seen[k] = val
                    fused = None
                    if FUSE_WAIT and waits and op.fn is not None and not op.is_dma:
                        fused = waits.pop()
                    for (sem, val) in waits:
                        eo.wait_ge(sem, val)
                        nw += 1
                    if op.fn is None:
                        continue
                    ins = op.fn(eo)
                    if fused is not None:
                        ins._wait_ge(fused[0], fused[1])
                    if op.is_dma:
                        ins.then_inc(op.sem, 16)
                    elif op.milestone:
                        ins.then_inc(op.sem, 1)
                stats[e] = (len(self.ops[e]), nw)

            @block.tensor
            def _(eo):
                emit("pe", eo)

            @block.scalar
            def _(eo):
                emit("act", eo)

            @block.vector
            def _(eo):
                emit("dve", eo)

            @block.gpsimd
            def _(eo):
                emit("pool", eo)

            @block.sync
            def _(eo):
                emit("sp", eo)
        return stats


class Seg:
    def __init__(self, ti, sample, b0, nb):
        self.ti = ti
        self.sample = sample
        self.b0 = b0
        self.nb = nb
        self.N = NS if sample else nb * 128
        self.nseq = NSEQ if sample else 1
        self.T = TS_ if sample else self.N
        self.nbt = 1 if sample else nb
        self.first = (not sample) and b0 == 0
        self.last = (not sample) and (b0 + nb == NBLK)


class _Stop(Exception):
    pass


DEBUG_STOP = None
NOCONV = False
ADD_ENG = 'dve'
NCONV = None
LOADMODE = 0


def _chk(stage):
    if DEBUG_STOP == stage:
        raise _Stop()


def build_program():
    nc = bass.Bass("TRN2", target_bir_lowering=False)

    def din(name, shape):
        return nc.dram_tensor(name, list(shape), F32, kind="ExternalInput").ap()

    def dout(name, shape):
        return nc.dram_tensor(name, list(shape), F32, kind="ExternalOutput").ap()

    xp_d = din("xp", [NBLK * 128, D])
    xs_d = din("xs", [NS, D])
    sca_d = din("sca", [L, NSEQ, 2, 256])
    ck_d = din("ck", [L, NSEQ, 128, 128])
    cv_d = din("cv", [L, NSEQ, 128, 128])
    sff_d = din("sff", [L, NSEQ, 2, 2 * DFF])
    w_in_d = din("w_in", [L, D, 5120])
    caw_d = din("conv_a_w", [L, 3, 256])
    sinks_d = din("attn_sinks", [L, 2, 4])
    spw_d = din("spatial_w", [L, 4, 128, 128])
    spb_d = din("spatial_b", [L, 4, 128])
    gv_d = din("g_v", [L, 256])
    wba_d = din("w_branch_a", [L, 256, D])
    wbb_d = din("w_branch_b", [L, 512, D])
    wbc_d = din("w_branch_c", [L, 256, D])
    wo_d = din("w_out", [L, D, D])
    g_d = [din(n, [L, D]) for n in ("g_pre_mix", "g_post_mix", "g_pre_ffn", "g_post_ffn")]
    wup_d = din("w_up", [L, D, 2 * DFF])
    cfw_d = din("conv_ffn_w", [L, 3, 2 * DFF])
    cfb_d = din("conv_ffn_b", [L, 2 * DFF])
    wdn_d = din("w_down", [L, DFF, D])

    yp_d = dout("yp", [NBLK * 128, D])
    ys_d = dout("ys", [NS, D])
    cap_d = dout("ca_p", [L, 2, 256])
    wkp_d = dout("wk_p", [L, 128, 128])
    wvp_d = dout("wv_p", [L, 128, 128])
    ffp_d = dout("ff_p", [L, 2, 2 * DFF])
    cas_d = dout("ca_s", [L, NSEQ, 2, 256])
    wks_d = dout("wk_s", [L, NSEQ, 128, 128])
    wvs_d = dout("wv_s", [L, NSEQ, 128, 128])
    ffs_d = dout("ff_s", [L, NSEQ, 2, 2 * DFF])
    cvs_d = dout("cv_s", [L, NSEQ, TS_, 256])

    wbf_d = nc.dram_tensor("wbf", [L * NG, 128, 4096], BF16).ap()
    kvscr_d = nc.dram_tensor("kvscr", [L, NS, 256], F32).ap()

    es = ExitStack()

    def sb(name, shape, dt):
        return es.enter_context(nc.sbuf_tensor(name, list(shape), dt))

    def psum(name, shape, dt):
        return es.enter_context(nc.psum_tensor(name, list(shape), dt))

    S = Sched(nc)

    ring = sb("ring", [128, NSLOT, 4096], BF16)
    xT = sb("xT", [128, 8, 512], F32)
    xin = sb("xin", [128, 1, 1024], F32)
    xout = xin
    h = sb("h", [128, 8, 512], BF16)
    rstd = sb("rstd", [128, 512], F32)
    zfull = sb("zfull", [128, 2, 520], F32)
    a_b = sb("a_b", [128, 2, 512], F32)
    c_u = sb("c_u", [128, 2, 512], F32)
    qT = sb("qT", [128, 4, 512], BF16)
    kT = sb("kT", [128, L, 640], BF16)
    vtok = sb("vtok", [128, L, 5, 128], BF16)
    vn = sb("vn", [128, 4, 256], BF16)
    kvo = sb("kvo", [128, 256], F32)
    vno = sb("vno", [128, 256], F32)
    junk = sb("junk", [128, 256], F32)
    ya = sb("ya", [128, 2, 512], BF16)
    yb = sb("yb", [128, 4, 512], BF16)
    yc = sb("yc", [128, 2, 512], BF16)
    ct = sb("ct", [128, 4, 512], F32)
    gate = sb("gate", [128, 1, 4, 512], BF16)
    mo = sb("mo", [128, 8, 512], F32)
    mbf = sb("mbf", [128, 8, 512], BF16)
    tmp = sb("tmp", [128, 2, 512], F32)
    a_in = tmp
    act_raw = sb("act_raw", [128, 5632], F32)
    ust = sb("ust", [128, 4, 520], F32)
    prev_p = sb("prev_p", [128, L, 44, 2], F32)
    prev_s = sb("prev_s", [128, 44, 32], F32)
    zc_p = sb("zc_p", [128, L, 2, 2], F32)
    zst_s = sb("zst_s", [128, L, 2, 32], F32)
    zout_s = sb("zout_s", [128, 2, 32], F32)
    sm = sb("sm", [128, 2, 256], F32)
    Pb = sb("Pb", [128, 2, 264], BF16)
    PTs = sb("PTs", [128, 2, 256], BF16)
    st4 = sb("st4", [128, 2, 8], F32)
    rs4 = sb("rs4", [128, 4], F32)
    dmy = sb("dmy", [128, 4], F32)
    ident_f = sb("ident_f", [128, 128], F32)
    ident_b = sb("ident_b", [128, 128], BF16)
    ones_b = sb("ones_b", [128, 128], BF16)
    maskb = sb("maskb", [128, 2, 256], BF16)
    sinkb = sb("sinkb", [128, L, 8], BF16)
    mask_s = sb("mask_s", [128, 132], F32)
    gall = sb("gall", [128, 8, 8], F32)
    caw = sb("caw", [128, 2, 6], F32)
    cfa = sb("cfa", [128, 44, 8], F32)
    sink = sb("sink", [128, L, 8], F32)
    nsink = sb("nsink", [128, L, 8], F32)
    sinkc = sb("sinkc", [128, L], F32)
    nsinkc = sb("nsinkc", [128, L], F32)
    gvb = sb("gvb", [128, L, 256], F32)
    WsT = sb("WsT", [128, L, 4, 128], BF16)
    WsT_f = tmp[:, 0, :]
    WbdF = tmp[0:64, 1, :].rearrange("p (a b) -> p a b", a=L * 4)
    Wbd = sb("Wbd", [64, L * 4, 64], BF16)
    brow = sb("brow", [128, L * 4, 128], BF16)
    brow_s = sb("brow_s", [64, L * 4, 64], BF16)
    bst = ct[:, 0:2, :].rearrange("p a b -> p (a b)")
    bst2 = ct[:, 2:4, :].rearrange("p a b -> p (a b)")
    bstb = mbf[:, 0:2, :].rearrange("p a b -> p (a b)")
    ipi = sb("ipi", [128, 2], I32)
    ipf = sb("ipf", [128, 4], F32)
    iotaj = sb("iotaj", [128, 132], F32)
    cvs = sb("cvs", [128, 4, 1024], F32)

    ps = [psum("ps%d" % i, [128, 512], F32) for i in range(7)]
    psT = psum("psT", [128, 1024], BF16)
    PT_ALL = [("ps", 7)]

    act_bf = act_raw[:, :].bitcast(BF16)
    act_p = act_bf.rearrange("p (c n) -> p c n", c=22)
    act_s = act_raw[:, 0:704].bitcast(BF16).rearrange("p (c n) -> p c n", c=22)
    o0 = 704
    sstage = act_raw[:, o0:o0 + 2048].rearrange("p (a s f) -> p a s f", a=2, s=8)
    KT_all = act_raw[:, o0 + 2048:o0 + 3104].bitcast(BF16).rearrange("p (s k) -> p s k", s=16)
    Vc_bf = act_raw[:, o0 + 3104:o0 + 4128].bitcast(BF16).rearrange("p (s k) -> p s k", s=16)
    qbd = act_raw[:, o0 + 4128:o0 + 4384].bitcast(BF16).rearrange("p (s k) -> p s k", s=16)
    PT_all = act_raw[:, o0 + 4384:o0 + 4640].bitcast(BF16)
    PTn = act_raw[:, o0 + 4640:o0 + 4896].bitcast(BF16)
    vnt_f = ct[0:4, :, :].rearrange("p a (b f) -> p (a b) f", f=128)
    vnt = mbf[0:4, 0:4, :].rearrange("p a (b f) -> p (a b) f", f=128)

    def mm(out, lhsT, rhs, start, stop, reads, writes, **kw):
        S.add("pe", lambda e: e.matmul(out, lhsT=lhsT, rhs=rhs, start=start, stop=stop, **kw), reads, writes)

    def tr(out, in_, ident, reads, writes):
        S.add("pe", lambda e: e.transpose(out, in_, ident), reads, writes)

    def ACT(out, in_, func, reads, writes, **kw):
        S.add("act", lambda e: e.activation(out=out, in_=in_, func=func, **kw), reads, writes)

    def CP(eng, out, in_, reads, writes):
        if eng == "act":
            S.add("act", lambda e: e.copy(out=out, in_=in_), reads, writes)
        else:
            S.add(eng, lambda e: e.tensor_copy(out=out, in_=in_), reads, writes)

    def TSC(eng, out, in0, s1, s2, op0, op1, reads, writes, **kw):
        if op1 is None:
            S.add(eng, lambda e: e.tensor_scalar(out=out, in0=in0, scalar1=s1, scalar2=None, op0=op0, **kw), reads, writes)
        else:
            S.add(eng, lambda e: e.tensor_scalar(out=out, in0=in0, scalar1=s1, scalar2=s2, op0=op0, op1=op1, **kw), reads, writes)

    def TT(eng, out, in0, in1, op, reads, writes):
        S.add(eng, lambda e: e.tensor_tensor(out=out, in0=in0, in1=in1, op=op), reads, writes)

    def STT(out, in0, scalar, in1, op0, op1, reads, writes):
        S.add("dve", lambda e: e.scalar_tensor_tensor(out=out, in0=in0, scalar=scalar, in1=in1, op0=op0, op1=op1), reads, writes)

    def WARM(func):
        ACT(dmy[:, 1:2], dmy[:, 0:1], func, ["dmy0"], ["dmy1"])

    def MEMSET(eng, ap, val, writes):
        S.add(eng, lambda e: e.memset(ap, val), (), writes)

    rr = {"mm": 0, "at": 0, "cp": 0, "mmn": 4}

    def mm_next():
        b = rr["mm"] % rr["mmn"]
        rr["mm"] = (b + 1) % rr["mmn"]
        return b

    def at_next():
        b = 5 + rr["at"]
        rr["at"] = (rr["at"] + 1) % 2
        return b

    def cp_eng():
        rr["cp"] = (rr["cp"] + 1) % 2
        return ("act", "dve")[rr["cp"]]

    ACT_ALL = [("act", c) for c in range(22)]
    MO_ALL = [("mo", m) for m in range(8)]

    MEMSET("pool", ident_f[:], 1.0, ["ident_f"])
    S.add("pool", lambda e: e.affine_select(out=ident_f[:], in_=ident_f[:], pattern=[[1, 128]], base=0, channel_multiplier=-1,
                                             compare_op=ALU.is_equal, fill=0.0), ["ident_f"], ["ident_f"])
    CP("dve", ident_b[:], ident_f[:], ["ident_f"], ["ident_b"])
    MEMSET("dve", ones_b[:], 1.0, ["ones"])
    MEMSET("dve", dmy[:], 1.0, ["dmy0", "dmy1"])
    MEMSET("pool", maskb[:, 0, :], 0.0, ["maskb"])
    S.add("pool", lambda e: e.affine_select(out=maskb[:, 0, :], in_=maskb[:, 0, :], pattern=[[1, 256]], base=-1, channel_multiplier=-1,
                                             compare_op=ALU.is_ge, fill=NEG), ["maskb"], ["maskb"])
    S.add("pool", lambda e: e.affine_select(out=maskb[:, 0, :], in_=maskb[:, 0, :], pattern=[[-1, 256]], base=128, channel_multiplier=1,
                                             compare_op=ALU.is_ge, fill=NEG), ["maskb"], ["maskb"])
    CP("pool", maskb[:, 1, :], maskb[:, 0, :], ["maskb"], ["maskb"])
    MEMSET("pool", maskb[:, 1, 0:128], NEG, ["maskb"])
    S.add("pool", lambda e: e.iota(ipi[:, 0:1], pattern=[[0, 1]], base=0, channel_multiplier=1), (), ["ipi"])
    S.add("dve", lambda e: e.tensor_single_scalar(out=ipi[:, 1:2], in_=ipi[:, 0:1], scalar=3, op=ALU.bitwise_and), ["ipi"], ["ipi1"])
    CP("dve", ipf[:, 0:1], ipi[:, 1:2], ["ipi1"], ["ipf0"])
    S.add("dve", lambda e: e.tensor_scalar(out=ipi[:, 1:2], in0=ipi[:, 0:1], scalar1=2, scalar2=7, op0=ALU.logical_shift_right,
                                            op1=ALU.bitwise_and), ["ipi", "ipf0"], ["ipi1"])
    CP("dve", ipf[:, 1:2], ipi[:, 1:2], ["ipi1"], ["ipf1"])
    S.add("pool", lambda e: e.iota(iotaj[:], pattern=[[1, 132]], base=0, channel_multiplier=0, allow_small_or_imprecise_dtypes=True), (), ["iotaj"])
    TSC("dve", iotaj[:], iotaj[:], ipf[:, 0:1], None, ALU.subtract, None, ["iotaj", "ipf0"], ["iotaj"])
    TSC("dve", mask_s[:], iotaj[:], 1.0, None, ALU.is_ge, None, ["iotaj"], ["mask_s"])
    TSC("dve", iotaj[:], iotaj[:], 128.0, None, ALU.is_le, None, ["mask_s"], ["iotaj"])
    TT("dve", mask_s[:], mask_s[:], iotaj[:], ALU.mult, ["iotaj", "mask_s"], ["mask_s"])
    TSC("dve", mask_s[:], mask_s[:], -1.0, -NEG, ALU.add, ALU.mult, ["mask_s"], ["mask_s"])
    for l in range(L):
        S.dma(sink[:, l, :], sinks_d[l].rearrange("k g -> (k g)").partition_broadcast(128), writes=[("sink", l)])
        TSC("dve", nsink[:, l, :], sink[:, l, :], -1.0, None, ALU.mult, None, [("sink", l)], [("nsink", l)])
        CP("dve", sinkb[:, l, :], sink[:, l, :], [("sink", l)], [("sinkb", l)])
        MEMSET("dve", sinkc[:, l:l + 1], 0.0, [("sinkc", l)])
        for X in range(4):
            for hk in range(2):
                TSC("dve", junk[:, 0:1], ipf[:, 1:2], float(X * 2 + hk), None, ALU.is_equal, None, ["ipf1", ("sinkc", l)], ["junk"])
                STT(sinkc[:, l:l + 1], junk[:, 0:1], sink[:, l, hk * 4 + X:hk * 4 + X + 1], sinkc[:, l:l + 1], ALU.mult, ALU.add,
                    ["junk", ("sink", l), ("sinkc", l)], [("sinkc", l)])
        TSC("dve", nsinkc[:, l:l + 1], sinkc[:, l:l + 1], -1.0, None, ALU.mult, None, [("sinkc", l)], [("nsinkc", l)])
        S.dma(gvb[:, l, :], gv_d[l].partition_broadcast(128), writes=[("gvb", l)])

    stage_rows = act_raw

    r2c_pending = []

    def rows_to_cols(row_srcs, R, W, dst, dst_tok, pbase=0, cbase=0):
        tok = [("stg", pbase, cbase)]
        for (r0, n, ap) in row_srcs:
            S.dma(stage_rows[pbase + r0:pbase + r0 + n, cbase:cbase + W], ap, writes=tok)

        def compute():
            nch = W // 128
            per = 512 // R
            for c0 in range(0, nch, per):
                bk = at_next()
                n = min(per, nch - c0)
                for c in range(c0, c0 + n):
                    tr(ps[bk][:, (c - c0) * R:(c - c0 + 1) * R], stage_rows[pbase:pbase + R, cbase + c * 128:cbase + (c + 1) * 128],
                       ident_f[pbase:pbase + R, pbase:pbase + R], tok + ["ident_f"], [("ps", bk)])
                CP(cp_eng(), dst[:, c0:c0 + n, :], ps[bk][:, 0:n * R].rearrange("p (c r) -> p c r", r=R), [("ps", bk)], [dst_tok])
        r2c_pending.append(compute)

    def r2c_flush():
        for f_ in r2c_pending:
            f_()
        del r2c_pending[:]

    def cols_to_rows(src_fn, R, W, dst_rows, src_toks):
        nch = W // 128
        for c0 in range(0, nch, 4):
            bk = at_next()
            n = min(4, nch - c0)
            for c in range(c0, c0 + n):
                tr(ps[bk][0:R, (c - c0) * 128:(c - c0 + 1) * 128], src_fn(c), ident_f[:, :], list(src_toks) + ["ident_f"], [("ps", bk)])
            CP(cp_eng(), stage_rows[0:R, c0 * 128:(c0 + n) * 128], ps[bk][0:R, 0:n * 128], [("ps", bk)], ACT_ALL)
        for (r0, n, ap) in dst_rows:
            S.dma(ap, stage_rows[r0:r0 + n, 0:W], reads=ACT_ALL)

    rows_to_cols([(l * 3, 3, cfw_d[l]) for l in range(L)] + [(6, 2, cfb_d)], 8, 2 * DFF, cfa[:], "cfa", pbase=0, cbase=0)
    rows_to_cols([(k * L, L, g_d[k]) for k in range(4)], 8, D, gall[:], "gall", pbase=32, cbase=0)
    rows_to_cols([(l * 3, 3, caw_d[l]) for l in range(L)], 6, 256, caw[:], "caw", pbase=32, cbase=1024)
    for l in range(L):
        rows_to_cols([(0, 32, sca_d[l].rearrange("s j f -> (s j) f"))], 32, 256, zst_s[:, l], ("zst_s", l), pbase=64, cbase=256 * l)
    r2c_flush()

    for l in range(L):
        S.dma(WsT_f.rearrange("p (g s) -> p g s", g=4), spw_d[l].rearrange("g t s -> t g s"), writes=["WsT_f"])
        bk = at_next()
        for g in range(4):
            tr(ps[bk][:, g * 128:(g + 1) * 128], WsT_f[:, g * 128:(g + 1) * 128], ident_f[:], ["WsT_f", "ident_f"], [("ps", bk)])
        CP("dve", WsT_f, ps[bk][:, :], [("ps", bk)], ["WsT_f"])
        S.add("pool", lambda e: e.affine_select(out=WsT_f.rearrange("p (g t) -> p g t", g=4), in_=WsT_f.rearrange("p (g t) -> p g t", g=4),
                                                 pattern=[[0, 4], [1, 128]], base=0, channel_multiplier=-1, compare_op=ALU.is_ge, fill=0.0),
              ["WsT_f"], ["WsT_f"])
        CP("dve", WsT[:, l, :, :], WsT_f.rearrange("p (g t) -> p g t", g=4), ["WsT_f"], [("WsT", l)])
    MEMSET("pool", WbdF, 0.0, ["WbdF"])
    for i in range(NSEQ):
        S.dma(WbdF[4 * i:4 * i + 4, :, 4 * i:4 * i + 4], spw_d[:, :, 0:4, 0:4].rearrange("l g t s -> t (l g) s"), writes=["WbdF"])
    bk = at_next()
    for a_ in range(L * 4):
        tr(ps[bk][0:64, a_ * 64:(a_ + 1) * 64], WbdF[:, a_, :], ident_f[0:64, 0:64], ["WbdF", "ident_f"], [("ps", bk)])
    CP("dve", WbdF, ps[bk][0:64, :].rearrange("p (a b) -> p a b", a=L * 4), [("ps", bk)], ["WbdF"])
    S.add("pool", lambda e: e.affine_select(out=WbdF, in_=WbdF, pattern=[[0, L * 4], [1, 64]], base=0, channel_multiplier=-1,
                                             compare_op=ALU.is_ge, fill=0.0), ["WbdF"], ["WbdF"])
    CP("dve", Wbd[:], WbdF, ["WbdF"], ["Wbd"])
    MEMSET("pool", brow[:], 0.0, ["brow"])
    MEMSET("pool", brow_s[:], 0.0, ["brow_s"])
    flat_b = spb_d.rearrange("l g t -> (l g t)")
    MEMSET("pool", bst[0:64, :], 0.0, ["bst"])
    for p0 in (0, 32):
        S.dma(bst[p0:p0 + 1, :], flat_b.partition_broadcast(1), writes=["bst"])
    CP("dve", bstb[0:64, :], bst[0:64, :], ["bst"], ["bstb"])
    TT("dve", bst2[0:64, :], bst[0:64, :], bstb[0:64, :], ALU.subtract, ["bst", "bstb"], ["bst2"])
    CP("dve", brow[0:1].rearrange("p a t -> p (a t)"), bstb[0:1, :], ["bstb", "brow"], ["brow"])
    CP("dve", brow[32:33].rearrange("p a t -> p (a t)"), bst2[32:33, :], ["bst2", "brow"], ["brow"])
    for sq_ in range(NSEQ):
        CP("dve", brow_s[0:1, :, 4 * sq_:4 * sq_ + 4], bstb[0:1, :].rearrange("p (a t) -> p a t", t=128)[:, :, 0:4], ["bstb", "brow_s"], ["brow_s"])
        CP("dve", brow_s[32:33, :, 4 * sq_:4 * sq_ + 4], bst2[32:33, :].rearrange("p (a t) -> p a t", t=128)[:, :, 0:4], ["bst2", "brow_s"], ["brow_s"])
    MEMSET("pool", kT[:], 0.0, [("kT", 0), ("kT", 1)])
    MEMSET("pool", vtok[:], 0.0, [("vtok", l, s) for l in range(L) for s in range(5)])
    MEMSET("pool", prev_p[:], 0.0, [("prev_p", l, c) for l in range(L) for c in range(44)])
    MEMSET("pool", zc_p[:], 0.0, [("zc_p", 0), ("zc_p", 1)])

    S.barrier()
    def w_in_src(l, c0, n):
        return w_in_d[l].rearrange("(kc p) n -> p kc n", p=128)[:, :, c0:c0 + n]

    def groups(l):
        gs = []
        gs.append(dict(K=8, cols=512, pieces=[(0, 8, 0, w_in_src(l, 0, 256)), (0, 8, 256, w_in_src(l, 512, 256))]))
        gs.append(dict(K=8, cols=512, pieces=[(0, 8, 0, w_in_src(l, 256, 256)), (0, 8, 256, w_in_src(l, 1536, 256))]))
        gs.append(dict(K=8, cols=512, pieces=[(0, 8, 0, w_in_src(l, 768, 512))], qperm=True))
        gs.append(dict(K=8, cols=512, pieces=[(0, 8, 0, w_in_src(l, 1280, 256)), (0, 8, 256, w_in_src(l, 1792, 256))]))
        for br, wd, kb in ((0, wba_d, 2), (2, wbc_d, 2), (1, wbb_d, 4)):
            gs.append(dict(K=8, cols=512, pieces=[(0, 8, 0, w_in_src(l, 2048 + br * 1024, 512))]))
            if br == 1:
                pb = []
                for X in range(4):
                    for hk in range(2):
                        hd = hk * 4 + X
                        pb.append((X, 1, 0, wd[l][hd * 64:(hd + 1) * 64, :].rearrange("p (k n) -> p k n", k=1), (hk * 64, 64)))
                gs.append(dict(K=kb, cols=1024, pieces=pb))
            else:
                gs.append(dict(K=kb, cols=1024, pieces=[(0, kb, 0, wd[l].rearrange("(kc p) n -> p kc n", p=128))]))
            gs.append(dict(K=8, cols=512, pieces=[(0, 8, 0, w_in_src(l, 2048 + br * 1024 + 512, 512))]))
        for hf in range(2):
            gs.append(dict(K=8, cols=512, pieces=[(0, 8, 0, wo_d[l].rearrange("(kc p) n -> p kc n", p=128)[:, :, hf * 512:(hf + 1) * 512])]))
        wu = wup_d[l].rearrange("(kc p) n -> p kc n", p=128)
        for gi in range(11):
            pcs = [(0, 8, 0, wu[:, :, gi * 256:(gi + 1) * 256]), (0, 8, 256, wu[:, :, (22 + 2 * gi) * 128:(24 + 2 * gi) * 128])]
            gs.append(dict(K=8, cols=512, pieces=pcs))
        wdn = wdn_d[l].rearrange("(kc p) n -> p kc n", p=128)
        for hf in range(2):
            for (k0, nk) in ((0, 8), (8, 8), (16, 6)):
                gs.append(dict(K=nk, cols=512, pieces=[(0, nk, 0, wdn[:, k0:k0 + nk, hf * 512:(hf + 1) * 512])]))
        assert len(gs) == NG
        return gs

    GROUPS = [groups(l) for l in range(L)]

    segs = [Seg(i, False, b0, nb) for i, (b0, nb) in enumerate(TILES)] + [Seg(len(TILES), True, 0, 0)]
    items = [(l, gi) for _ in segs for l in range(L) for gi in range(NG)]
    wstate = {"next_pref": 0, "next_get": 0}

    cvstate = {"n": 0}

    def convert_into_slot(l, gi, slot):
        g = GROUPS[l][gi]
        K, cols = g["K"], g["cols"]
        base, rem = divmod(K, 4)
        bounds = []
        k_ = 0
        for q_ in range(4):
            n_ = base + (1 if q_ < rem else 0)
            bounds.append((k_, k_ + n_))
            k_ += n_
        for q_, (ka, kb) in enumerate(bounds):
            wtok = [("ring", slot, t_) for t_ in range(q_, 4)]
            if kb > ka:
                b = cvstate["n"] % 4
                cvstate["n"] += 1
                n_el = (kb - ka) * cols
                stv = cvs[:, b, 0:n_el].rearrange("p (k n) -> p k n", k=kb - ka)
                npc = 0
                for pc in g["pieces"]:
                    k0, nk, c0, src = pc[:4]
                    p0, np_ = pc[4] if len(pc) > 4 else (0, 128)
                    lo, hi = max(k0, ka), min(k0 + nk, kb)
                    if lo >= hi:
                        continue
                    ncol = src.shape[-1]
                    S.dma(stv[p0:p0 + np_, lo - ka:hi - ka, c0:c0 + ncol], src[:, lo - k0:hi - k0, :], writes=[("cst", b, npc)])
                    npc += 1
                rtok = [("cst", b, pi) for pi in range(8)]
                if g.get("qperm"):
                    for kk in range(kb - ka):
                        src_v = cvs[:, b, kk * cols:(kk + 1) * cols].rearrange("p (hk g d) -> p g hk d", hk=2, g=4)
                        dst_v = ring[:, slot, (ka + kk) * cols:(ka + kk + 1) * cols].rearrange("p (g hk d) -> p g hk d", g=4, hk=2)
                        CP("act", dst_v, src_v, rtok, wtok)
                else:
                    CP("dve", ring[:, slot, ka * cols:kb * cols], cvs[:, b, 0:n_el], rtok, wtok)
            if q_ in (1, 3):
                sa, sb_ = bounds[q_ - 1][0], bounds[q_][1]
                if sb_ > sa:
                    S.dma(wbf_d[l * NG + gi][:, sa * cols:sb_ * cols], ring[:, slot, sa * cols:sb_ * cols],
                          reads=[("ring", slot, q_ - 1), ("ring", slot, q_)], writes=[("wbf", l, gi, q_ // 2)], q="pool")

    def prefetch_upto(k):
        while wstate["next_pref"] <= k and wstate["next_pref"] < len(items):
            i = wstate["next_pref"]
            l, gi = items[i]
            g = GROUPS[l][gi]
            used = g["K"] * g["cols"]
            slot = i % NSLOT
            if i < L * NG:
                convert_into_slot(l, gi, slot)
            else:
                S.dma(ring[:, slot, 0:used], wbf_d[l * NG + gi][:, 0:used], reads=[("wbf", l, gi, 0), ("wbf", l, gi, 1)],
                      writes=[("ring", slot, t_) for t_ in range(4)])
            wstate["next_pref"] += 1

    def wget(l_expect, gi_expect, hold=0):
        i = wstate["next_get"]
        l, gi = items[i]
        assert (l, gi) == (l_expect, gi_expect), (l, gi, l_expect, gi_expect)
        prefetch_upto(i + NSLOT - 1 - hold)
        wstate["next_get"] += 1
        g = GROUPS[l][gi]
        slot = i % NSLOT
        view = ring[:, slot, 0:g["K"] * g["cols"]].rearrange("p (k n) -> p k n", k=g["K"])
        return view, [("ring", slot, t_) for t_ in range(4)]

    def xtok(seg, c):
        return [("x", c, j) for j in range(seg.nbt)]

    STG = [(xin[:, 0, :], [("xin", 0)]),
           (tmp[:, :, :].rearrange("p a b -> p (a b)"), [("tmp", 0), ("tmp", 1)]),
           (ct[:, 0:2, :].rearrange("p a b -> p (a b)"), [("ct", 0), ("ct", 1)]),
           (ct[:, 2:4, :].rearrange("p a b -> p (a b)"), [("ct", 2), ("ct", 3)])]
    LOAD_STG = [0, 2, 3, 0]
    STORE_STG = [1, 2, 3, 1]

    def x_src(seg, j):
        return xs_d if seg.sample else xp_d[(seg.b0 + j) * 128:(seg.b0 + j + 1) * 128, :]

    def preload_x0(seg):
        R = 64 if seg.sample else 128
        buf, toks = STG[LOAD_STG[0]]
        S.dma(buf[0:R, :], x_src(seg, 0), writes=toks)

    def load_x(seg, skip_first_dma=False):
        for j in range(seg.nbt):
            R = 64 if seg.sample else 128
            buf, toks = STG[LOAD_STG[j]]
            if not (j == 0 and skip_first_dma):
                S.dma(buf[0:R, :], x_src(seg, j), writes=toks)
            for hf in range(2):
                bk = at_next()
                for cc in range(4):
                    c = hf * 4 + cc
                    tr(ps[bk][:, cc * R:(cc + 1) * R], buf[0:R, c * 128:(c + 1) * 128], ident_f[0:R, 0:R], toks + ["ident_f"], [("ps", bk)])
                CP(cp_eng(), xT[:, hf * 4:hf * 4 + 4, j * 128:j * 128 + R], ps[bk][:, 0:4 * R].rearrange("p (c t) -> p c t", c=4),
                   [("ps", bk)], [("x", hf * 4 + cc, j) for cc in range(4)])

    def store_x(seg):
        for j in range(seg.nbt):
            R = 64 if seg.sample else 128
            buf, toks = STG[STORE_STG[j]]
            for hf in range(2):
                bk = at_next()
                for cc in range(4):
                    c = hf * 4 + cc
                    tr(ps[bk][0:R, cc * 128:(cc + 1) * 128], xT[:, c, j * 128:j * 128 + R], ident_f[:, :], [("x", c, j), "ident_f"], [("ps", bk)])
                CP(cp_eng(), buf[0:R, hf * 512:(hf + 1) * 512], ps[bk][0:R, 0:512], [("ps", bk)], toks)
            dst = ys_d if seg.sample else yp_d[(seg.b0 + j) * 128:(seg.b0 + j + 1) * 128, :]
            S.dma(dst, buf[0:R, :], reads=toks)

    def norm_stats(seg, eps=EPS):
        N = seg.N
        for c in range(8):
            mm(ps[4][:, :N], ones_b[:, :], h[:, c, :N], c == 0, c == 7, [("h", c), "ones"], [("ps", 4)])
        ACT(rstd[:, :N], ps[4][:, :N], AF.Sqrt, [("ps", 4)], ["rstd"], scale=1.0 / D, bias=eps)
        S.add("dve", lambda e: e.reciprocal(out=rstd[:, :N], in_=rstd[:, :N]), ["rstd"], ["rstd"])

    def rmsnorm_to_h(seg, l, kind):
        N = seg.N
        for c in range(8):
            ACT(h[:, c, :N], xT[:, c, :N], AF.Square, xtok(seg, c), [("h", c)])
        norm_stats(seg)
        for c in range(8):
            STT(h[:, c, :N], xT[:, c, :N], gall[:, c, kind * L + l:kind * L + l + 1], rstd[:, :N], ALU.mult, ALU.mult,
                xtok(seg, c) + ["rstd", "gall"], [("h", c)])

    def post_norm_add(seg, l, kind):
        N = seg.N
        norm_stats(seg, 4.0 * EPS if kind == 1 else EPS)
        for m in range(8):
            STT(tmp[:, m % 2, :N], mo[:, m, :N], gall[:, m, kind * L + l:kind * L + l + 1], rstd[:, :N], ALU.mult, ALU.mult,
                [("mo", m), "rstd", "gall"], [("tmp", m % 2)])
            TT(ADD_ENG, xT[:, m, :N], xT[:, m, :N], tmp[:, m % 2, :N], ALU.add, xtok(seg, m) + [("tmp", m % 2)], xtok(seg, m))

    def v3(ap2d, seg, off=0):
        return ap2d[:, 0:seg.nseq * (seg.T + off)].rearrange("p (s t) -> p s t", s=seg.nseq)

    def out_proj_evac(seg, bk, m):
        N = seg.N
        CP("dve", mo[:, m, :N], ps[bk][:, :N], [("ps", bk)], [("mo", m)])
        ACT(h[:, m, :N], ps[bk][:, :N], AF.Square, [("ps", bk)], [("h", m)])

    def mixer(seg, l):
        N, T, nseq = seg.N, seg.T, seg.nseq
        rmsnorm_to_h(seg, l, 0)
        _chk("norm")
        hr = lambda kc: [("h", kc)]

        def proj_chunk(wv, wt, K, col0, rhs, rreads, evac):
            bk = mm_next()
            for kc in range(K):
                mm(ps[bk][:, :N], wv[:, kc, col0:col0 + 128], rhs(kc), kc == 0, kc == K - 1, wt + rreads(kc), [("ps", bk)])
            evac(bk)

        hrhs = lambda kc: h[:, kc, :N]
        wv, wt = wget(l, 0)
        for c in range(2):
            proj_chunk(wv, wt, 8, c * 128, hrhs, hr, lambda bk, c=c: CP("act", a_in[:, c, :N], ps[bk][:, :N], [("ps", bk)], [("tmp", c)]))
        for c in range(2):
            def ev(bk, c=c):
                zv = v3(zfull[:, c, :], seg, 2)
                TT("dve", zv[:, :, 2:2 + T], v3(ps[bk], seg), v3(a_in[:, c, :], seg), ALU.mult, [("ps", bk), ("tmp", c)], [("z", c)])
            proj_chunk(wv, wt, 8, 256 + c * 128, hrhs, hr, ev)
        wv, wt = wget(l, 1)
        for c in range(2):
            proj_chunk(wv, wt, 8, c * 128, hrhs, hr, lambda bk, c=c: CP("act", a_b[:, c, :N], ps[bk][:, :N], [("ps", bk)], [("a_b", c)]))
        for c in range(2):
            proj_chunk(wv, wt, 8, 256 + c * 128, hrhs, hr, lambda bk, c=c: CP("act", c_u[:, c, :N], ps[bk][:, :N], [("ps", bk)], [("c_u", c)]))
        if seg.sample:
            attn_sample_prep(seg, l)
        wv, wt = wget(l, 2)
        for X in range(4):
            proj_chunk(wv, wt, 8, X * 128, hrhs, hr, lambda bk, X=X: ACT(qT[:, X, :N], ps[bk][:, :N], AF.Copy, [("ps", bk)], [("q", X)], scale=0.125))
        wv, wt = wget(l, 3)
        if seg.sample:
            proj_chunk(wv, wt, 8, 0, hrhs, hr, lambda bk: CP("act", KT_all[:, :, 128:132], v3(ps[bk], seg), [("ps", bk)], ["KT_new"]))
        else:
            proj_chunk(wv, wt, 8, 0, hrhs, hr, lambda bk: CP("act", kT[:, l, 128:128 + N], ps[bk][:, :N], [("ps", bk)], [("kT", l)]))
        for j in range(seg.nbt):
            R = 64 if seg.sample else 128
            bk = at_next()
            c_lo = 0 if (seg.sample or (seg.last and j == seg.nb - 1)) else 128
            for kc in range(8):
                mm(ps[bk][0:R, c_lo:512], h[:, kc, j * 128:j * 128 + R], wv[:, kc, c_lo:512], kc == 0, kc == 7, wt + [("h", kc)], [("ps", bk)])
            if not seg.sample:
                CP("act", vtok[:, l, 1 + j, :], ps[bk][:, 128:256], [("ps", bk)], [("vtok", l, 1 + j)])
            ACT(junk[0:R, :], ps[bk][0:R, 256:512], AF.Square, [("ps", bk)], ["junk", ("st4", 0)], accum_out=st4[0:R, 0, 0:1])
            ACT(st4[0:R, 0, 0:1], st4[0:R, 0, 0:1], AF.Sqrt, [("st4", 0)], [("st4", 0)], scale=1.0 / 256, bias=EPS)
            S.add("dve", lambda e, R=R: e.reciprocal(out=st4[0:R, 0, 0:1], in_=st4[0:R, 0, 0:1]), [("st4", 0)], [("st4", 0)])
            if seg.sample:
                STT(vno[0:R, :], ps[bk][0:R, 256:512], st4[0:R, 0, 0:1], gvb[0:R, l, :], ALU.mult, ALU.mult,
                    [("ps", bk), ("st4", 0), ("gvb", l)], ["vno"])
                CP("dve", vn[0:R, 0, :], vno[0:R, :], ["vno"], [("vn", 0)])
                S.dma(cvs_d[l].rearrange("s t f -> (s t) f"), vno[0:R, :], reads=["vno"])
                CP("act", kvo[0:R, :], ps[bk][0:R, 0:256], [("ps", bk)], ["kvo"])
                S.dma(kvscr_d[l], kvo[0:R, :], reads=["kvo"], writes=[("kvscr", l)])
                S.dma(wks_d[l, :, 124:128, :], kvscr_d[l][:, 0:128].rearrange("(s t) f -> s t f", t=4), reads=[("kvscr", l)])
                S.dma(wvs_d[l, :, 124:128, :], kvscr_d[l][:, 128:256].rearrange("(s t) f -> s t f", t=4), reads=[("kvscr", l)])
                S.dma(vnt_f, kvscr_d[l][:, 128:256].rearrange("(s t) f -> t s f", t=4), reads=[("kvscr", l)], writes=["vnt_f"] + [("ct", c) for c in range(4)])
                CP("dve", vnt, vnt_f, ["vnt_f"] + [("ct", c) for c in range(4)] + [("mbf", m) for m in range(4)], ["vnt"] + [("mbf", m) for m in range(4)])
                S.dma(wks_d[l, :, 0:124, :], ck_d[l, :, 4:128, :])
                S.dma(wvs_d[l, :, 0:124, :], cv_d[l, :, 4:128, :])
            else:
                STT(vn[:, j, :], ps[bk][:, 256:512], st4[:, 0, 0:1], gvb[:, l, :], ALU.mult, ALU.mult,
                    [("ps", bk), ("st4", 0), ("gvb", l)], [("vn", j)])
                if seg.last and j == seg.nb - 1:
                    CP("act", kvo[:, :], ps[bk][:, 0:256], [("ps", bk)], ["kvo"])
                    S.dma(wkp_d[l], kvo[:, 0:128], reads=["kvo"])
                    S.dma(wvp_d[l], kvo[:, 128:256], reads=["kvo"])

        _chk("proj")
        for c in range(2):
            zv = v3(zfull[:, c, :], seg, 2)
            if seg.sample:
                CP("pool", zv[:, :, 0:2], zst_s[:, l, c, :].rearrange("p (s j) -> p s j", j=2), [("zst_s", l)], [("z", c)])
            else:
                CP("pool", zv[:, :, 0:2], zc_p[:, l, c:c + 1, :], [("zc_p", l)], [("z", c)])
            c1 = v3(ct[:, c, :], seg)
            w = lambda i, c=c: caw[:, c, l * 3 + i:l * 3 + i + 1]
            TSC("dve", c1, zv[:, :, 0:T], w(0), None, ALU.mult, None, [("z", c), "caw"], [("ct", c)])
            STT(c1, zv[:, :, 1:T + 1], w(1), c1, ALU.mult, ALU.add, [("z", c), "caw", ("ct", c)], [("ct", c)])
            STT(c1, zv[:, :, 2:T + 2], w(2), c1, ALU.mult, ALU.add, [("z", c), "caw", ("ct", c)], [("ct", c)])
            TT("dve", ya[:, c, :N], ct[:, c, :N], a_b[:, c, :N], ALU.mult, [("ct", c), ("a_b", c)], [("ya", c)])
            if seg.sample:
                CP("pool", zout_s[:, c, :].rearrange("p (s j) -> p s j", j=2), zv[:, :, T:T + 2], [("z", c)], ["zout_s"])
            else:
                CP("pool", zc_p[:, l, c:c + 1, :], zv[:, :, T:T + 2], [("z", c)], [("zc_p", l)])
        if seg.sample:
            cols_to_rows(lambda c: zout_s[:, c, :], 32, 256, [(0, 32, cas_d[l].rearrange("s j f -> (s j) f"))], ["zout_s"])
        elif seg.last:
            cols_to_rows(lambda c: zc_p[:, l, c, :], 2, 256, [(0, 2, cap_d[l])], [("zc_p", l)])

        _chk("mixA")
        def mixer_c(seg, l):
            for j in range(seg.nbt):
                bk = at_next()
                if seg.sample:
                    for g in range(4):
                        o = ps[bk][(g % 2) * 64:(g % 2) * 64 + 64, (g // 2) * 64:(g // 2) * 64 + 64]
                        mm(o, vn[0:64, 0, g * 64:(g + 1) * 64], Wbd[0:64, l * 4 + g, :], True, False, [("vn", 0), "Wbd"], [("ps", bk)])
                        mm(o, ones_b[0:64, 0:64], brow_s[0:64, l * 4 + g, :], False, True, ["ones", "brow_s"], [("ps", bk)])
                    TT("dve", yc[:, :, 0:64], ps[bk][:, 0:128].rearrange("p (g t) -> p g t", g=2), c_u[:, :, 0:64], ALU.mult,
                       [("ps", bk), ("c_u", 0), ("c_u", 1)], [("yc", 0, 0), ("yc", 1, 0)])
                else:
                    for g in range(4):
                        o = ps[bk][(g % 2) * 64:(g % 2) * 64 + 64, (g // 2) * 128:(g // 2) * 128 + 128]
                        mm(o, vn[:, j, g * 64:(g + 1) * 64], WsT[:, l, g, :], True, False, [("vn", j), ("WsT", l)], [("ps", bk)])
                        mm(o, ones_b[:, 0:64], brow[:, l * 4 + g, :], False, True, ["ones", "brow"], [("ps", bk)])
                    TT("dve", yc[:, :, j * 128:(j + 1) * 128], ps[bk][:, 0:256].rearrange("p (g t) -> p g t", g=2), c_u[:, :, j * 128:(j + 1) * 128],
                       ALU.mult, [("ps", bk), ("c_u", 0), ("c_u", 1)], [("yc", 0, j), ("yc", 1, j)])


        def merge_branches(seg, l, bis, gi):
            order = ((0, ya, 2, lambda kc: [("ya", kc)]),
                     (2, yc, 2, lambda kc: [("yc", kc, j) for j in range(seg.nbt)]),
                     (1, yb, 4, lambda kc: [("yb", kc, j) for j in range(seg.nbt)] if not seg.sample else [("yb", kc, 0)]))
            for bi in bis:
                br, ybuf, kb, yreads = order[bi]
                wbr = wbrt = None
                for hf in range(2):
                    if hf == 0:
                        wg, wgt = wget(l, gi)
                    else:
                        wg, wgt = wget(l, gi + 2, hold=1)
                    for mi in range(4):
                        proj_chunk(wg, wgt, 8, mi * 128, hrhs, hr,
                                   lambda bk, mi=mi, hf=hf: ACT(gate[:, 0, mi, :N], ps[bk][:, :N], AF.Tanh, [("ps", bk)], [("gate", mi)], scale=0.5))
                        yield
                    if hf == 0:
                        wbr, wbrt = wget(l, gi + 1)
                    for mi in range(4):
                        m = hf * 4 + mi

                        def ev(bk, m=m, mi=mi, hf=hf, bi=bi):
                            if bi == 0:
                                STT(mo[:, m, :N], gate[:, 0, mi, :N], 1.0, ps[bk][:, :N], ALU.add, ALU.mult, [("ps", bk), ("gate", mi)], [("mo", m)])
                            else:
                                STT(tmp[:, m % 2, :N], gate[:, 0, mi, :N], 1.0, ps[bk][:, :N], ALU.add, ALU.mult, [("ps", bk), ("gate", mi)],
                                    [("tmp", m % 2)])
                                if bi == 1:
                                    TT(ADD_ENG, mo[:, m, :N], mo[:, m, :N], tmp[:, m % 2, :N], ALU.add, [("mo", m), ("tmp", m % 2)], [("mo", m)])
                                else:
                                    TT(ADD_ENG, mbf[:, m, :N], mo[:, m, :N], tmp[:, m % 2, :N], ALU.add, [("mo", m), ("tmp", m % 2)], [("mbf", m)])
                        proj_chunk(wbr, wbrt, kb, m * 128, lambda kc, ybuf=ybuf: ybuf[:, kc, :N], yreads, ev)
                        yield
                gi += 3

        WARM(AF.Exp)
        mixer_c(seg, l)
        filler = merge_branches(seg, l, (0, 1), 4)

        def fill(n=1):
            for _ in range(n):
                try:
                    next(filler)
                except StopIteration:
                    return

        if seg.sample:
            attn_sample(seg, l)
        else:
            PbV = [Pb, ct[:, 3, 0:264].bitcast(BF16).rearrange("p (a b) -> p a b", a=2)]
            PTsV = [PTs, ct[:, 2, 0:256].bitcast(BF16).rearrange("p (a b) -> p a b", a=2)]
            COLS = [(1, 3), (5, 6)]
            SBK = [4, 5, 6, 3]
            fine1 = [(nm_, 1, hh) for nm_ in ("Pb", "PTs") for hh in (0, 1)]
            S.add("dve", lambda e: e.memset(st4[:, 0, 2:3], 0.0), [], [("ct", 2), ("ct", 3)] + fine1)
            rr["mmn"] = 3
            rr["mm"] = rr["mm"] % 3
            sls = [slice(hh * 64, (hh + 1) * 64) for hh in (0, 1)]
            for j in range(seg.nb):
                mi_ = 1 if (seg.first and j == 0) else 0
                for Xp in range(2):
                    H4 = [(2 * Xp + g, hh) for g in range(2) for hh in range(2)]
                    for h4, (X, hh) in enumerate(H4):
                        hidx = hh * 4 + X
                        sb_ = ps[SBK[h4]]
                        tk = [("ps", SBK[h4])]
                        mm(sb_[:, 0:256], qT[sls[hh], X, j * 128:(j + 1) * 128], kT[sls[hh], l, j * 128:j * 128 + 256], True, False,
                           [("q", X), ("kT", l)], tk)
                        mm(sb_[:, 0:256], ident_b[:, :], maskb[:, mi_, :], False, False, ["ident_b", "maskb"], tk)
                        mm(sb_[:, 256:257], ident_b[:, :], sinkb[:, l, hidx:hidx + 1], False, True, ["ident_b", ("sinkb", l)], tk)
                    for h4, (X, hh) in enumerate(H4):
                        g = h4 // 2
                        cm, cs = COLS[g]
                        S.add("dve", lambda e, hh=hh, cm=cm, bkk=SBK[h4]: e.tensor_reduce(out=st4[:, hh, cm:cm + 1], in_=ps[bkk][:, 0:257], axis=AX.X,
                                                                                       op=ALU.max, negate=True), [("ps", SBK[h4])], [("st4m", g, hh)])
                    for h4, (X, hh) in enumerate(H4):
                        g = h4 // 2
                        cm, cs = COLS[g]
                        ACT(PbV[g][:, hh, 0:257], ps[SBK[h4]][:, 0:257], AF.Exp, [("ps", SBK[h4]), ("st4m", g, hh)], [("Pb", g, hh), ("rs4", h4)],
                            bias=st4[:, hh, cm:cm + 1], accum_out=rs4[:, h4:h4 + 1])
                    S.add("dve", lambda e: e.reciprocal(out=rs4[:, 0:4], in_=rs4[:, 0:4]), [("rs4", k) for k in range(4)], [("rs4", k) for k in range(4)])
                    for h4, (X, hh) in enumerate(H4):
                        g = h4 // 2
                        TSC("dve", PbV[g][:, hh, 0:256], PbV[g][:, hh, 0:256], rs4[:, h4:h4 + 1], None, ALU.mult, None,
                            [("Pb", g, hh), ("rs4", h4)], [("Pb", g, hh)])
                    fill(2)
                    for h4, (X, hh) in enumerate(H4):
                        g = h4 // 2
                        for kb in range(2):
                            tr(psT[:, h4 * 256 + kb * 128:h4 * 256 + (kb + 1) * 128], PbV[g][:, hh, kb * 128:(kb + 1) * 128], ident_b[:, :],
                               [("Pb", g, hh), "ident_b"], [("ps", 7)])
                    for g in range(2):
                        CP("act", PTsV[g][:, :, :].rearrange("p a b -> p (a b)"), psT[:, g * 512:(g + 1) * 512], [("ps", 7)], [("PTs", g, 0), ("PTs", g, 1)])
                    fill(2)
                    for g in range(2):
                        X = 2 * Xp + g
                        obk = mm_next()
                        for hh in range(2):
                            for kb in range(2):
                                mm(ps[obk][sls[hh], 0:128], vtok[:, l, j + kb, sls[hh]], PTsV[g][:, hh, kb * 128:(kb + 1) * 128], kb == 0, kb == 1,
                                   [("vtok", l, j + kb), ("PTs", g, hh)], [("ps", obk)])
                        CP("act", yb[:, X, j * 128:(j + 1) * 128], ps[obk][:, 0:128], [("ps", obk)], [("yb", X, j)])
            rr["mmn"] = 4
            S.add("dve", lambda e: e.memset(st4[:, 1, 2:3], 0.0), fine1, [("ct", 2), ("ct", 3)])
            CP("pool", kT[:, l, 0:128], kT[:, l, N:N + 128], [("kT", l)], [("kT", l)])
            CP("pool", vtok[:, l, 0, :], vtok[:, l, seg.nb, :], [("vtok", l, seg.nb)], [("vtok", l, 0)])

        _chk("attn")
        fill(100)
        for _ in merge_branches(seg, l, (2,), 10):
            pass
        WARM(AF.Sqrt)
        _chk("merge")
        for hf in range(2):
            wv, wt = wget(l, 13 + hf)
            for mi in range(4):
                m = hf * 4 + mi
                proj_chunk(wv, wt, 8, mi * 128, lambda kc: mbf[:, kc, :N], lambda kc: [("mbf", kc)], lambda bk, m=m: out_proj_evac(seg, bk, m))
        _chk("wout0")
        post_norm_add(seg, l, 1)
        _chk("wout")

    def attn_sample_prep(seg, l):
        for hfi in range(2):
            for which, src_d in ((0, ck_d), (1, cv_d)):
                stg = sstage[:, which]
                S.dma(stg, src_d[l, hfi * 8:(hfi + 1) * 8].rearrange("s k f -> k s f"), writes=[("sstage", which)])
                if which == 0:
                    for q4 in range(2):
                        bk = at_next()
                        for i in range(4):
                            tr(ps[bk][:, i * 128:(i + 1) * 128], stg[:, q4 * 4 + i, :], ident_f[:, :], [("sstage", 0), "ident_f"], [("ps", bk)])
                        s0 = hfi * 8 + q4 * 4
                        CP(cp_eng(), KT_all[:, s0:s0 + 4, 0:128], ps[bk][:, :].rearrange("p (s k) -> p s k", s=4), [("ps", bk)], ["KT_c"])
                else:
                    CP("pool", Vc_bf[:, hfi * 8:(hfi + 1) * 8, :], stg, [("sstage", 1)], ["Vc_bf"])
        MEMSET("pool", qbd[:], 0.0, ["qbd"])
        MEMSET("pool", PTn[:, :], 0.0, ["PTn"])

    def attn_sample(seg, l):
        for X in range(4):
            for hk in range(2):
                sl = slice(hk * 64, (hk + 1) * 64)
                CP("dve", qbd[sl, :, X * 8 + hk * 4:X * 8 + hk * 4 + 4], qT[sl, X, 0:64].rearrange("p (s t) -> p s t", t=4), [("q", X), "qbd"], ["qbd"])
        for i in range(4):
            bk = 5 + (i % 2)
            u = i % 2
            for s4 in range(4):
                s = i * 4 + s4
                mm(ps[bk][32 * s4:32 * s4 + 32, 0:132], qbd[:, s, :], KT_all[:, s, :], True, True, ["qbd", "KT_c", "KT_new"], [("ps", bk)],
                   tile_position=(0, 32 * s4))
            TT("dve", sm[:, u, 0:132], ps[bk][:, 0:132], mask_s[:, :], ALU.add, [("ps", bk), "mask_s"], [("sm", u)])
            S.add("dve", lambda e, u=u: e.tensor_reduce(out=st4[:, u, 1:2], in_=sm[:, u, 0:132], axis=AX.X, op=ALU.max), [("sm", u)], [("st4m", u)])
            TSC("dve", st4[:, u, 2:3], st4[:, u, 1:2], -1.0, nsinkc[:, l:l + 1], ALU.mult, ALU.min, [("st4m", u), ("nsinkc", l)], [("st4n", u)])
            ACT(Pb[:, u, 0:132], sm[:, u, 0:132], AF.Exp, [("sm", u), ("st4n", u)], [("Pb", u), ("st4r", u)], scale=1.0, bias=st4[:, u, 2:3],
                accum_out=st4[:, u, 3:4])
            ACT(st4[:, u, 4:5], sinkc[:, l:l + 1], AF.Exp, [("sinkc", l), ("st4n", u)], [("st4e", u)], bias=st4[:, u, 2:3])
            TT("dve", st4[:, u, 3:4], st4[:, u, 3:4], st4[:, u, 4:5], ALU.add, [("st4r", u), ("st4e", u)], [("st4r", u)])
            S.add("dve", lambda e, u=u: e.reciprocal(out=st4[:, u, 3:4], in_=st4[:, u, 3:4]), [("st4r", u)], [("st4r", u)])
            TSC("dve", Pb[:, u, 0:132], Pb[:, u, 0:132], st4[:, u, 3:4], None, ALU.mult, None, [("Pb", u), ("st4r", u)], [("Pb", u)])
            tr(psT[:, i * 128:(i + 1) * 128], Pb[:, u, 0:128], ident_b[:, :], [("Pb", u), "ident_b"], PT_ALL)
            tr(psT[0:4, 512 + i * 128:512 + (i + 1) * 128], Pb[:, u, 128:132], ident_b[:, :], [("Pb", u), "ident_b"], PT_ALL)
        CP("act", PT_all[:, :], psT[:, 0:512], PT_ALL, ["PT_all"])
        CP("dve", PTn[0:4, :], psT[0:4, 512:1024], PT_ALL + ["PTn"], ["PTn"])
        for s in range(NSEQ):
            mm(ps[5][:, 32 * s:32 * s + 32], Vc_bf[:, s, :], PT_all[:, 32 * s:32 * s + 32], True, False, ["Vc_bf", "PT_all"], [("ps", 5)])
            mm(ps[5][:, 32 * s:32 * s + 32], vnt[0:4, s, :], PTn[0:4, 32 * s:32 * s + 32], False, True, ["vnt", "PTn"], [("ps", 5)])
        ov = ps[5][:, :].rearrange("p (s k) -> p s k", s=16)
        for X in range(4):
            for hk in range(2):
                sl = slice(hk * 64, (hk + 1) * 64)
                CP(cp_eng(), yb[sl, X, 0:64].rearrange("p (s t) -> p s t", t=4), ov[sl, :, X * 8 + hk * 4:X * 8 + hk * 4 + 4], [("ps", 5)], [("yb", X, 0)])

    def ffn(seg, l):
        N, T, nseq = seg.N, seg.T, seg.nseq
        actv = act_s if seg.sample else act_p
        rmsnorm_to_h(seg, l, 2)
        if seg.sample:
            load_prev_s(l)
        pend_gate = []
        for gi in range(11):
            wv, wt = wget(l, 15 + gi)
            for pi in range(2):
                ca = gi * 2 + pi
                AB = (0, 1)
                cidx = [ca, ca + 22]
                ui = [(ca % 2) * 2, (ca % 2) * 2 + 1]
                uv = [v3(ust[:, ui[ab], :], seg, 2) for ab in AB]
                c1 = [v3(ct[:, ui[ab], :], seg) for ab in AB]
                bks = []
                for ab in AB:
                    bk = mm_next()
                    bks.append(bk)
                    for kc in range(8):
                        mm(ps[bk][:, :N], wv[:, kc, (ab * 2 + pi) * 128:(ab * 2 + pi + 1) * 128], h[:, kc, :N], kc == 0, kc == 7, wt + [("h", kc)], [("ps", bk)])
                for ab in AB:
                    if seg.sample:
                        CP("pool", uv[ab][:, :, 0:2], prev_s[:, cidx[ab], :].rearrange("p (s j) -> p s j", j=2), [("prev_s", cidx[ab])], [("ust", ui[ab])])
                    else:
                        CP("pool", uv[ab][:, :, 0:2], prev_p[:, l, cidx[ab]:cidx[ab] + 1, :], [("prev_p", l, cidx[ab])], [("ust", ui[ab])])
                for ab in AB:
                    CP("act", uv[ab][:, :, 2:2 + T], v3(ps[bks[ab]], seg), [("ps", bks[ab])], [("ust", ui[ab])])
                w = lambda i, ab: cfa[:, cidx[ab], l * 3 + i:l * 3 + i + 1]
                for ab in AB:
                    ACT(c1[ab], uv[ab][:, :, 0:T], AF.Identity, [("ust", ui[ab]), "cfa"], [("ct", ui[ab])], scale=w(0, ab), bias=cfa[:, cidx[ab], 6 + l:7 + l])
                for ab in AB:
                    STT(c1[ab], uv[ab][:, :, 1:T + 1], w(1, ab), c1[ab], ALU.mult, ALU.add, [("ust", ui[ab]), "cfa", ("ct", ui[ab])], [("ct", ui[ab])])
                for ab in AB:
                    STT(c1[ab], uv[ab][:, :, 2:T + 2], w(2, ab), c1[ab], ALU.mult, ALU.add, [("ust", ui[ab]), "cfa", ("ct", ui[ab])], [("ct", ui[ab])])
                for ab in AB:
                    if seg.sample:
                        CP("pool", prev_s[:, cidx[ab], :].rearrange("p (s j) -> p s j", j=2), uv[ab][:, :, T:T + 2], [("ust", ui[ab])], [("prev_s", cidx[ab])])
                    else:
                        CP("pool", prev_p[:, l, cidx[ab]:cidx[ab] + 1, :], uv[ab][:, :, T:T + 2], [("ust", ui[ab])], [("prev_p", l, cidx[ab])])
                if pend_gate:
                    pend_gate.pop()()

                def gate_fn(ua=ui[0], ub=ui[1], ca=ca):
                    ACT(ct[:, ua, :N], ct[:, ua, :N], AF.Silu, [("ct", ua)], [("ct", ua)])
                    TT("dve", actv[:, ca, :N], ct[:, ua, :N], ct[:, ub, :N], ALU.mult, [("ct", ua), ("ct", ub)], [("act", ca)])
                pend_gate.append(gate_fn)
        if pend_gate:
            pend_gate.pop()()
        WARM(AF.Sqrt)
        _chk("up")
        gi = 26
        for hf in range(2):
            banks = [mm_next() for _ in range(4)]
            k0 = 0
            for part, nk in enumerate((8, 8, 6)):
                wv, wt = wget(l, gi)
                gi += 1
                for mi in range(4):
                    for kc in range(nk):
                        mm(ps[banks[mi]][:, :N], wv[:, kc, mi * 128:(mi + 1) * 128], actv[:, k0 + kc, :N], (part == 0 and kc == 0),
                           (part == 2 and kc == nk - 1), wt + [("act", k0 + kc)], [("ps", banks[mi])])
                k0 += nk
            for mi in range(4):
                out_proj_evac(seg, banks[mi], hf * 4 + mi)
        _chk("down")
        post_norm_add(seg, l, 3)
        if seg.sample:
            store_prev_s(l)
        elif seg.last:
            cols_to_rows(lambda c: prev_p[:, l, c, :], 2, 2 * DFF, [(0, 2, ffp_d[l])], [("prev_p", l, c) for c in range(44)])

    rowst = mo[:, :, :].rearrange("p a b -> p (a b)")

    def load_prev_s(l):
        src = sff_d[l].rearrange("s j f -> (s j) f")
        for part, (c0, n) in enumerate(((0, 32), (32, 12))):
            S.dma(rowst[0:32, 0:n * 128], src[:, c0 * 128:(c0 + n) * 128], writes=MO_ALL)
            for cc0 in range(0, n, 16):
                bk = at_next()
                nn = min(16, n - cc0)
                for c in range(cc0, cc0 + nn):
                    tr(ps[bk][:, (c - cc0) * 32:(c - cc0 + 1) * 32], rowst[0:32, c * 128:(c + 1) * 128], ident_f[0:32, 0:32], MO_ALL + ["ident_f"], [("ps", bk)])
                CP(cp_eng(), prev_s[:, c0 + cc0:c0 + cc0 + nn, :], ps[bk][:, 0:nn * 32].rearrange("p (c r) -> p c r", r=32), [("ps", bk)],
                   [("prev_s", c) for c in range(c0 + cc0, c0 + cc0 + nn)])

    def store_prev_s(l):
        dst = ffs_d[l].rearrange("s j f -> (s j) f")
        for part, (c0, n) in enumerate(((0, 32), (32, 12))):
            for cc0 in range(0, n, 4):
                bk = at_next()
                for c in range(cc0, cc0 + 4):
                    tr(ps[bk][0:32, (c - cc0) * 128:(c - cc0 + 1) * 128], prev_s[:, c0 + c, :], ident_f[:, :], [("prev_s", c0 + c), "ident_f"], [("ps", bk)])
                CP(cp_eng(), rowst[0:32, cc0 * 128:(cc0 + 4) * 128], ps[bk][0:32, 0:512], [("ps", bk)], MO_ALL)
            S.dma(dst[:, c0 * 128:(c0 + n) * 128], rowst[0:32, 0:n * 128], reads=MO_ALL)

    try:
        for si, seg in enumerate(segs):
            if seg.sample:
                S.barrier()
            load_x(seg, skip_first_dma=(si > 0 and not seg.sample))
            _chk("load")
            for l in range(L):
                mixer(seg, l)
                _chk("mixer")
                ffn(seg, l)
                _chk("ffn")
            if si + 1 < len(segs) and not segs[si + 1].sample:
                preload_x0(segs[si + 1])
            store_x(seg)
            _chk("tile")
    except _Stop:
        pass

    stats = S.finalize()
    stats['lane_max'] = max(S.lane_cnt)
    stats['sbuf_left'] = nc.sbuf_bytes_remaining
    es.close()
    return nc, stats


_CACHE = {}


def kernel(x_prompt, x_sample, state_conv_a, cache_win_k, cache_win_v, state_ffn_conv,
           w_in, conv_a_w, attn_sinks, spatial_w, spatial_b, g_v, w_branch_a, w_branch_b,
           w_branch_c, w_out, g_pre_mix, g_post_mix, g_pre_ffn, g_post_ffn, w_up,
           conv_ffn_w, conv_ffn_b, w_down):
    f = lambda a: np.ascontiguousarray(np.asarray(a, dtype=np.float32))
    if "nc" not in _CACHE:
        _CACHE["nc"] = build_program()
    nc, stats = _CACHE["nc"]
    shared = {
        "w_in": f(w_in), "conv_a_w": f(conv_a_w), "attn_sinks": f(attn_sinks), "spatial_w": f(spatial_w),
        "spatial_b": f(spatial_b), "g_v": f(g_v), "w_branch_a": f(w_branch_a), "w_branch_b": f(w_branch_b),
        "w_branch_c": f(w_branch_c), "w_out": f(w_out), "g_pre_mix": f(g_pre_mix), "g_post_mix": f(g_post_mix),
        "g_pre_ffn": f(g_pre_ffn), "g_post_ffn": f(g_post_ffn), "w_up": f(w_up), "conv_ffn_w": f(conv_ffn_w),
        "conv_ffn_b": f(conv_ffn_b), "w_down": f(w_down),
    }
    xp = np.asarray(x_prompt, dtype=np.float32)
    xs = np.asarray(x_sample, dtype=np.float32)
    sca = np.asarray(state_conv_a, dtype=np.float32)
    ck = np.asarray(cache_win_k, dtype=np.float32)
    cv = np.asarray(cache_win_v, dtype=np.float32)
    sff = np.asarray(state_ffn_conv, dtype=np.float32)
    in_maps = []
    for c in range(8):
        b, half = c // 2, c % 2
        blk0 = 0 if half == 0 else 14
        m = dict(shared)
        m["xp"] = f(xp[b, blk0 * 128:(blk0 + NBLK) * 128, :])
        sl = slice(c * NSEQ, (c + 1) * NSEQ)
        m["xs"] = f(xs[sl].reshape(NS, D))
        m["sca"] = f(sca[:, sl])
        m["ck"] = f(ck[:, sl].reshape(L, NSEQ, 128, 128))
        m["cv"] = f(cv[:, sl].reshape(L, NSEQ, 128, 128))
        m["sff"] = f(sff[:, sl])
        in_maps.append(m)
    res = run_bass_kernel_spmd(nc, in_maps, core_ids=list(range(8)))
    R = res.results
    B = 4
    y_prompt = np.zeros((B, 4096, D), np.float32)
    ca_p = np.zeros((L, B, 2, 256), np.float32)
    wk_p = np.zeros((L, B, 128, 2, 64), np.float32)
    wv_p = np.zeros((L, B, 128, 2, 64), np.float32)
    ff_p = np.zeros((L, B, 2, 2 * DFF), np.float32)
    y_sample = np.zeros((128, TS_, D), np.float32)
    ca_s = np.zeros((L, 128, 2, 256), np.float32)
    wk_s = np.zeros((L, 128, 128, 2, 64), np.float32)
    wv_s = np.zeros((L, 128, 128, 2, 64), np.float32)
    ff_s = np.zeros((L, 128, 2, 2 * DFF), np.float32)
    cv_s = np.zeros((L, 128, TS_, 256), np.float32)
    for c in range(8):
        b, half = c // 2, c % 2
        r = R[c]
        if half == 0:
            y_prompt[b, 0:17 * 128] = r["yp"][0:17 * 128]
        else:
            y_prompt[b, 17 * 128:] = r["yp"][3 * 128:]
            ca_p[:, b] = r["ca_p"]
            wk_p[:, b] = r["wk_p"].reshape(L, 128, 2, 64)
            wv_p[:, b] = r["wv_p"].reshape(L, 128, 2, 64)
            ff_p[:, b] = r["ff_p"]
        sl = slice(c * NSEQ, (c + 1) * NSEQ)
        y_sample[sl] = r["ys"].reshape(NSEQ, TS_, D)
        ca_s[:, sl] = r["ca_s"]
        wk_s[:, sl] = r["wk_s"].reshape(L, NSEQ, 128, 2, 64)
        wv_s[:, sl] = r["wv_s"].reshape(L, NSEQ, 128, 2, 64)
        ff_s[:, sl] = r["ff_s"]
        cv_s[:, sl] = r["cv_s"]
    return (y_prompt, y_sample, ca_p, wk_p, wv_p, ff_p, ca_s, wk_s, wv_s, ff_s, cv_s)
```

```python
import numpy as np
from contextlib import ExitStack
import concourse.bass as bass
import concourse.mybir as mybir
from concourse.bass_utils import run_bass_kernel_spmd

F32 = mybir.dt.float32
BF16 = mybir.dt.bfloat16
I32 = mybir.dt.int32
ALU = mybir.AluOpType
AF = mybir.ActivationFunctionType
AX = mybir.AxisListType

ENGS = ("pe", "act", "dve", "pool", "sp")
N_LANES = 24
N_PLANES = 4
SAME_ENGINE_SYNC = True
FUSE_WAIT = True

L = 2
D = 1024
NBLK = 18
NS = 64
NSEQ = 16
TS_ = 4
DFF = 2816
NG = 32
NSLOT = 4
EPS = 1e-6
NEG = -1e30
TILES = [(0, 4), (4, 4), (8, 4), (12, 4), (16, 2)]


class Op:
    __slots__ = ("eng", "fn", "deps", "is_dma", "sem", "val", "milestone")

    def __init__(self, eng, fn, deps, is_dma=False):
        self.eng = eng
        self.fn = fn
        self.deps = deps
        self.is_dma = is_dma
        self.sem = None
        self.val = None
        self.milestone = False


class Res:
    __slots__ = ("w", "r")

    def __init__(self):
        self.w = None
        self.r = []


class Sched:
    def __init__(self, nc):
        self.nc = nc
        self.ops = {e: [] for e in ENGS}
        self.esem = {e: nc.alloc_semaphore("s_" + e) for e in ENGS}
        self.lanes = [nc.alloc_semaphore("s_dma%d" % i) for i in range(N_LANES + N_PLANES)]
        self.lane_cnt = [0] * (N_LANES + N_PLANES)
        self.lane_last = [None] * (N_LANES + N_PLANES)
        self.lane_rr = 0
        self.plane_rr = 0
        self.res = {}

    def _res(self, t):
        r = self.res.get(t)
        if r is None:
            r = self.res[t] = Res()
        return r

    def _collect(self, reads, writes):
        deps = []
        for t in reads:
            r = self._res(t)
            if r.w is not None:
                deps.append(r.w)
        for t in writes:
            r = self._res(t)
            if r.w is not None:
                deps.append(r.w)
            deps.extend(r.r)
        return deps

    def _update(self, op, reads, writes):
        for t in reads:
            self._res(t).r.append(op)
        for t in writes:
            r = self._res(t)
            r.w = op
            r.r = []

    @staticmethod
    def _excl(reads, writes):
        ex = [t for t in reads if isinstance(t, tuple) and t[0] == "ps"]
        if ex:
            reads = [t for t in reads if not (isinstance(t, tuple) and t[0] == "ps")]
            writes = list(writes) + ex
        return reads, writes

    def add(self, eng, fn, reads=(), writes=()):
        reads, writes = self._excl(reads, writes)
        deps = self._collect(reads, writes)
        op = Op(eng, fn, deps)
        self.ops[eng].append(op)
        self._update(op, reads, writes)
        return op

    def dma(self, out, in_, reads=(), writes=(), q="sp", **kw):
        reads, writes = self._excl(reads, writes)
        deps = self._collect(reads, writes)
        if q == "pool":
            lane = N_LANES + self.plane_rr
            self.plane_rr = (self.plane_rr + 1) % N_PLANES
        else:
            lane = self.lane_rr
            self.lane_rr = (self.lane_rr + 1) % N_LANES
        if self.lane_last[lane] is not None:
            deps.append(self.lane_last[lane])
        self.lane_cnt[lane] += 16

        def fn(e, out=out, in_=in_, kw=kw):
            return e.dma_start(out=out, in_=in_, **kw)

        op = Op(q, fn, deps, is_dma=True)
        op.sem = self.lanes[lane]
        op.val = self.lane_cnt[lane]
        self.lane_last[lane] = op
        self.ops[q].append(op)
        self._update(op, reads, writes)
        return op

    def barrier(self):
        deps = [op for op in self.lane_last if op is not None]
        for e in ENGS:
            for op in reversed(self.ops[e]):
                if not op.is_dma and op.fn is not None:
                    deps.append(op)
                    break
        for e in ENGS:
            self.ops[e].append(Op(e, None, list(deps)))

    def finalize(self, final_eng="sp"):
        nc = self.nc
        last_deps = [op for op in self.lane_last if op is not None]
        for e in ENGS:
            if e != final_eng:
                for op in reversed(self.ops[e]):
                    if not op.is_dma and op.fn is not None:
                        last_deps.append(op)
                        break
        self.ops[final_eng].append(Op(final_eng, None, last_deps))
        for e in ENGS:
            for op in self.ops[e]:
                for d in op.deps:
                    if d.is_dma:
                        continue
                    if d.eng != op.eng or (SAME_ENGINE_SYNC and d.eng != "pe"):
                        d.milestone = True
        for e in ENGS:
            c = 0
            for op in self.ops[e]:
                if op.is_dma:
                    continue
                if op.milestone:
                    c += 1
                    op.val = c
                    op.sem = self.esem[e]
        stats = {}
        with nc.Block() as block:
            def emit(e, eo):
                seen = {}
                nw = 0
                for op in self.ops[e]:
                    need = {}
                    for d in op.deps:
                        if not d.is_dma and d.eng == e and (not SAME_ENGINE_SYNC or e == "pe"):
                            continue
                        k = id(d.sem)
                        if seen.get(k, 0) >= d.val:
                            continue
                        if k not in need or need[k][1] < d.val:
                            need[k] = (d.sem, d.val)
                    waits = list(need.values())
                    for k, (sem, val) in need.items():
                        seen[k] = val
                    fused = None
                    if FUSE_WAIT and waits and op.fn is not None and not op.is_dma:
                        fused = waits.pop()
                    for (sem, val) in waits:
                        eo.wait_ge(sem, val)
                        nw += 1
                    if op.fn is None:
                        continue
                    ins = op.fn(eo)
                    if fused is not None:
                        ins._wait_ge(fused[0], fused[1])
                    if op.is_dma:
                        ins.then_inc(op.sem, 16)
                    elif op.milestone:
                        ins.then_inc(op.sem, 1)
                stats[e] = (len(self.ops[e]), nw)

            @block.tensor
            def _(eo):
                emit("pe", eo)

            @block.scalar
            def _(eo):
                emit("act", eo)

            @block.vector
            def _(eo):
                emit("dve", eo)

            @block.gpsimd
            def _(eo):
                emit("pool", eo)

            @block.sync
            def _(eo):
                emit("sp", eo)
        return stats


class Seg:
    def __init__(self, ti, sample, b0, nb):
        self.ti = ti
        self.sample = sample
        self.b0 = b0
        self.nb = nb
        self.N = NS if sample else nb * 128
        self.nseq = NSEQ if sample else 1
        self.T = TS_ if sample else self.N
        self.nbt = 1 if sample else nb
        self.first = (not sample) and b0 == 0
        self.last = (not sample) and (b0 + nb == NBLK)


class _Stop(Exception):
    pass


DEBUG_STOP = None
NOCONV = False
ADD_ENG = 'dve'
NCONV = None
LOADMODE = 0


def _chk(stage):
    if DEBUG_STOP == stage:
        raise _Stop()


def build_program():
    nc = bass.Bass("TRN2", target_bir_lowering=False)

    def din(name, shape):
        return nc.dram_tensor(name, list(shape), F32, kind="ExternalInput").ap()

    def dout(name, shape):
        return nc.dram_tensor(name, list(shape), F32, kind="ExternalOutput").ap()

    xp_d = din("xp", [NBLK * 128, D])
    xs_d = din("xs", [NS, D])
    sca_d = din("sca", [L, NSEQ, 2, 256])
    ck_d = din("ck", [L, NSEQ, 128, 128])
    cv_d = din("cv", [L, NSEQ, 128, 128])
    sff_d = din("sff", [L, NSEQ, 2, 2 * DFF])
    w_in_d = din("w_in", [L, D, 5120])
    caw_d = din("conv_a_w", [L, 3, 256])
    sinks_d = din("attn_sinks", [L, 2, 4])
    spw_d = din("spatial_w", [L, 4, 128, 128])
    spb_d = din("spatial_b", [L, 4, 128])
    gv_d = din("g_v", [L, 256])
    wba_d = din("w_branch_a", [L, 256, D])
    wbb_d = din("w_branch_b", [L, 512, D])
    wbc_d = din("w_branch_c", [L, 256, D])
    wo_d = din("w_out", [L, D, D])
    g_d = [din(n, [L, D]) for n in ("g_pre_mix", "g_post_mix", "g_pre_ffn", "g_post_ffn")]
    wup_d = din("w_up", [L, D, 2 * DFF])
    cfw_d = din("conv_ffn_w", [L, 3, 2 * DFF])
    cfb_d = din("conv_ffn_b", [L, 2 * DFF])
    wdn_d = din("w_down", [L, DFF, D])

    yp_d = dout("yp", [NBLK * 128, D])
    ys_d = dout("ys", [NS, D])
    cap_d = dout("ca_p", [L, 2, 256])
    wkp_d = dout("wk_p", [L, 128, 128])
    wvp_d = dout("wv_p", [L, 128, 128])
    ffp_d = dout("ff_p", [L, 2, 2 * DFF])
    cas_d = dout("ca_s", [L, NSEQ, 2, 256])
    wks_d = dout("wk_s", [L, NSEQ, 128, 128])
    wvs_d = dout("wv_s", [L, NSEQ, 128, 128])
    ffs_d = dout("ff_s", [L, NSEQ, 2, 2 * DFF])
    cvs_d = dout("cv_s", [L, NSEQ, TS_, 256])

    wbf_d = nc.dram_tensor("wbf", [L * NG, 128, 4096], BF16).ap()
    kvscr_d = nc.dram_tensor("kvscr", [L, NS, 256], F32).ap()

    es = ExitStack()

    def sb(name, shape, dt):
        return es.enter_context(nc.sbuf_tensor(name, list(shape), dt))

    def psum(name, shape, dt):
        return es.enter_context(nc.psum_tensor(name, list(shape), dt))

    S = Sched(nc)

    ring = sb("ring", [128, NSLOT, 4096], BF16)
    xT = sb("xT", [128, 8, 512], F32)
    xin = sb("xin", [128, 1, 1024], F32)
    xout = xin
    h = sb("h", [128, 8, 512], BF16)
    rstd = sb("rstd", [128, 512], F32)
    zfull = sb("zfull", [128, 2, 520], F32)
    a_b = sb("a_b", [128, 2, 512], F32)
    c_u = sb("c_u", [128, 2, 512], F32)
    qT = sb("qT", [128, 4, 512], BF16)
    kT = sb("kT", [128, L, 640], BF16)
    vtok = sb("vtok", [128, L, 5, 128], BF16)
    vn = sb("vn", [128, 4, 256], BF16)
    kvo = sb("kvo", [128, 256], F32)
    vno = sb("vno", [128, 256], F32)
    junk = sb("junk", [128, 256], F32)
    ya = sb("ya", [128, 2, 512], BF16)
    yb = sb("yb", [128, 4, 512], BF16)
    yc = sb("yc", [128, 2, 512], BF16)
    ct = sb("ct", [128, 4, 512], F32)
    gate = sb("gate", [128, 1, 4, 512], BF16)
    mo = sb("mo", [128, 8, 512], F32)
    mbf = sb("mbf", [128, 8, 512], BF16)
    tmp = sb("tmp", [128, 2, 512], F32)
    a_in = tmp
    act_raw = sb("act_raw", [128, 5632], F32)
    ust = sb("ust", [128, 4, 520], F32)
    prev_p = sb("prev_p", [128, L, 44, 2], F32)
    prev_s = sb("prev_s", [128, 44, 32], F32)
    zc_p = sb("zc_p", [128, L, 2, 2], F32)
    zst_s = sb("zst_s", [128, L, 2, 32], F32)
    zout_s = sb("zout_s", [128, 2, 32], F32)
    sm = sb("sm", [128, 2, 256], F32)
    Pb = sb("Pb", [128, 2, 264], BF16)
    PTs = sb("PTs", [128, 2, 256], BF16)
    st4 = sb("st4", [128, 2, 8], F32)
    rs4 = sb("rs4", [128, 4], F32)
    dmy = sb("dmy", [128, 4], F32)
    ident_f = sb("ident_f", [128, 128], F32)
    ident_b = sb("ident_b", [128, 128], BF16)
    ones_b = sb("ones_b", [128, 128], BF16)
    maskb = sb("maskb", [128, 2, 256], BF16)
    sinkb = sb("sinkb", [128, L, 8], BF16)
    mask_s = sb("mask_s", [128, 132], F32)
    gall = sb("gall", [128, 8, 8], F32)
    caw = sb("caw", [128, 2, 6], F32)
    cfa = sb("cfa", [128, 44, 8], F32)
    sink = sb("sink", [128, L, 8], F32)
    nsink = sb("nsink", [128, L, 8], F32)
    sinkc = sb("sinkc", [128, L], F32)
    nsinkc = sb("nsinkc", [128, L], F32)
    gvb = sb("gvb", [128, L, 256], F32)
    WsT = sb("WsT", [128, L, 4, 128], BF16)
    WsT_f = tmp[:, 0, :]
    WbdF = tmp[0:64, 1, :].rearrange("p (a b) -> p a b", a=L * 4)
    Wbd = sb("Wbd", [64, L * 4, 64], BF16)
    brow = sb("brow", [128, L * 4, 128], BF16)
    brow_s = sb("brow_s", [64, L * 4, 64], BF16)
    bst = ct[:, 0:2, :].rearrange("p a b -> p (a b)")
    bst2 = ct[:, 2:4, :].rearrange("p a b -> p (a b)")
    bstb = mbf[:, 0:2, :].rearrange("p a b -> p (a b)")
    ipi = sb("ipi", [128, 2], I32)
    ipf = sb("ipf", [128, 4], F32)
    iotaj = sb("iotaj", [128, 132], F32)
    cvs = sb("cvs", [128, 4, 1024], F32)

    ps = [psum("ps%d" % i, [128, 512], F32) for i in range(7)]
    psT = psum("psT", [128, 1024], BF16)
    PT_ALL = [("ps", 7)]

    act_bf = act_raw[:, :].bitcast(BF16)
    act_p = act_bf.rearrange("p (c n) -> p c n", c=22)
    act_s = act_raw[:, 0:704].bitcast(BF16).rearrange("p (c n) -> p c n", c=22)
    o0 = 704
    sstage = act_raw[:, o0:o0 + 2048].rearrange("p (a s f) -> p a s f", a=2, s=8)
    KT_all = act_raw[:, o0 + 2048:o0 + 3104].bitcast(BF16).rearrange("p (s k) -> p s k", s=16)
    Vc_bf = act_raw[:, o0 + 3104:o0 + 4128].bitcast(BF16).rearrange("p (s k) -> p s k", s=16)
    qbd = act_raw[:, o0 + 4128:o0 + 4384].bitcast(BF16).rearrange("p (s k) -> p s k", s=16)
    PT_all = act_raw[:, o0 + 4384:o0 + 4640].bitcast(BF16)
    PTn = act_raw[:, o0 + 4640:o0 + 4896].bitcast(BF16)
    vnt_f = ct[0:4, :, :].rearrange("p a (b f) -> p (a b) f", f=128)
    vnt = mbf[0:4, 0:4, :].rearrange("p a (b f) -> p (a b) f", f=128)

    def mm(out, lhsT, rhs, start, stop, reads, writes, **kw):
        S.add("pe", lambda e: e.matmul(out, lhsT=lhsT, rhs=rhs, start=start, stop=stop, **kw), reads, writes)

    def tr(out, in_, ident, reads, writes):
        S.add("pe", lambda e: e.transpose(out, in_, ident), reads, writes)

    def ACT(out, in_, func, reads, writes, **kw):
        S.add("act", lambda e: e.activation(out=out, in_=in_, func=func, **kw), reads, writes)

    def CP(eng, out, in_, reads, writes):
        if eng == "act":
            S.add("act", lambda e: e.copy(out=out, in_=in_), reads, writes)
        else:
            S.add(eng, lambda e: e.tensor_copy(out=out, in_=in_), reads, writes)

    def TSC(eng, out, in0, s1, s2, op0, op1, reads, writes, **kw):
        if op1 is None:
            S.add(eng, lambda e: e.tensor_scalar(out=out, in0=in0, scalar1=s1, scalar2=None, op0=op0, **kw), reads, writes)
        else:
            S.add(eng, lambda e: e.tensor_scalar(out=out, in0=in0, scalar1=s1, scalar2=s2, op0=op0, op1=op1, **kw), reads, writes)

    def TT(eng, out, in0, in1, op, reads, writes):
        S.add(eng, lambda e: e.tensor_tensor(out=out, in0=in0, in1=in1, op=op), reads, writes)

    def STT(out, in0, scalar, in1, op0, op1, reads, writes):
        S.add("dve", lambda e: e.scalar_tensor_tensor(out=out, in0=in0, scalar=scalar, in1=in1, op0=op0, op1=op1), reads, writes)

    def WARM(func):
        ACT(dmy[:, 1:2], dmy[:, 0:1], func, ["dmy0"], ["dmy1"])

    def MEMSET(eng, ap, val, writes):
        S.add(eng, lambda e: e.memset(ap, val), (), writes)

    rr = {"mm": 0, "at": 0, "cp": 0, "mmn": 4}

    def mm_next():
        b = rr["mm"] % rr["mmn"]
        rr["mm"] = (b + 1) % rr["mmn"]
        return b

    def at_next():
        b = 5 + rr["at"]
        rr["at"] = (rr["at"] + 1) % 2
        return b

    def cp_eng():
        rr["cp"] = (rr["cp"] + 1) % 2
        return ("act", "dve")[rr["cp"]]

    ACT_ALL = [("act", c) for c in range(22)]
    MO_ALL = [("mo", m) for m in range(8)]

    MEMSET("pool", ident_f[:], 1.0, ["ident_f"])
    S.add("pool", lambda e: e.affine_select(out=ident_f[:], in_=ident_f[:], pattern=[[1, 128]], base=0, channel_multiplier=-1,
                                             compare_op=ALU.is_equal, fill=0.0), ["ident_f"], ["ident_f"])
    CP("dve", ident_b[:], ident_f[:], ["ident_f"], ["ident_b"])
    MEMSET("dve", ones_b[:], 1.0, ["ones"])
    MEMSET("dve", dmy[:], 1.0, ["dmy0", "dmy1"])
    MEMSET("pool", maskb[:, 0, :], 0.0, ["maskb"])
    S.add("pool", lambda e: e.affine_select(out=maskb[:, 0, :], in_=maskb[:, 0, :], pattern=[[1, 256]], base=-1, channel_multiplier=-1,
                                             compare_op=ALU.is_ge, fill=NEG), ["maskb"], ["maskb"])
    S.add("pool", lambda e: e.affine_select(out=maskb[:, 0, :], in_=maskb[:, 0, :], pattern=[[-1, 256]], base=128, channel_multiplier=1,
                                             compare_op=ALU.is_ge, fill=NEG), ["maskb"], ["maskb"])
    CP("pool", maskb[:, 1, :], maskb[:, 0, :], ["maskb"], ["maskb"])
    MEMSET("pool", maskb[:, 1, 0:128], NEG, ["maskb"])
    S.add("pool", lambda e: e.iota(ipi[:, 0:1], pattern=[[0, 1]], base=0, channel_multiplier=1), (), ["ipi"])
    S.add("dve", lambda e: e.tensor_single_scalar(out=ipi[:, 1:2], in_=ipi[:, 0:1], scalar=3, op=ALU.bitwise_and), ["ipi"], ["ipi1"])
    CP("dve", ipf[:, 0:1], ipi[:, 1:2], ["ipi1"], ["ipf0"])
    S.add("dve", lambda e: e.tensor_scalar(out=ipi[:, 1:2], in0=ipi[:, 0:1], scalar1=2, scalar2=7, op0=ALU.logical_shift_right,
                                            op1=ALU.bitwise_and), ["ipi", "ipf0"], ["ipi1"])
    CP("dve", ipf[:, 1:2], ipi[:, 1:2], ["ipi1"], ["ipf1"])
    S.add("pool", lambda e: e.iota(iotaj[:], pattern=[[1, 132]], base=0, channel_multiplier=0, allow_small_or_imprecise_dtypes=True), (), ["iotaj"])
    TSC("dve", iotaj[:], iotaj[:], ipf[:, 0:1], None, ALU.subtract, None, ["iotaj", "ipf0"], ["iotaj"])
    TSC("dve", mask_s[:], iotaj[:], 1.0, None, ALU.is_ge, None, ["iotaj"], ["mask_s"])
    TSC("dve", iotaj[:], iotaj[:], 128.0, None, ALU.is_le, None, ["mask_s"], ["iotaj"])
    TT("dve", mask_s[:], mask_s[:], iotaj[:], ALU.mult, ["iotaj", "mask_s"], ["mask_s"])
    TSC("dve", mask_s[:], mask_s[:], -1.0, -NEG, ALU.add, ALU.mult, ["mask_s"], ["mask_s"])
    for l in range(L):
        S.dma(sink[:, l, :], sinks_d[l].rearrange("k g -> (k g)").partition_broadcast(128), writes=[("sink", l)])
        TSC("dve", nsink[:, l, :], sink[:, l, :], -1.0, None, ALU.mult, None, [("sink", l)], [("nsink", l)])
        CP("dve", sinkb[:, l, :], sink[:, l, :], [("sink", l)], [("sinkb", l)])
        MEMSET("dve", sinkc[:, l:l + 1], 0.0, [("sinkc", l)])
        for X in range(4):
            for hk in range(2):
                TSC("dve", junk[:, 0:1], ipf[:, 1:2], float(X * 2 + hk), None, ALU.is_equal, None, ["ipf1", ("sinkc", l)], ["junk"])
                STT(sinkc[:, l:l + 1], junk[:, 0:1], sink[:, l, hk * 4 + X:hk * 4 + X + 1], sinkc[:, l:l + 1], ALU.mult, ALU.add,
                    ["junk", ("sink", l), ("sinkc", l)], [("sinkc", l)])
        TSC("dve", nsinkc[:, l:l + 1], sinkc[:, l:l + 1], -1.0, None, ALU.mult, None, [("sinkc", l)], [("nsinkc", l)])
        S.dma(gvb[:, l, :], gv_d[l].partition_broadcast(128), writes=[("gvb", l)])

    stage_rows = act_raw

    r2c_pending = []

    def rows_to_cols(row_srcs, R, W, dst, dst_tok, pbase=0, cbase=0):
        tok = [("stg", pbase, cbase)]
        for (r0, n, ap) in row_srcs:
            S.dma(stage_rows[pbase + r0:pbase + r0 + n, cbase:cbase + W], ap, writes=tok)

        def compute():
            nch = W // 128
            per = 512 // R
            for c0 in range(0, nch, per):
                bk = at_next()
                n = min(per, nch - c0)
                for c in range(c0, c0 + n):
                    tr(ps[bk][:, (c - c0) * R:(c - c0 + 1) * R], stage_rows[pbase:pbase + R, cbase + c * 128:cbase + (c + 1) * 128],
                       ident_f[pbase:pbase + R, pbase:pbase + R], tok + ["ident_f"], [("ps", bk)])
                CP(cp_eng(), dst[:, c0:c0 + n, :], ps[bk][:, 0:n * R].rearrange("p (c r) -> p c r", r=R), [("ps", bk)], [dst_tok])
        r2c_pending.append(compute)

    def r2c_flush():
        for f_ in r2c_pending:
            f_()
        del r2c_pending[:]

    def cols_to_rows(src_fn, R, W, dst_rows, src_toks):
        nch = W // 128
        for c0 in range(0, nch, 4):
            bk = at_next()
            n = min(4, nch - c0)
            for c in range(c0, c0 + n):
                tr(ps[bk][0:R, (c - c0) * 128:(c - c0 + 1) * 128], src_fn(c), ident_f[:, :], list(src_toks) + ["ident_f"], [("ps", bk)])
            CP(cp_eng(), stage_rows[0:R, c0 * 128:(c0 + n) * 128], ps[bk][0:R, 0:n * 128], [("ps", bk)], ACT_ALL)
        for (r0, n, ap) in dst_rows:
            S.dma(ap, stage_rows[r0:r0 + n, 0:W], reads=ACT_ALL)

    rows_to_cols([(l * 3, 3, cfw_d[l]) for l in range(L)] + [(6, 2, cfb_d)], 8, 2 * DFF, cfa[:], "cfa", pbase=0, cbase=0)
    rows_to_cols([(k * L, L, g_d[k]) for k in range(4)], 8, D, gall[:], "gall", pbase=32, cbase=0)
    rows_to_cols([(l * 3, 3, caw_d[l]) for l in range(L)], 6, 256, caw[:], "caw", pbase=32, cbase=1024)
    for l in range(L):
        rows_to_cols([(0, 32, sca_d[l].rearrange("s j f -> (s j) f"))], 32, 256, zst_s[:, l], ("zst_s", l), pbase=64, cbase=256 * l)
    r2c_flush()

    for l in range(L):
        S.dma(WsT_f.rearrange("p (g s) -> p g s", g=4), spw_d[l].rearrange("g t s -> t g s"), writes=["WsT_f"])
        bk = at_next()
        for g in range(4):
            tr(ps[bk][:, g * 128:(g + 1) * 128], WsT_f[:, g * 128:(g + 1) * 128], ident_f[:], ["WsT_f", "ident_f"], [("ps", bk)])
        CP("dve", WsT_f, ps[bk][:, :], [("ps", bk)], ["WsT_f"])
        S.add("pool", lambda e: e.affine_select(out=WsT_f.rearrange("p (g t) -> p g t", g=4), in_=WsT_f.rearrange("p (g t) -> p g t", g=4),
                                                 pattern=[[0, 4], [1, 128]], base=0, channel_multiplier=-1, compare_op=ALU.is_ge, fill=0.0),
              ["WsT_f"], ["WsT_f"])
        CP("dve", WsT[:, l, :, :], WsT_f.rearrange("p (g t) -> p g t", g=4), ["WsT_f"], [("WsT", l)])
    MEMSET("pool", WbdF, 0.0, ["WbdF"])
    for i in range(NSEQ):
        S.dma(WbdF[4 * i:4 * i + 4, :, 4 * i:4 * i + 4], spw_d[:, :, 0:4, 0:4].rearrange("l g t s -> t (l g) s"), writes=["WbdF"])
    bk = at_next()
    for a_ in range(L * 4):
        tr(ps[bk][0:64, a_ * 64:(a_ + 1) * 64], WbdF[:, a_, :], ident_f[0:64, 0:64], ["WbdF", "ident_f"], [("ps", bk)])
    CP("dve", WbdF, ps[bk][0:64, :].rearrange("p (a b) -> p a b", a=L * 4), [("ps", bk)], ["WbdF"])
    S.add("pool", lambda e: e.affine_select(out=WbdF, in_=WbdF, pattern=[[0, L * 4], [1, 64]], base=0, channel_multiplier=-1,
                                             compare_op=ALU.is_ge, fill=0.0), ["WbdF"], ["WbdF"])
    CP("dve", Wbd[:], WbdF, ["WbdF"], ["Wbd"])
    MEMSET("pool", brow[:], 0.0, ["brow"])
    MEMSET("pool", brow_s[:], 0.0, ["brow_s"])
    flat_b = spb_d.rearrange("l g t -> (l g t)")
    MEMSET("pool", bst[0:64, :], 0.0, ["bst"])
    for p0 in (0, 32):
        S.dma(bst[p0:p0 + 1, :], flat_b.partition_broadcast(1), writes=["bst"])
    CP("dve", bstb[0:64, :], bst[0:64, :], ["bst"], ["bstb"])
    TT("dve", bst2[0:64, :], bst[0:64, :], bstb[0:64, :], ALU.subtract, ["bst", "bstb"], ["bst2"])
    CP("dve", brow[0:1].rearrange("p a t -> p (a t)"), bstb[0:1, :], ["bstb", "brow"], ["brow"])
    CP("dve", brow[32:33].rearrange("p a t -> p (a t)"), bst2[32:33, :], ["bst2", "brow"], ["brow"])
    for sq_ in range(NSEQ):
        CP("dve", brow_s[0:1, :, 4 * sq_:4 * sq_ + 4], bstb[0:1, :].rearrange("p (a t) -> p a t", t=128)[:, :, 0:4], ["bstb", "brow_s"], ["brow_s"])
        CP("dve", brow_s[32:33, :, 4 * sq_:4 * sq_ + 4], bst2[32:33, :].rearrange("p (a t) -> p a t", t=128)[:, :, 0:4], ["bst2", "brow_s"], ["brow_s"])
    MEMSET("pool", kT[:], 0.0, [("kT", 0), ("kT", 1)])
    MEMSET("pool", vtok[:], 0.0, [("vtok", l, s) for l in range(L) for s in range(5)])
    MEMSET("pool", prev_p[:], 0.0, [("prev_p", l, c) for l in range(L) for c in range(44)])
    MEMSET("pool", zc_p[:], 0.0, [("zc_p", 0), ("zc_p", 1)])

    S.barrier()
    def w_in_src(l, c0, n):
        return w_in_d[l].rearrange("(kc p) n -> p kc n", p=128)[:, :, c0:c0 + n]

    def groups(l):
        gs = []
        gs.append(dict(K=8, cols=512, pieces=[(0, 8, 0, w_in_src(l, 0, 256)), (0, 8, 256, w_in_src(l, 512, 256))]))
        gs.append(dict(K=8, cols=512, pieces=[(0, 8, 0, w_in_src(l, 256, 256)), (0, 8, 256, w_in_src(l, 1536, 256))]))
        gs.append(dict(K=8, cols=512, pieces=[(0, 8, 0, w_in_src(l, 768, 512))], qperm=True))
        gs.append(dict(K=8, cols=512, pieces=[(0, 8, 0, w_in_src(l, 1280, 256)), (0, 8, 256, w_in_src(l, 1792, 256))]))
        for br, wd, kb in ((0, wba_d, 2), (2, wbc_d, 2), (1, wbb_d, 4)):
            gs.append(dict(K=8, cols=512, pieces=[(0, 8, 0, w_in_src(l, 2048 + br * 1024, 512))]))
            if br == 1:
                pb = []
                for X in range(4):
                    for hk in range(2):
                        hd = hk * 4 + X
                        pb.append((X, 1, 0, wd[l][hd * 64:(hd + 1) * 64, :].rearrange("p (k n) -> p k n", k=1), (hk * 64, 64)))
                gs.append(dict(K=kb, cols=1024, pieces=pb))
            else:
                gs.append(dict(K=kb, cols=1024, pieces=[(0, kb, 0, wd[l].rearrange("(kc p) n -> p kc n", p=128))]))
            gs.append(dict(K=8, cols=512, pieces=[(0, 8, 0, w_in_src(l, 2048 + br * 1024 + 512, 512))]))
        for hf in range(2):
            gs.append(dict(K=8, cols=512, pieces=[(0, 8, 0, wo_d[l].rearrange("(kc p) n -> p kc n", p=128)[:, :, hf * 512:(hf + 1) * 512])]))
        wu = wup_d[l].rearrange("(kc p) n -> p kc n", p=128)
        for gi in range(11):
            pcs = [(0, 8, 0, wu[:, :, gi * 256:(gi + 1) * 256]), (0, 8, 256, wu[:, :, (22 + 2 * gi) * 128:(24 + 2 * gi) * 128])]
            gs.append(dict(K=8, cols=512, pieces=pcs))
        wdn = wdn_d[l].rearrange("(kc p) n -> p kc n", p=128)
        for hf in range(2):
            for (k0, nk) in ((0, 8), (8, 8), (16, 6)):
                gs.append(dict(K=nk, cols=512, pieces=[(0, nk, 0, wdn[:, k0:k0 + nk, hf * 512:(hf + 1) * 512])]))
        assert len(gs) == NG
        return gs

    GROUPS = [groups(l) for l in range(L)]

    segs = [Seg(i, False, b0, nb) for i, (b0, nb) in enumerate(TILES)] + [Seg(len(TILES), True, 0, 0)]
    items = [(l, gi) for _ in segs for l in range(L) for gi in range(NG)]
    wstate = {"next_pref": 0, "next_get": 0}

    cvstate = {"n": 0}

    def convert_into_slot(l, gi, slot):
        g = GROUPS[l][gi]
        K, cols = g["K"], g["cols"]
        base, rem = divmod(K, 4)
        bounds = []
        k_ = 0
        for q_ in range(4):
            n_ = base + (1 if q_ < rem else 0)
            bounds.append((k_, k_ + n_))
            k_ += n_
        for q_, (ka, kb) in enumerate(bounds):
            wtok = [("ring", slot, t_) for t_ in range(q_, 4)]
            if kb > ka:
                b = cvstate["n"] % 4
                cvstate["n"] += 1
                n_el = (kb - ka) * cols
                stv = cvs[:, b, 0:n_el].rearrange("p (k n) -> p k n", k=kb - ka)
                npc = 0
                for pc in g["pieces"]:
                    k0, nk, c0, src = pc[:4]
                    p0, np_ = pc[4] if len(pc) > 4 else (0, 128)
                    lo, hi = max(k0, ka), min(k0 + nk, kb)
                    if lo >= hi:
                        continue
                    ncol = src.shape[-1]
                    S.dma(stv[p0:p0 + np_, lo - ka:hi - ka, c0:c0 + ncol], src[:, lo - k0:hi - k0, :], writes=[("cst", b, npc)])
                    npc += 1
                rtok = [("cst", b, pi) for pi in range(8)]
                if g.get("qperm"):
                    for kk in range(kb - ka):
                        src_v = cvs[:, b, kk * cols:(kk + 1) * cols].rearrange("p (hk g d) -> p g hk d", hk=2, g=4)
                        dst_v = ring[:, slot, (ka + kk) * cols:(ka + kk + 1) * cols].rearrange("p (g hk d) -> p g hk d", g=4, hk=2)
                        CP("act", dst_v, src_v, rtok, wtok)
                else:
                    CP("dve", ring[:, slot, ka * cols:kb * cols], cvs[:, b, 0:n_el], rtok, wtok)
            if q_ in (1, 3):
                sa, sb_ = bounds[q_ - 1][0], bounds[q_][1]
                if sb_ > sa:
                    S.dma(wbf_d[l * NG + gi][:, sa * cols:sb_ * cols], ring[:, slot, sa * cols:sb_ * cols],
                          reads=[("ring", slot, q_ - 1), ("ring", slot, q_)], writes=[("wbf", l, gi, q_ // 2)], q="pool")

    def prefetch_upto(k):
        while wstate["next_pref"] <= k and wstate["next_pref"] < len(items):
            i = wstate["next_pref"]
            l, gi = items[i]
            g = GROUPS[l][gi]
            used = g["K"] * g["cols"]
            slot = i % NSLOT
            if i < L * NG:
                convert_into_slot(l, gi, slot)
            else:
                S.dma(ring[:, slot, 0:used], wbf_d[l * NG + gi][:, 0:used], reads=[("wbf", l, gi, 0), ("wbf", l, gi, 1)],
                      writes=[("ring", slot, t_) for t_ in range(4)])
            wstate["next_pref"] += 1

    def wget(l_expect, gi_expect, hold=0):
        i = wstate["next_get"]
        l, gi = items[i]
        assert (l, gi) == (l_expect, gi_expect), (l, gi, l_expect, gi_expect)
        prefetch_upto(i + NSLOT - 1 - hold)
        wstate["next_get"] += 1
        g = GROUPS[l][gi]
        slot = i % NSLOT
        view = ring[:, slot, 0:g["K"] * g["cols"]].rearrange("p (k n) -> p k n", k=g["K"])
        return view, [("ring", slot, t_) for t_ in range(4)]

    def xtok(seg, c):
        return [("x", c, j) for j in range(seg.nbt)]

    STG = [(xin[:, 0, :], [("xin", 0)]),
           (tmp[:, :, :].rearrange("p a b -> p (a b)"), [("tmp", 0), ("tmp", 1)]),
           (ct[:, 0:2, :].rearrange("p a b -> p (a b)"), [("ct", 0), ("ct", 1)]),
           (ct[:, 2:4, :].rearrange("p a b -> p (a b)"), [("ct", 2), ("ct", 3)])]
    LOAD_STG = [0, 2, 3, 0]
    STORE_STG = [1, 2, 3, 1]

    def x_src(seg, j):
        return xs_d if seg.sample else xp_d[(seg.b0 + j) * 128:(seg.b0 + j + 1) * 128, :]

    def preload_x0(seg):
        R = 64 if seg.sample else 128
        buf, toks = STG[LOAD_STG[0]]
        S.dma(buf[0:R, :], x_src(seg, 0), writes=toks)

    def load_x(seg, skip_first_dma=False):
        for j in range(seg.nbt):
            R = 64 if seg.sample else 128
            buf, toks = STG[LOAD_STG[j]]
            if not (j == 0 and skip_first_dma):
                S.dma(buf[0:R, :], x_src(seg, j), writes=toks)
            for hf in range(2):
                bk = at_next()
                for cc in range(4):
                    c = hf * 4 + cc
                    tr(ps[bk][:, cc * R:(cc + 1) * R], buf[0:R, c * 128:(c + 1) * 128], ident_f[0:R, 0:R], toks + ["ident_f"], [("ps", bk)])
                CP(cp_eng(), xT[:, hf * 4:hf * 4 + 4, j * 128:j * 128 + R], ps[bk][:, 0:4 * R].rearrange("p (c t) -> p c t", c=4),
                   [("ps", bk)], [("x", hf * 4 + cc, j) for cc in range(4)])

    def store_x(seg):
        for j in range(seg.nbt):
            R = 64 if seg.sample else 128
            buf, toks = STG[STORE_STG[j]]
            for hf in range(2):
                bk = at_next()
                for cc in range(4):
                    c = hf * 4 + cc
                    tr(ps[bk][0:R, cc * 128:(cc + 1) * 128], xT[:, c, j * 128:j * 128 + R], ident_f[:, :], [("x", c, j), "ident_f"], [("ps", bk)])
                CP(cp_eng(), buf[0:R, hf * 512:(hf + 1) * 512], ps[bk][0:R, 0:512], [("ps", bk)], toks)
            dst = ys_d if seg.sample else yp_d[(seg.b0 + j) * 128:(seg.b0 + j + 1) * 128, :]
            S.dma(dst, buf[0:R, :], reads=toks)

    def norm_stats(seg, eps=EPS):
        N = seg.N
        for c in range(8):
            mm(ps[4][:, :N], ones_b[:, :], h[:, c, :N], c == 0, c == 7, [("h", c), "ones"], [("ps", 4)])
        ACT(rstd[:, :N], ps[4][:, :N], AF.Sqrt, [("ps", 4)], ["rstd"], scale=1.0 / D, bias=eps)
        S.add("dve", lambda e: e.reciprocal(out=rstd[:, :N], in_=rstd[:, :N]), ["rstd"], ["rstd"])

    def rmsnorm_to_h(seg, l, kind):
        N = seg.N
        for c in range(8):
            ACT(h[:, c, :N], xT[:, c, :N], AF.Square, xtok(seg, c), [("h", c)])
        norm_stats(seg)
        for c in range(8):
            STT(h[:, c, :N], xT[:, c, :N], gall[:, c, kind * L + l:kind * L + l + 1], rstd[:, :N], ALU.mult, ALU.mult,
                xtok(seg, c) + ["rstd", "gall"], [("h", c)])

    def post_norm_add(seg, l, kind):
        N = seg.N
        norm_stats(seg, 4.0 * EPS if kind == 1 else EPS)
        for m in range(8):
            STT(tmp[:, m % 2, :N], mo[:, m, :N], gall[:, m, kind * L + l:kind * L + l + 1], rstd[:, :N], ALU.mult, ALU.mult,
                [("mo", m), "rstd", "gall"], [("tmp", m % 2)])
            TT(ADD_ENG, xT[:, m, :N], xT[:, m, :N], tmp[:, m % 2, :N], ALU.add, xtok(seg, m) + [("tmp", m % 2)], xtok(seg, m))

    def v3(ap2d, seg, off=0):
        return ap2d[:, 0:seg.nseq * (seg.T + off)].rearrange("p (s t) -> p s t", s=seg.nseq)

    def out_proj_evac(seg, bk, m):
        N = seg.N
        CP("dve", mo[:, m, :N], ps[bk][:, :N], [("ps", bk)], [("mo", m)])
        ACT(h[:, m, :N], ps[bk][:, :N], AF.Square, [("ps", bk)], [("h", m)])

    def mixer(seg, l):
        N, T, nseq = seg.N, seg.T, seg.nseq
        rmsnorm_to_h(seg, l, 0)
        _chk("norm")
        hr = lambda kc: [("h", kc)]

        def proj_chunk(wv, wt, K, col0, rhs, rreads, evac):
            bk = mm_next()
            for kc in range(K):
                mm(ps[bk][:, :N], wv[:, kc, col0:col0 + 128], rhs(kc), kc == 0, kc == K - 1, wt + rreads(kc), [("ps", bk)])
            evac(bk)

        hrhs = lambda kc: h[:, kc, :N]
        wv, wt = wget(l, 0)
        for c in range(2):
            proj_chunk(wv, wt, 8, c * 128, hrhs, hr, lambda bk, c=c: CP("act", a_in[:, c, :N], ps[bk][:, :N], [("ps", bk)], [("tmp", c)]))
        for c in range(2):
            def ev(bk, c=c):
                zv = v3(zfull[:, c, :], seg, 2)
                TT("dve", zv[:, :, 2:2 + T], v3(ps[bk], seg), v3(a_in[:, c, :], seg), ALU.mult, [("ps", bk), ("tmp", c)], [("z", c)])
            proj_chunk(wv, wt, 8, 256 + c * 128, hrhs, hr, ev)
        wv, wt = wget(l, 1)
        for c in range(2):
            proj_chunk(wv, wt, 8, c * 128, hrhs, hr, lambda bk, c=c: CP("act", a_b[:, c, :N], ps[bk][:, :N], [("ps", bk)], [("a_b", c)]))
        for c in range(2):
            proj_chunk(wv, wt, 8, 256 + c * 128, hrhs, hr, lambda bk, c=c: CP("act", c_u[:, c, :N], ps[bk][:, :N], [("ps", bk)], [("c_u", c)]))
        if seg.sample:
            attn_sample_prep(seg, l)
        wv, wt = wget(l, 2)
        for X in range(4):
            proj_chunk(wv, wt, 8, X * 128, hrhs, hr, lambda bk, X=X: ACT(qT[:, X, :N], ps[bk][:, :N], AF.Copy, [("ps", bk)], [("q", X)], scale=0.125))
        wv, wt = wget(l, 3)
        if seg.sample:
            proj_chunk(wv, wt, 8, 0, hrhs, hr, lambda bk: CP("act", KT_all[:, :, 128:132], v3(ps[bk], seg), [("ps", bk)], ["KT_new"]))
        else:
            proj_chunk(wv, wt, 8, 0, hrhs, hr, lambda bk: CP("act", kT[:, l, 128:128 + N], ps[bk][:, :N], [("ps", bk)], [("kT", l)]))
        for j in range(seg.nbt):
            R = 64 if seg.sample else 128
            bk = at_next()
            c_lo = 0 if (seg.sample or (seg.last and j == seg.nb - 1)) else 128
            for kc in range(8):
                mm(ps[bk][0:R, c_lo:512], h[:, kc, j * 128:j * 128 + R], wv[:, kc, c_lo:512], kc == 0, kc == 7, wt + [("h", kc)], [("ps", bk)])
            if not seg.sample:
                CP("act", vtok[:, l, 1 + j, :], ps[bk][:, 128:256], [("ps", bk)], [("vtok", l, 1 + j)])
            ACT(junk[0:R, :], ps[bk][0:R, 256:512], AF.Square, [("ps", bk)], ["junk", ("st4", 0)], accum_out=st4[0:R, 0, 0:1])
            ACT(st4[0:R, 0, 0:1], st4[0:R, 0, 0:1], AF.Sqrt, [("st4", 0)], [("st4", 0)], scale=1.0 / 256, bias=EPS)
            S.add("dve", lambda e, R=R: e.reciprocal(out=st4[0:R, 0, 0:1], in_=st4[0:R, 0, 0:1]), [("st4", 0)], [("st4", 0)])
            if seg.sample:
                STT(vno[0:R, :], ps[bk][0:R, 256:512], st4[0:R, 0, 0:1], gvb[0:R, l, :], ALU.mult, ALU.mult,
                    [("ps", bk), ("st4", 0), ("gvb", l)], ["vno"])
                CP("dve", vn[0:R, 0, :], vno[0:R, :], ["vno"], [("vn", 0)])
                S.dma(cvs_d[l].rearrange("s t f -> (s t) f"), vno[0:R, :], reads=["vno"])
                CP("act", kvo[0:R, :], ps[bk][0:R, 0:256], [("ps", bk)], ["kvo"])
                S.dma(kvscr_d[l], kvo[0:R, :], reads=["kvo"], writes=[("kvscr", l)])
                S.dma(wks_d[l, :, 124:128, :], kvscr_d[l][:, 0:128].rearrange("(s t) f -> s t f", t=4), reads=[("kvscr", l)])
                S.dma(wvs_d[l, :, 124:128, :], kvscr_d[l][:, 128:256].rearrange("(s t) f -> s t f", t=4), reads=[("kvscr", l)])
                S.dma(vnt_f, kvscr_d[l][:, 128:256].rearrange("(s t) f -> t s f", t=4), reads=[("kvscr", l)], writes=["vnt_f"] + [("ct", c) for c in range(4)])
                CP("dve", vnt, vnt_f, ["vnt_f"] + [("ct", c) for c in range(4)] + [("mbf", m) for m in range(4)], ["vnt"] + [("mbf", m) for m in range(4)])
                S.dma(wks_d[l, :, 0:124, :], ck_d[l, :, 4:128, :])
                S.dma(wvs_d[l, :, 0:124, :], cv_d[l, :, 4:128, :])
            else:
                STT(vn[:, j, :], ps[bk][:, 256:512], st4[:, 0, 0:1], gvb[:, l, :], ALU.mult, ALU.mult,
                    [("ps", bk), ("st4", 0), ("gvb", l)], [("vn", j)])
                if seg.last and j == seg.nb - 1:
                    CP("act", kvo[:, :], ps[bk][:, 0:256], [("ps", bk)], ["kvo"])
                    S.dma(wkp_d[l], kvo[:, 0:128], reads=["kvo"])
                    S.dma(wvp_d[l], kvo[:, 128:256], reads=["kvo"])

        _chk("proj")
        for c in range(2):
            zv = v3(zfull[:, c, :], seg, 2)
            if seg.sample:
                CP("pool", zv[:, :, 0:2], zst_s[:, l, c, :].rearrange("p (s j) -> p s j", j=2), [("zst_s", l)], [("z", c)])
            else:
                CP("pool", zv[:, :, 0:2], zc_p[:, l, c:c + 1, :], [("zc_p", l)], [("z", c)])
            c1 = v3(ct[:, c, :], seg)
            w = lambda i, c=c: caw[:, c, l * 3 + i:l * 3 + i + 1]
            TSC("dve", c1, zv[:, :, 0:T], w(0), None, ALU.mult, None, [("z", c), "caw"], [("ct", c)])
            STT(c1, zv[:, :, 1:T + 1], w(1), c1, ALU.mult, ALU.add, [("z", c), "caw", ("ct", c)], [("ct", c)])
            STT(c1, zv[:, :, 2:T + 2], w(2), c1, ALU.mult, ALU.add, [("z", c), "caw", ("ct", c)], [("ct", c)])
            TT("dve", ya[:, c, :N], ct[:, c, :N], a_b[:, c, :N], ALU.mult, [("ct", c), ("a_b", c)], [("ya", c)])
            if seg.sample:
                CP("pool", zout_s[:, c, :].rearrange("p (s j) -> p s j", j=2), zv[:, :, T:T + 2], [("z", c)], ["zout_s"])
            else:
                CP("pool", zc_p[:, l, c:c + 1, :], zv[:, :, T:T + 2], [("z", c)], [("zc_p", l)])
        if seg.sample:
            cols_to_rows(lambda c: zout_s[:, c, :], 32, 256, [(0, 32, cas_d[l].rearrange("s j f -> (s j) f"))], ["zout_s"])
        elif seg.last:
            cols_to_rows(lambda c: zc_p[:, l, c, :], 2, 256, [(0, 2, cap_d[l])], [("zc_p", l)])

        _chk("mixA")
        def mixer_c(seg, l):
            for j in range(seg.nbt):
                bk = at_next()
                if seg.sample:
                    for g in range(4):
                        o = ps[bk][(g % 2) * 64:(g % 2) * 64 + 64, (g // 2) * 64:(g // 2) * 64 + 64]
                        mm(o, vn[0:64, 0, g * 64:(g + 1) * 64], Wbd[0:64, l * 4 + g, :], True, False, [("vn", 0), "Wbd"], [("ps", bk)])
                        mm(o, ones_b[0:64, 0:64], brow_s[0:64, l * 4 + g, :], False, True, ["ones", "brow_s"], [("ps", bk)])
                    TT("dve", yc[:, :, 0:64], ps[bk][:, 0:128].rearrange("p (g t) -> p g t", g=2), c_u[:, :, 0:64], ALU.mult,
                       [("ps", bk), ("c_u", 0), ("c_u", 1)], [("yc", 0, 0), ("yc", 1, 0)])
                else:
                    for g in range(4):
                        o = ps[bk][(g % 2) * 64:(g % 2) * 64 + 64, (g // 2) * 128:(g // 2) * 128 + 128]
                        mm(o, vn[:, j, g * 64:(g + 1) * 64], WsT[:, l, g, :], True, False, [("vn", j), ("WsT", l)], [("ps", bk)])
                        mm(o, ones_b[:, 0:64], brow[:, l * 4 + g, :], False, True, ["ones", "brow"], [("ps", bk)])
                    TT("dve", yc[:, :, j * 128:(j + 1) * 128], ps[bk][:, 0:256].rearrange("p (g t) -> p g t", g=2), c_u[:, :, j * 128:(j + 1) * 128],
                       ALU.mult, [("ps", bk), ("c_u", 0), ("c_u", 1)], [("yc", 0, j), ("yc", 1, j)])


        def merge_branches(seg, l, bis, gi):
            order = ((0, ya, 2, lambda kc: [("ya", kc)]),
                     (2, yc, 2, lambda kc: [("yc", kc, j) for j in range(seg.nbt)]),
                     (1, yb, 4, lambda kc: [("yb", kc, j) for j in range(seg.nbt)] if not seg.sample else [("yb", kc, 0)]))
            for bi in bis:
                br, ybuf, kb, yreads = order[bi]
                wbr = wbrt = None
                for hf in range(2):
                    if hf == 0:
                        wg, wgt = wget(l, gi)
                    else:
                        wg, wgt = wget(l, gi + 2, hold=1)
                    for mi in range(4):
                        proj_chunk(wg, wgt, 8, mi * 128, hrhs, hr,
                                   lambda bk, mi=mi, hf=hf: ACT(gate[:, 0, mi, :N], ps[bk][:, :N], AF.Tanh, [("ps", bk)], [("gate", mi)], scale=0.5))
                        yield
                    if hf == 0:
                        wbr, wbrt = wget(l, gi + 1)
                    for mi in range(4):
                        m = hf * 4 + mi

                        def ev(bk, m=m, mi=mi, hf=hf, bi=bi):
                            if bi == 0:
                                STT(mo[:, m, :N], gate[:, 0, mi, :N], 1.0, ps[bk][:, :N], ALU.add, ALU.mult, [("ps", bk), ("gate", mi)], [("mo", m)])
                            else:
                                STT(tmp[:, m % 2, :N], gate[:, 0, mi, :N], 1.0, ps[bk][:, :N], ALU.add, ALU.mult, [("ps", bk), ("gate", mi)],
                                    [("tmp", m % 2)])
                                if bi == 1:
                                    TT(ADD_ENG, mo[:, m, :N], mo[:, m, :N], tmp[:, m % 2, :N], ALU.add, [("mo", m), ("tmp", m % 2)], [("mo", m)])
                                else:
                                    TT(ADD_ENG, mbf[:, m, :N], mo[:, m, :N], tmp[:, m % 2, :N], ALU.add, [("mo", m), ("tmp", m % 2)], [("mbf", m)])
                        proj_chunk(wbr, wbrt, kb, m * 128, lambda kc, ybuf=ybuf: ybuf[:, kc, :N], yreads, ev)
                        yield
                gi += 3

        WARM(AF.Exp)
        mixer_c(seg, l)
        filler = merge_branches(seg, l, (0, 1), 4)

        def fill(n=1):
            for _ in range(n):
                try:
                    next(filler)
                except StopIteration:
                    return

        if seg.sample:
            attn_sample(seg, l)
        else:
            PbV = [Pb, ct[:, 3, 0:264].bitcast(BF16).rearrange("p (a b) -> p a b", a=2)]
            PTsV = [PTs, ct[:, 2, 0:256].bitcast(BF16).rearrange("p (a b) -> p a b", a=2)]
            COLS = [(1, 3), (5, 6)]
            SBK = [4, 5, 6, 3]
            fine1 = [(nm_, 1, hh) for nm_ in ("Pb", "PTs") for hh in (0, 1)]
            S.add("dve", lambda e: e.memset(st4[:, 0, 2:3], 0.0), [], [("ct", 2), ("ct", 3)] + fine1)
            rr["mmn"] = 3
            rr["mm"] = rr["mm"] % 3
            sls = [slice(hh * 64, (hh + 1) * 64) for hh in (0, 1)]
            for j in range(seg.nb):
                mi_ = 1 if (seg.first and j == 0) else 0
                for Xp in range(2):
                    H4 = [(2 * Xp + g, hh) for g in range(2) for hh in range(2)]
                    for h4, (X, hh) in enumerate(H4):
                        mm(ps[SBK[h4]][:, 0:256], qT[sls[hh], X, j * 128:(j + 1) * 128], kT[sls[hh], l, j * 128:j * 128 + 256], True, False,
                           [("q", X), ("kT", l)], [("ps", SBK[h4])])
                    for h4, (X, hh) in enumerate(H4):
                        mm(ps[SBK[h4]][:, 0:256], ident_b[:, :], maskb[:, mi_, :], False, False, ["ident_b", "maskb"], [("ps", SBK[h4])])
                    for h4, (X, hh) in enumerate(H4):
                        hidx = hh * 4 + X
                        mm(ps[SBK[h4]][:, 256:257], ident_b[:, :], sinkb[:, l, hidx:hidx + 1], False, True, ["ident_b", ("sinkb", l)],
                           [("ps", SBK[h4])])
                    for h4, (X, hh) in enumerate(H4):
                        g = h4 // 2
                        cm, cs = COLS[g]
                        S.add("dve", lambda e, hh=hh, cm=cm, bkk=SBK[h4]: e.tensor_reduce(out=st4[:, hh, cm:cm + 1], in_=ps[bkk][:, 0:257], axis=AX.X,
                                                                                       op=ALU.max, negate=True), [("ps", SBK[h4])], [("st4m", g, hh)])
                    for h4, (X, hh) in enumerate(H4):
                        g = h4 // 2
                        cm, cs = COLS[g]
                        ACT(PbV[g][:, hh, 0:257], ps[SBK[h4]][:, 0:257], AF.Exp, [("ps", SBK[h4]), ("st4m", g, hh)], [("Pb", g, hh), ("rs4", h4)],
                            bias=st4[:, hh, cm:cm + 1], accum_out=rs4[:, h4:h4 + 1])
                    S.add("dve", lambda e: e.reciprocal(out=rs4[:, 0:4], in_=rs4[:, 0:4]), [("rs4", k) for k in range(4)], [("rs4", k) for k in range(4)])
                    for h4, (X, hh) in enumerate(H4):
                        g = h4 // 2
                        TSC("dve", PbV[g][:, hh, 0:256], PbV[g][:, hh, 0:256], rs4[:, h4:h4 + 1], None, ALU.mult, None,
                            [("Pb", g, hh), ("rs4", h4)], [("Pb", g, hh)])
                    fill(2)
                    for h4, (X, hh) in enumerate(H4):
                        g = h4 // 2
                        for kb in range(2):
                            tr(psT[:, h4 * 256 + kb * 128:h4 * 256 + (kb + 1) * 128], PbV[g][:, hh, kb * 128:(kb + 1) * 128], ident_b[:, :],
                               [("Pb", g, hh), "ident_b"], [("ps", 7)])
                    for g in range(2):
                        CP("act", PTsV[g][:, :, :].rearrange("p a b -> p (a b)"), psT[:, g * 512:(g + 1) * 512], [("ps", 7)], [("PTs", g, 0), ("PTs", g, 1)])
                    fill(2)
                    for g in range(2):
                        X = 2 * Xp + g
                        obk = mm_next()
                        for hh in range(2):
                            for kb in range(2):
                                mm(ps[obk][sls[hh], 0:128], vtok[:, l, j + kb, sls[hh]], PTsV[g][:, hh, kb * 128:(kb + 1) * 128], kb == 0, kb == 1,
                                   [("vtok", l, j + kb), ("PTs", g, hh)], [("ps", obk)])
                        CP("act", yb[:, X, j * 128:(j + 1) * 128], ps[obk][:, 0:128], [("ps", obk)], [("yb", X, j)])
            rr["mmn"] = 4
            S.add("dve", lambda e: e.memset(st4[:, 1, 2:3], 0.0), fine1, [("ct", 2), ("ct", 3)])
            CP("pool", kT[:, l, 0:128], kT[:, l, N:N + 128], [("kT", l)], [("kT", l)])
            CP("pool", vtok[:, l, 0, :], vtok[:, l, seg.nb, :], [("vtok", l, seg.nb)], [("vtok", l, 0)])

        _chk("attn")
        fill(100)
        for _ in merge_branches(seg, l, (2,), 10):
            pass
        WARM(AF.Sqrt)
        _chk("merge")
        for hf in range(2):
            wv, wt = wget(l, 13 + hf)
            for mi in range(4):
                m = hf * 4 + mi
                proj_chunk(wv, wt, 8, mi * 128, lambda kc: mbf[:, kc, :N], lambda kc: [("mbf", kc)], lambda bk, m=m: out_proj_evac(seg, bk, m))
        _chk("wout0")
        post_norm_add(seg, l, 1)
        _chk("wout")

    def attn_sample_prep(seg, l):
        for hfi in range(2):
            for which, src_d in ((0, ck_d), (1, cv_d)):
                stg = sstage[:, which]
                S.dma(stg, src_d[l, hfi * 8:(hfi + 1) * 8].rearrange("s k f -> k s f"), writes=[("sstage", which)])
                if which == 0:
                    for q4 in range(2):
                        bk = at_next()
                        for i in range(4):
                            tr(ps[bk][:, i * 128:(i + 1) * 128], stg[:, q4 * 4 + i, :], ident_f[:, :], [("sstage", 0), "ident_f"], [("ps", bk)])
                        s0 = hfi * 8 + q4 * 4
                        CP(cp_eng(), KT_all[:, s0:s0 + 4, 0:128], ps[bk][:, :].rearrange("p (s k) -> p s k", s=4), [("ps", bk)], ["KT_c"])
                else:
                    CP("pool", Vc_bf[:, hfi * 8:(hfi + 1) * 8, :], stg, [("sstage", 1)], ["Vc_bf"])
        MEMSET("pool", qbd[:], 0.0, ["qbd"])
        MEMSET("pool", PTn[:, :], 0.0, ["PTn"])

    def attn_sample(seg, l):
        for X in range(4):
            for hk in range(2):
                sl = slice(hk * 64, (hk + 1) * 64)
                CP("dve", qbd[sl, :, X * 8 + hk * 4:X * 8 + hk * 4 + 4], qT[sl, X, 0:64].rearrange("p (s t) -> p s t", t=4), [("q", X), "qbd"], ["qbd"])
        for i in range(4):
            bk = 5 + (i % 2)
            u = i % 2
            for s4 in range(4):
                s = i * 4 + s4
                mm(ps[bk][32 * s4:32 * s4 + 32, 0:132], qbd[:, s, :], KT_all[:, s, :], True, True, ["qbd", "KT_c", "KT_new"], [("ps", bk)],
                   tile_position=(0, 32 * s4))
            TT("dve", sm[:, u, 0:132], ps[bk][:, 0:132], mask_s[:, :], ALU.add, [("ps", bk), "mask_s"], [("sm", u)])
            S.add("dve", lambda e, u=u: e.tensor_reduce(out=st4[:, u, 1:2], in_=sm[:, u, 0:132], axis=AX.X, op=ALU.max), [("sm", u)], [("st4m", u)])
            TSC("dve", st4[:, u, 2:3], st4[:, u, 1:2], -1.0, nsinkc[:, l:l + 1], ALU.mult, ALU.min, [("st4m", u), ("nsinkc", l)], [("st4n", u)])
            ACT(Pb[:, u, 0:132], sm[:, u, 0:132], AF.Exp, [("sm", u), ("st4n", u)], [("Pb", u), ("st4r", u)], scale=1.0, bias=st4[:, u, 2:3],
                accum_out=st4[:, u, 3:4])
            ACT(st4[:, u, 4:5], sinkc[:, l:l + 1], AF.Exp, [("sinkc", l), ("st4n", u)], [("st4e", u)], bias=st4[:, u, 2:3])
            TT("dve", st4[:, u, 3:4], st4[:, u, 3:4], st4[:, u, 4:5], ALU.add, [("st4r", u), ("st4e", u)], [("st4r", u)])
            S.add("dve", lambda e, u=u: e.reciprocal(out=st4[:, u, 3:4], in_=st4[:, u, 3:4]), [("st4r", u)], [("st4r", u)])
            TSC("dve", Pb[:, u, 0:132], Pb[:, u, 0:132], st4[:, u, 3:4], None, ALU.mult, None, [("Pb", u), ("st4r", u)], [("Pb", u)])
            tr(psT[:, i * 128:(i + 1) * 128], Pb[:, u, 0:128], ident_b[:, :], [("Pb", u), "ident_b"], PT_ALL)
            tr(psT[0:4, 512 + i * 128:512 + (i + 1) * 128], Pb[:, u, 128:132], ident_b[:, :], [("Pb", u), "ident_b"], PT_ALL)
        CP("act", PT_all[:, :], psT[:, 0:512], PT_ALL, ["PT_all"])
        CP("dve", PTn[0:4, :], psT[0:4, 512:1024], PT_ALL + ["PTn"], ["PTn"])
        for s in range(NSEQ):
            mm(ps[5][:, 32 * s:32 * s + 32], Vc_bf[:, s, :], PT_all[:, 32 * s:32 * s + 32], True, False, ["Vc_bf", "PT_all"], [("ps", 5)])
            mm(ps[5][:, 32 * s:32 * s + 32], vnt[0:4, s, :], PTn[0:4, 32 * s:32 * s + 32], False, True, ["vnt", "PTn"], [("ps", 5)])
        ov = ps[5][:, :].rearrange("p (s k) -> p s k", s=16)
        for X in range(4):
            for hk in range(2):
                sl = slice(hk * 64, (hk + 1) * 64)
                CP(cp_eng(), yb[sl, X, 0:64].rearrange("p (s t) -> p s t", t=4), ov[sl, :, X * 8 + hk * 4:X * 8 + hk * 4 + 4], [("ps", 5)], [("yb", X, 0)])

    def ffn(seg, l):
        N, T, nseq = seg.N, seg.T, seg.nseq
        actv = act_s if seg.sample else act_p
        rmsnorm_to_h(seg, l, 2)
        if seg.sample:
            load_prev_s(l)
        pend_gate = []
        for gi in range(11):
            wv, wt = wget(l, 15 + gi)
            for pi in range(2):
                ca = gi * 2 + pi
                AB = (0, 1)
                cidx = [ca, ca + 22]
                ui = [(ca % 2) * 2, (ca % 2) * 2 + 1]
                uv = [v3(ust[:, ui[ab], :], seg, 2) for ab in AB]
                c1 = [v3(ct[:, ui[ab], :], seg) for ab in AB]
                bks = []
                for ab in AB:
                    bk = mm_next()
                    bks.append(bk)
                    for kc in range(8):
                        mm(ps[bk][:, :N], wv[:, kc, (ab * 2 + pi) * 128:(ab * 2 + pi + 1) * 128], h[:, kc, :N], kc == 0, kc == 7, wt + [("h", kc)], [("ps", bk)])
                for ab in AB:
                    if seg.sample:
                        CP("pool", uv[ab][:, :, 0:2], prev_s[:, cidx[ab], :].rearrange("p (s j) -> p s j", j=2), [("prev_s", cidx[ab])], [("ust", ui[ab])])
                    else:
                        CP("pool", uv[ab][:, :, 0:2], prev_p[:, l, cidx[ab]:cidx[ab] + 1, :], [("prev_p", l, cidx[ab])], [("ust", ui[ab])])
                for ab in AB:
                    CP("act", uv[ab][:, :, 2:2 + T], v3(ps[bks[ab]], seg), [("ps", bks[ab])], [("ust", ui[ab])])
                w = lambda i, ab: cfa[:, cidx[ab], l * 3 + i:l * 3 + i + 1]
                for ab in AB:
                    ACT(c1[ab], uv[ab][:, :, 0:T], AF.Identity, [("ust", ui[ab]), "cfa"], [("ct", ui[ab])], scale=w(0, ab), bias=cfa[:, cidx[ab], 6 + l:7 + l])
                for ab in AB:
                    STT(c1[ab], uv[ab][:, :, 1:T + 1], w(1, ab), c1[ab], ALU.mult, ALU.add, [("ust", ui[ab]), "cfa", ("ct", ui[ab])], [("ct", ui[ab])])
                for ab in AB:
                    STT(c1[ab], uv[ab][:, :, 2:T + 2], w(2, ab), c1[ab], ALU.mult, ALU.add, [("ust", ui[ab]), "cfa", ("ct", ui[ab])], [("ct", ui[ab])])
                for ab in AB:
                    if seg.sample:
                        CP("pool", prev_s[:, cidx[ab], :].rearrange("p (s j) -> p s j", j=2), uv[ab][:, :, T:T + 2], [("ust", ui[ab])], [("prev_s", cidx[ab])])
                    else:
                        CP("pool", prev_p[:, l, cidx[ab]:cidx[ab] + 1, :], uv[ab][:, :, T:T + 2], [("ust", ui[ab])], [("prev_p", l, cidx[ab])])
                if pend_gate:
                    pend_gate.pop()()

                def gate_fn(ua=ui[0], ub=ui[1], ca=ca):
                    ACT(ct[:, ua, :N], ct[:, ua, :N], AF.Silu, [("ct", ua)], [("ct", ua)])
                    TT("dve", actv[:, ca, :N], ct[:, ua, :N], ct[:, ub, :N], ALU.mult, [("ct", ua), ("ct", ub)], [("act", ca)])
                pend_gate.append(gate_fn)
        if pend_gate:
            pend_gate.pop()()
        WARM(AF.Sqrt)
        _chk("up")
        gi = 26
        for hf in range(2):
            banks = [mm_next() for _ in range(4)]
            k0 = 0
            for part, nk in enumerate((8, 8, 6)):
                wv, wt = wget(l, gi)
                gi += 1
                for mi in range(4):
                    for kc in range(nk):
                        mm(ps[banks[mi]][:, :N], wv[:, kc, mi * 128:(mi + 1) * 128], actv[:, k0 + kc, :N], (part == 0 and kc == 0),
                           (part == 2 and kc == nk - 1), wt + [("act", k0 + kc)], [("ps", banks[mi])])
                k0 += nk
            for mi in range(4):
                out_proj_evac(seg, banks[mi], hf * 4 + mi)
        _chk("down")
        post_norm_add(seg, l, 3)
        if seg.sample:
            store_prev_s(l)
        elif seg.last:
            cols_to_rows(lambda c: prev_p[:, l, c, :], 2, 2 * DFF, [(0, 2, ffp_d[l])], [("prev_p", l, c) for c in range(44)])

    rowst = mo[:, :, :].rearrange("p a b -> p (a b)")

    def load_prev_s(l):
        src = sff_d[l].rearrange("s j f -> (s j) f")
        for part, (c0, n) in enumerate(((0, 32), (32, 12))):
            S.dma(rowst[0:32, 0:n * 128], src[:, c0 * 128:(c0 + n) * 128], writes=MO_ALL)
            for cc0 in range(0, n, 16):
                bk = at_next()
                nn = min(16, n - cc0)
                for c in range(cc0, cc0 + nn):
                    tr(ps[bk][:, (c - cc0) * 32:(c - cc0 + 1) * 32], rowst[0:32, c * 128:(c + 1) * 128], ident_f[0:32, 0:32], MO_ALL + ["ident_f"], [("ps", bk)])
                CP(cp_eng(), prev_s[:, c0 + cc0:c0 + cc0 + nn, :], ps[bk][:, 0:nn * 32].rearrange("p (c r) -> p c r", r=32), [("ps", bk)],
                   [("prev_s", c) for c in range(c0 + cc0, c0 + cc0 + nn)])

    def store_prev_s(l):
        dst = ffs_d[l].rearrange("s j f -> (s j) f")
        for part, (c0, n) in enumerate(((0, 32), (32, 12))):
            for cc0 in range(0, n, 4):
                bk = at_next()
                for c in range(cc0, cc0 + 4):
                    tr(ps[bk][0:32, (c - cc0) * 128:(c - cc0 + 1) * 128], prev_s[:, c0 + c, :], ident_f[:, :], [("prev_s", c0 + c), "ident_f"], [("ps", bk)])
                CP(cp_eng(), rowst[0:32, cc0 * 128:(cc0 + 4) * 128], ps[bk][0:32, 0:512], [("ps", bk)], MO_ALL)
            S.dma(dst[:, c0 * 128:(c0 + n) * 128], rowst[0:32, 0:n * 128], reads=MO_ALL)

    try:
        for si, seg in enumerate(segs):
            if seg.sample:
                S.barrier()
            load_x(seg, skip_first_dma=(si > 0 and not seg.sample))
            _chk("load")
            for l in range(L):
                mixer(seg, l)
                _chk("mixer")
                ffn(seg, l)
                _chk("ffn")
            if si + 1 < len(segs) and not segs[si + 1].sample:
                preload_x0(segs[si + 1])
            store_x(seg)
            _chk("tile")
    except _Stop:
        pass

    stats = S.finalize()
    stats['lane_max'] = max(S.lane_cnt)
    stats['sbuf_left'] = nc.sbuf_bytes_remaining
    es.close()
    return nc, stats


_CACHE = {}


def kernel(x_prompt, x_sample, state_conv_a, cache_win_k, cache_win_v, state_ffn_conv,
           w_in, conv_a_w, attn_sinks, spatial_w, spatial_b, g_v, w_branch_a, w_branch_b,
           w_branch_c, w_out, g_pre_mix, g_post_mix, g_pre_ffn, g_post_ffn, w_up,
           conv_ffn_w, conv_ffn_b, w_down):
    f = lambda a: np.ascontiguousarray(np.asarray(a, dtype=np.float32))
    if "nc" not in _CACHE:
        _CACHE["nc"] = build_program()
    nc, stats = _CACHE["nc"]
    shared = {
        "w_in": f(w_in), "conv_a_w": f(conv_a_w), "attn_sinks": f(attn_sinks), "spatial_w": f(spatial_w),
        "spatial_b": f(spatial_b), "g_v": f(g_v), "w_branch_a": f(w_branch_a), "w_branch_b": f(w_branch_b),
        "w_branch_c": f(w_branch_c), "w_out": f(w_out), "g_pre_mix": f(g_pre_mix), "g_post_mix": f(g_post_mix),
        "g_pre_ffn": f(g_pre_ffn), "g_post_ffn": f(g_post_ffn), "w_up": f(w_up), "conv_ffn_w": f(conv_ffn_w),
        "conv_ffn_b": f(conv_ffn_b), "w_down": f(w_down),
    }
    xp = np.asarray(x_prompt, dtype=np.float32)
    xs = np.asarray(x_sample, dtype=np.float32)
    sca = np.asarray(state_conv_a, dtype=np.float32)
    ck = np.asarray(cache_win_k, dtype=np.float32)
    cv = np.asarray(cache_win_v, dtype=np.float32)
    sff = np.asarray(state_ffn_conv, dtype=np.float32)
    in_maps = []
    for c in range(8):
        b, half = c // 2, c % 2
        blk0 = 0 if half == 0 else 14
        m = dict(shared)
        m["xp"] = f(xp[b, blk0 * 128:(blk0 + NBLK) * 128, :])
        sl = slice(c * NSEQ, (c + 1) * NSEQ)
        m["xs"] = f(xs[sl].reshape(NS, D))
        m["sca"] = f(sca[:, sl])
        m["ck"] = f(ck[:, sl].reshape(L, NSEQ, 128, 128))
        m["cv"] = f(cv[:, sl].reshape(L, NSEQ, 128, 128))
        m["sff"] = f(sff[:, sl])
        in_maps.append(m)
    res = run_bass_kernel_spmd(nc, in_maps, core_ids=list(range(8)))
    R = res.results
    B = 4
    y_prompt = np.zeros((B, 4096, D), np.float32)
    ca_p = np.zeros((L, B, 2, 256), np.float32)
    wk_p = np.zeros((L, B, 128, 2, 64), np.float32)
    wv_p = np.zeros((L, B, 128, 2, 64), np.float32)
    ff_p = np.zeros((L, B, 2, 2 * DFF), np.float32)
    y_sample = np.zeros((128, TS_, D), np.float32)
    ca_s = np.zeros((L, 128, 2, 256), np.float32)
    wk_s = np.zeros((L, 128, 128, 2, 64), np.float32)
    wv_s = np.zeros((L, 128, 128, 2, 64), np.float32)
    ff_s = np.zeros((L, 128, 2, 2 * DFF), np.float32)
    cv_s = np.zeros((L, 128, TS_, 256), np.float32)
    for c in range(8):
        b, half = c // 2, c % 2
        r = R[c]
        if half == 0:
            y_prompt[b, 0:17 * 128] = r["yp"][0:17 * 128]
        else:
            y_prompt[b, 17 * 128:] = r["yp"][3 * 128:]
            ca_p[:, b] = r["ca_p"]
            wk_p[:, b] = r["wk_p"].reshape(L, 128, 2, 64)
            wv_p[:, b] = r["wv_p"].reshape(L, 128, 2, 64)
            ff_p[:, b] = r["ff_p"]
        sl = slice(c * NSEQ, (c + 1) * NSEQ)
        y_sample[sl] = r["ys"].reshape(NSEQ, TS_, D)
        ca_s[:, sl] = r["ca_s"]
        wk_s[:, sl] = r["wk_s"].reshape(L, NSEQ, 128, 2, 64)
        wv_s[:, sl] = r["wv_s"].reshape(L, NSEQ, 128, 2, 64)
        ff_s[:, sl] = r["ff_s"]
        cv_s[:, sl] = r["cv_s"]
    return (y_prompt, y_sample, ca_p, wk_p, wv_p, ff_p, ca_s, wk_s, wv_s, ff_s, cv_s)
```

```python
import numpy as np
from contextlib import ExitStack
import concourse.bass as bass
import concourse.mybir as mybir
from concourse.bass_utils import run_bass_kernel_spmd

F32 = mybir.dt.float32
BF16 = mybir.dt.bfloat16
I32 = mybir.dt.int32
ALU = mybir.AluOpType
AF = mybir.ActivationFunctionType
AX = mybir.AxisListType

ENGS = ("pe", "act", "dve", "pool", "sp")
N_LANES = 24
N_PLANES = 4
SAME_ENGINE_SYNC = True
FUSE_WAIT = True

L = 2
D = 1024
NBLK = 18
NS = 64
NSEQ = 16
TS_ = 4
DFF = 2816
NG = 32
NSLOT = 4
EPS = 1e-6
NEG = -1e30
TILES = [(0, 4), (4, 4), (8, 4), (12, 4), (16, 2)]


class Op:
    __slots__ = ("eng", "fn", "deps", "is_dma", "sem", "val", "milestone")

    def __init__(self, eng, fn, deps, is_dma=False):
        self.eng = eng
        self.fn = fn
        self.deps = deps
        self.is_dma = is_dma
        self.sem = None
        self.val = None
        self.milestone = False


class Res:
    __slots__ = ("w", "r")

    def __init__(self):
        self.w = None
        self.r = []


class Sched:
    def __init__(self, nc):
        self.nc = nc
        self.ops = {e: [] for e in ENGS}
        self.esem = {e: nc.alloc_semaphore("s_" + e) for e in ENGS}
        self.lanes = [nc.alloc_semaphore("s_dma%d" % i) for i in range(N_LANES + N_PLANES)]
        self.lane_cnt = [0] * (N_LANES + N_PLANES)
        self.lane_last = [None] * (N_LANES + N_PLANES)
        self.lane_rr = 0
        self.plane_rr = 0
        self.res = {}

    def _res(self, t):
        r = self.res.get(t)
        if r is None:
            r = self.res[t] = Res()
        return r

    def _collect(self, reads, writes):
        deps = []
        for t in reads:
            r = self._res(t)
            if r.w is not None:
                deps.append(r.w)
        for t in writes:
            r = self._res(t)
            if r.w is not None:
                deps.append(r.w)
            deps.extend(r.r)
        return deps

    def _update(self, op, reads, writes):
        for t in reads:
            self._res(t).r.append(op)
        for t in writes:
            r = self._res(t)
            r.w = op
            r.r = []

    @staticmethod
    def _excl(reads, writes):
        ex = [t for t in reads if isinstance(t, tuple) and t[0] == "ps"]
        if ex:
            reads = [t for t in reads if not (isinstance(t, tuple) and t[0] == "ps")]
            writes = list(writes) + ex
        return reads, writes

    def add(self, eng, fn, reads=(), writes=()):
        reads, writes = self._excl(reads, writes)
        deps = self._collect(reads, writes)
        op = Op(eng, fn, deps)
        self.ops[eng].append(op)
        self._update(op, reads, writes)
        return op

    def dma(self, out, in_, reads=(), writes=(), q="sp", **kw):
        reads, writes = self._excl(reads, writes)
        deps = self._collect(reads, writes)
        if q == "pool":
            lane = N_LANES + self.plane_rr
            self.plane_rr = (self.plane_rr + 1) % N_PLANES
        else:
            lane = self.lane_rr
            self.lane_rr = (self.lane_rr + 1) % N_LANES
        if self.lane_last[lane] is not None:
            deps.append(self.lane_last[lane])
        self.lane_cnt[lane] += 16

        def fn(e, out=out, in_=in_, kw=kw):
            return e.dma_start(out=out, in_=in_, **kw)

        op = Op(q, fn, deps, is_dma=True)
        op.sem = self.lanes[lane]
        op.val = self.lane_cnt[lane]
        self.lane_last[lane] = op
        self.ops[q].append(op)
        self._update(op, reads, writes)
        return op

    def barrier(self):
        deps = [op for op in self.lane_last if op is not None]
        for e in ENGS:
            for op in reversed(self.ops[e]):
                if not op.is_dma and op.fn is not None:
                    deps.append(op)
                    break
        for e in ENGS:
            self.ops[e].append(Op(e, None, list(deps)))

    def finalize(self, final_eng="sp"):
        nc = self.nc
        last_deps = [op for op in self.lane_last if op is not None]
        for e in ENGS:
            if e != final_eng:
                for op in reversed(self.ops[e]):
                    if not op.is_dma and op.fn is not None:
                        last_deps.append(op)
                        break
        self.ops[final_eng].append(Op(final_eng, None, last_deps))
        for e in ENGS:
            for op in self.ops[e]:
                for d in op.deps:
                    if d.is_dma:
                        continue
                    if d.eng != op.eng or (SAME_ENGINE_SYNC and d.eng != "pe"):
                        d.milestone = True
        for e in ENGS:
            c = 0
            for op in self.ops[e]:
                if op.is_dma:
                    continue
                if op.milestone:
                    c += 1
                    op.val = c
                    op.sem = self.esem[e]
        stats = {}
        with nc.Block() as block:
            def emit(e, eo):
                seen = {}
                nw = 0
                for op in self.ops[e]:
                    need = {}
                    for d in op.deps:
                        if not d.is_dma and d.eng == e and (not SAME_ENGINE_SYNC or e == "pe"):
                            continue
                        k = id(d.sem)
                        if seen.get(k, 0) >= d.val:
                            continue
                        if k not in need or need[k][1] < d.val:
                            need[k] = (d.sem, d.val)
                    waits = list(need.values())
                    for k, (sem, val) in need.items():
                        seen[k] = val
                    fused = None
                    if FUSE_WAIT and waits and op.fn is not None and not op.is_dma:
                        fused = waits.pop()
                    for (sem, val) in waits:
                        eo.wait_ge(sem, val)
                        nw += 1
                    if op.fn is None:
                        continue
                    ins = op.fn(eo)
                    if fused is not None:
                        ins._wait_ge(fused[0], fused[1])
                    if op.is_dma:
                        ins.then_inc(op.sem, 16)
                    elif op.milestone:
                        ins.then_inc(op.sem, 1)
                stats[e] = (len(self.ops[e]), nw)

            @block.tensor
            def _(eo):
                emit("pe", eo)

            @block.scalar
            def _(eo):
                emit("act", eo)

            @block.vector
            def _(eo):
                emit("dve", eo)

            @block.gpsimd
            def _(eo):
                emit("pool", eo)

            @block.sync
            def _(eo):
                emit("sp", eo)
        return stats


class Seg:
    def __init__(self, ti, sample, b0, nb):
        self.ti = ti
        self.sample = sample
        self.b0 = b0
        self.nb = nb
        self.N = NS if sample else nb * 128
        self.nseq = NSEQ if sample else 1
        self.T = TS_ if sample else self.N
        self.nbt = 1 if sample else nb
        self.first = (not sample) and b0 == 0
        self.last = (not sample) and (b0 + nb == NBLK)


class _Stop(Exception):
    pass


DEBUG_STOP = None
NOCONV = False
ADD_ENG = 'dve'
NCONV = None
LOADMODE = 0


def _chk(stage):
    if DEBUG_STOP == stage:
        raise _Stop()


def build_program():
    nc = bass.Bass("TRN2", target_bir_lowering=False)

    def din(name, shape):
        return nc.dram_tensor(name, list(shape), F32, kind="ExternalInput").ap()

    def dout(name, shape):
        return nc.dram_tensor(name, list(shape), F32, kind="ExternalOutput").ap()

    xp_d = din("xp", [NBLK * 128, D])
    xs_d = din("xs", [NS, D])
    sca_d = din("sca", [L, NSEQ, 2, 256])
    ck_d = din("ck", [L, NSEQ, 128, 128])
    cv_d = din("cv", [L, NSEQ, 128, 128])
    sff_d = din("sff", [L, NSEQ, 2, 2 * DFF])
    w_in_d = din("w_in", [L, D, 5120])
    caw_d = din("conv_a_w", [L, 3, 256])
    sinks_d = din("attn_sinks", [L, 2, 4])
    spw_d = din("spatial_w", [L, 4, 128, 128])
    spb_d = din("spatial_b", [L, 4, 128])
    gv_d = din("g_v", [L, 256])
    wba_d = din("w_branch_a", [L, 256, D])
    wbb_d = din("w_branch_b", [L, 512, D])
    wbc_d = din("w_branch_c", [L, 256, D])
    wo_d = din("w_out", [L, D, D])
    g_d = [din(n, [L, D]) for n in ("g_pre_mix", "g_post_mix", "g_pre_ffn", "g_post_ffn")]
    wup_d = din("w_up", [L, D, 2 * DFF])
    cfw_d = din("conv_ffn_w", [L, 3, 2 * DFF])
    cfb_d = din("conv_ffn_b", [L, 2 * DFF])
    wdn_d = din("w_down", [L, DFF, D])

    yp_d = dout("yp", [NBLK * 128, D])
    ys_d = dout("ys", [NS, D])
    cap_d = dout("ca_p", [L, 2, 256])
    wkp_d = dout("wk_p", [L, 128, 128])
    wvp_d = dout("wv_p", [L, 128, 128])
    ffp_d = dout("ff_p", [L, 2, 2 * DFF])
    cas_d = dout("ca_s", [L, NSEQ, 2, 256])
    wks_d = dout("wk_s", [L, NSEQ, 128, 128])
    wvs_d = dout("wv_s", [L, NSEQ, 128, 128])
    ffs_d = dout("ff_s", [L, NSEQ, 2, 2 * DFF])
    cvs_d = dout("cv_s", [L, NSEQ, TS_, 256])

    wbf_d = nc.dram_tensor("wbf", [L * NG, 128, 4096], BF16).ap()
    kvscr_d = nc.dram_tensor("kvscr", [L, NS, 256], F32).ap()

    es = ExitStack()

    def sb(name, shape, dt):
        return es.enter_context(nc.sbuf_tensor(name, list(shape), dt))

    def psum(name, shape, dt):
        return es.enter_context(nc.psum_tensor(name, list(shape), dt))

    S = Sched(nc)

    ring = sb("ring", [128, NSLOT, 4096], BF16)
    xT = sb("xT", [128, 8, 512], F32)
    xin = sb("xin", [128, 1, 1024], F32)
    xout = xin
    h = sb("h", [128, 8, 512], BF16)
    rstd = sb("rstd", [128, 512], F32)
    zfull = sb("zfull", [128, 2, 520], F32)
    a_b = sb("a_b", [128, 2, 512], F32)
    c_u = sb("c_u", [128, 2, 512], F32)
    qT = sb("qT", [128, 4, 512], BF16)
    kT = sb("kT", [128, L, 640], BF16)
    vtok = sb("vtok", [128, L, 5, 128], BF16)
    vn = sb("vn", [128, 4, 256], BF16)
    kvo = sb("kvo", [128, 256], F32)
    vno = sb("vno", [128, 256], F32)
    junk = sb("junk", [128, 256], F32)
    ya = sb("ya", [128, 2, 512], BF16)
    yb = sb("yb", [128, 4, 512], BF16)
    yc = sb("yc", [128, 2, 512], BF16)
    ct = sb("ct", [128, 4, 512], F32)
    gate = sb("gate", [128, 1, 4, 512], BF16)
    mo = sb("mo", [128, 8, 512], F32)
    mbf = sb("mbf", [128, 8, 512], BF16)
    tmp = sb("tmp", [128, 2, 512], F32)
    a_in = tmp
    act_raw = sb("act_raw", [128, 5632], F32)
    ust = sb("ust", [128, 4, 520], F32)
    prev_p = sb("prev_p", [128, L, 44, 2], F32)
    prev_s = sb("prev_s", [128, 44, 32], F32)
    zc_p = sb("zc_p", [128, L, 2, 2], F32)
    zst_s = sb("zst_s", [128, L, 2, 32], F32)
    zout_s = sb("zout_s", [128, 2, 32], F32)
    sm = sb("sm", [128, 2, 256], F32)
    Pb = sb("Pb", [128, 2, 264], BF16)
    PTs = sb("PTs", [128, 2, 256], BF16)
    st4 = sb("st4", [128, 2, 8], F32)
    rs4 = sb("rs4", [128, 4], F32)
    dmy = sb("dmy", [128, 4], F32)
    ident_f = sb("ident_f", [128, 128], F32)
    ident_b = sb("ident_b", [128, 128], BF16)
    ones_b = sb("ones_b", [128, 128], BF16)
    maskb = sb("maskb", [128, 2, 256], BF16)
    sinkb = sb("sinkb", [128, L, 8], BF16)
    mask_s = sb("mask_s", [128, 132], F32)
    gall = sb("gall", [128, 8, 8], F32)
    caw = sb("caw", [128, 2, 6], F32)
    cfa = sb("cfa", [128, 44, 8], F32)
    sink = sb("sink", [128, L, 8], F32)
    nsink = sb("nsink", [128, L, 8], F32)
    sinkc = sb("sinkc", [128, L], F32)
    nsinkc = sb("nsinkc", [128, L], F32)
    gvb = sb("gvb", [128, L, 256], F32)
    WsT = sb("WsT", [128, L, 4, 128], BF16)
    WsT_f = tmp[:, 0, :]
    WbdF = tmp[0:64, 1, :].rearrange("p (a b) -> p a b", a=L * 4)
    Wbd = sb("Wbd", [64, L * 4, 64], BF16)
    brow = sb("brow", [128, L * 4, 128], BF16)
    brow_s = sb("brow_s", [64, L * 4, 64], BF16)
    bst = ct[:, 0:2, :].rearrange("p a b -> p (a b)")
    bst2 = ct[:, 2:4, :].rearrange("p a b -> p (a b)")
    bstb = mbf[:, 0:2, :].rearrange("p a b -> p (a b)")
    ipi = sb("ipi", [128, 2], I32)
    ipf = sb("ipf", [128, 4], F32)
    iotaj = sb("iotaj", [128, 132], F32)
    cvs = sb("cvs", [128, 4, 1024], F32)

    ps = [psum("ps%d" % i, [128, 512], F32) for i in range(7)]
    psT = psum("psT", [128, 1024], BF16)
    PT_ALL = [("ps", 7)]

    act_bf = act_raw[:, :].bitcast(BF16)
    act_p = act_bf.rearrange("p (c n) -> p c n", c=22)
    act_s = act_raw[:, 0:704].bitcast(BF16).rearrange("p (c n) -> p c n", c=22)
    o0 = 704
    sstage = act_raw[:, o0:o0 + 2048].rearrange("p (a s f) -> p a s f", a=2, s=8)
    KT_all = act_raw[:, o0 + 2048:o0 + 3104].bitcast(BF16).rearrange("p (s k) -> p s k", s=16)
    Vc_bf = act_raw[:, o0 + 3104:o0 + 4128].bitcast(BF16).rearrange("p (s k) -> p s k", s=16)
    qbd = act_raw[:, o0 + 4128:o0 + 4384].bitcast(BF16).rearrange("p (s k) -> p s k", s=16)
    PT_all = act_raw[:, o0 + 4384:o0 + 4640].bitcast(BF16)
    PTn = act_raw[:, o0 + 4640:o0 + 4896].bitcast(BF16)
    vnt_f = ct[0:4, :, :].rearrange("p a (b f) -> p (a b) f", f=128)
    vnt = mbf[0:4, 0:4, :].rearrange("p a (b f) -> p (a b) f", f=128)

    def mm(out, lhsT, rhs, start, stop, reads, writes, **kw):
        S.add("pe", lambda e: e.matmul(out, lhsT=lhsT, rhs=rhs, start=start, stop=stop, **kw), reads, writes)

    def tr(out, in_, ident, reads, writes):
        S.add("pe", lambda e: e.transpose(out, in_, ident), reads, writes)

    def ACT(out, in_, func, reads, writes, **kw):
        S.add("act", lambda e: e.activation(out=out, in_=in_, func=func, **kw), reads, writes)

    def CP(eng, out, in_, reads, writes):
        if eng == "act":
            S.add("act", lambda e: e.copy(out=out, in_=in_), reads, writes)
        else:
            S.add(eng, lambda e: e.tensor_copy(out=out, in_=in_), reads, writes)

    def TSC(eng, out, in0, s1, s2, op0, op1, reads, writes, **kw):
        if op1 is None:
            S.add(eng, lambda e: e.tensor_scalar(out=out, in0=in0, scalar1=s1, scalar2=None, op0=op0, **kw), reads, writes)
        else:
            S.add(eng, lambda e: e.tensor_scalar(out=out, in0=in0, scalar1=s1, scalar2=s2, op0=op0, op1=op1, **kw), reads, writes)

    def TT(eng, out, in0, in1, op, reads, writes):
        S.add(eng, lambda e: e.tensor_tensor(out=out, in0=in0, in1=in1, op=op), reads, writes)

    def STT(out, in0, scalar, in1, op0, op1, reads, writes):
        S.add("dve", lambda e: e.scalar_tensor_tensor(out=out, in0=in0, scalar=scalar, in1=in1, op0=op0, op1=op1), reads, writes)

    def WARM(func):
        ACT(dmy[:, 1:2], dmy[:, 0:1], func, ["dmy0"], ["dmy1"])

    def MEMSET(eng, ap, val, writes):
        S.add(eng, lambda e: e.memset(ap, val), (), writes)

    rr = {"mm": 0, "at": 0, "cp": 0, "mmn": 4}

    def mm_next():
        b = rr["mm"] % rr["mmn"]
        rr["mm"] = (b + 1) % rr["mmn"]
        return b

    def at_next():
        b = 5 + rr["at"]
        rr["at"] = (rr["at"] + 1) % 2
        return b

    def cp_eng():
        rr["cp"] = (rr["cp"] + 1) % 2
        return ("act", "dve")[rr["cp"]]

    ACT_ALL = [("act", c) for c in range(22)]
    MO_ALL = [("mo", m) for m in range(8)]

    MEMSET("pool", ident_f[:], 1.0, ["ident_f"])
    S.add("pool", lambda e: e.affine_select(out=ident_f[:], in_=ident_f[:], pattern=[[1, 128]], base=0, channel_multiplier=-1,
                                             compare_op=ALU.is_equal, fill=0.0), ["ident_f"], ["ident_f"])
    CP("dve", ident_b[:], ident_f[:], ["ident_f"], ["ident_b"])
    MEMSET("dve", ones_b[:], 1.0, ["ones"])
    MEMSET("dve", dmy[:], 1.0, ["dmy0", "dmy1"])
    MEMSET("pool", maskb[:, 0, :], 0.0, ["maskb"])
    S.add("pool", lambda e: e.affine_select(out=maskb[:, 0, :], in_=maskb[:, 0, :], pattern=[[1, 256]], base=-1, channel_multiplier=-1,
                                             compare_op=ALU.is_ge, fill=NEG), ["maskb"], ["maskb"])
    S.add("pool", lambda e: e.affine_select(out=maskb[:, 0, :], in_=maskb[:, 0, :], pattern=[[-1, 256]], base=128, channel_multiplier=1,
                                             compare_op=ALU.is_ge, fill=NEG), ["maskb"], ["maskb"])
    CP("pool", maskb[:, 1, :], maskb[:, 0, :], ["maskb"], ["maskb"])
    MEMSET("pool", maskb[:, 1, 0:128], NEG, ["maskb"])
    S.add("pool", lambda e: e.iota(ipi[:, 0:1], pattern=[[0, 1]], base=0, channel_multiplier=1), (), ["ipi"])
    S.add("dve", lambda e: e.tensor_single_scalar(out=ipi[:, 1:2], in_=ipi[:, 0:1], scalar=3, op=ALU.bitwise_and), ["ipi"], ["ipi1"])
    CP("dve", ipf[:, 0:1], ipi[:, 1:2], ["ipi1"], ["ipf0"])
    S.add("dve", lambda e: e.tensor_scalar(out=ipi[:, 1:2], in0=ipi[:, 0:1], scalar1=2, scalar2=7, op0=ALU.logical_shift_right,
                                            op1=ALU.bitwise_and), ["ipi", "ipf0"], ["ipi1"])
    CP("dve", ipf[:, 1:2], ipi[:, 1:2], ["ipi1"], ["ipf1"])
    S.add("pool", lambda e: e.iota(iotaj[:], pattern=[[1, 132]], base=0, channel_multiplier=0, allow_small_or_imprecise_dtypes=True), (), ["iotaj"])
    TSC("dve", iotaj[:], iotaj[:], ipf[:, 0:1], None, ALU.subtract, None, ["iotaj", "ipf0"], ["iotaj"])
    TSC("dve", mask_s[:], iotaj[:], 1.0, None, ALU.is_ge, None, ["iotaj"], ["mask_s"])
    TSC("dve", iotaj[:], iotaj[:], 128.0, None, ALU.is_le, None, ["mask_s"], ["iotaj"])
    TT("dve", mask_s[:], mask_s[:], iotaj[:], ALU.mult, ["iotaj", "mask_s"], ["mask_s"])
    TSC("dve", mask_s[:], mask_s[:], -1.0, -NEG, ALU.add, ALU.mult, ["mask_s"], ["mask_s"])
    for l in range(L):
        S.dma(sink[:, l, :], sinks_d[l].rearrange("k g -> (k g)").partition_broadcast(128), writes=[("sink", l)])
        TSC("dve", nsink[:, l, :], sink[:, l, :], -1.0, None, ALU.mult, None, [("sink", l)], [("nsink", l)])
        CP("dve", sinkb[:, l, :], sink[:, l, :], [("sink", l)], [("sinkb", l)])
        MEMSET("dve", sinkc[:, l:l + 1], 0.0, [("sinkc", l)])
        for X in range(4):
            for hk in range(2):
                TSC("dve", junk[:, 0:1], ipf[:, 1:2], float(X * 2 + hk), None, ALU.is_equal, None, ["ipf1", ("sinkc", l)], ["junk"])
                STT(sinkc[:, l:l + 1], junk[:, 0:1], sink[:, l, hk * 4 + X:hk * 4 + X + 1], sinkc[:, l:l + 1], ALU.mult, ALU.add,
                    ["junk", ("sink", l), ("sinkc", l)], [("sinkc", l)])
        TSC("dve", nsinkc[:, l:l + 1], sinkc[:, l:l + 1], -1.0, None, ALU.mult, None, [("sinkc", l)], [("nsinkc", l)])
        S.dma(gvb[:, l, :], gv_d[l].partition_broadcast(128), writes=[("gvb", l)])

    stage_rows = act_raw

    r2c_pending = []

    def rows_to_cols(row_srcs, R, W, dst, dst_tok, pbase=0, cbase=0):
        tok = [("stg", pbase, cbase)]
        for (r0, n, ap) in row_srcs:
            S.dma(stage_rows[pbase + r0:pbase + r0 + n, cbase:cbase + W], ap, writes=tok)

        def compute():
            nch = W // 128
            per = 512 // R
            for c0 in range(0, nch, per):
                bk = at_next()
                n = min(per, nch - c0)
                for c in range(c0, c0 + n):
                    tr(ps[bk][:, (c - c0) * R:(c - c0 + 1) * R], stage_rows[pbase:pbase + R, cbase + c * 128:cbase + (c + 1) * 128],
                       ident_f[pbase:pbase + R, pbase:pbase + R], tok + ["ident_f"], [("ps", bk)])
                CP(cp_eng(), dst[:, c0:c0 + n, :], ps[bk][:, 0:n * R].rearrange("p (c r) -> p c r", r=R), [("ps", bk)], [dst_tok])
        r2c_pending.append(compute)

    def r2c_flush():
        for f_ in r2c_pending:
            f_()
        del r2c_pending[:]

    def cols_to_rows(src_fn, R, W, dst_rows, src_toks):
        nch = W // 128
        for c0 in range(0, nch, 4):
            bk = at_next()
            n = min(4, nch - c0)
            for c in range(c0, c0 + n):
                tr(ps[bk][0:R, (c - c0) * 128:(c - c0 + 1) * 128], src_fn(c), ident_f[:, :], list(src_toks) + ["ident_f"], [("ps", bk)])
            CP(cp_eng(), stage_rows[0:R, c0 * 128:(c0 + n) * 128], ps[bk][0:R, 0:n * 128], [("ps", bk)], ACT_ALL)
        for (r0, n, ap) in dst_rows:
            S.dma(ap, stage_rows[r0:r0 + n, 0:W], reads=ACT_ALL)

    rows_to_cols([(l * 3, 3, cfw_d[l]) for l in range(L)] + [(6, 2, cfb_d)], 8, 2 * DFF, cfa[:], "cfa", pbase=0, cbase=0)
    rows_to_cols([(k * L, L, g_d[k]) for k in range(4)], 8, D, gall[:], "gall", pbase=32, cbase=0)
    rows_to_cols([(l * 3, 3, caw_d[l]) for l in range(L)], 6, 256, caw[:], "caw", pbase=32, cbase=1024)
    for l in range(L):
        rows_to_cols([(0, 32, sca_d[l].rearrange("s j f -> (s j) f"))], 32, 256, zst_s[:, l], ("zst_s", l), pbase=64, cbase=256 * l)
    r2c_flush()

    for l in range(L):
        S.dma(WsT_f.rearrange("p (g s) -> p g s", g=4), spw_d[l].rearrange("g t s -> t g s"), writes=["WsT_f"])
        bk = at_next()
        for g in range(4):
            tr(ps[bk][:, g * 128:(g + 1) * 128], WsT_f[:, g * 128:(g + 1) * 128], ident_f[:], ["WsT_f", "ident_f"], [("ps", bk)])
        CP("dve", WsT_f, ps[bk][:, :], [("ps", bk)], ["WsT_f"])
        S.add("pool", lambda e: e.affine_select(out=WsT_f.rearrange("p (g t) -> p g t", g=4), in_=WsT_f.rearrange("p (g t) -> p g t", g=4),
                                                 pattern=[[0, 4], [1, 128]], base=0, channel_multiplier=-1, compare_op=ALU.is_ge, fill=0.0),
              ["WsT_f"], ["WsT_f"])
        CP("dve", WsT[:, l, :, :], WsT_f.rearrange("p (g t) -> p g t", g=4), ["WsT_f"], [("WsT", l)])
    MEMSET("pool", WbdF, 0.0, ["WbdF"])
    for i in range(NSEQ):
        S.dma(WbdF[4 * i:4 * i + 4, :, 4 * i:4 * i + 4], spw_d[:, :, 0:4, 0:4].rearrange("l g t s -> t (l g) s"), writes=["WbdF"])
    bk = at_next()
    for a_ in range(L * 4):
        tr(ps[bk][0:64, a_ * 64:(a_ + 1) * 64], WbdF[:, a_, :], ident_f[0:64, 0:64], ["WbdF", "ident_f"], [("ps", bk)])
    CP("dve", WbdF, ps[bk][0:64, :].rearrange("p (a b) -> p a b", a=L * 4), [("ps", bk)], ["WbdF"])
    S.add("pool", lambda e: e.affine_select(out=WbdF, in_=WbdF, pattern=[[0, L * 4], [1, 64]], base=0, channel_multiplier=-1,
                                             compare_op=ALU.is_ge, fill=0.0), ["WbdF"], ["WbdF"])
    CP("dve", Wbd[:], WbdF, ["WbdF"], ["Wbd"])
    MEMSET("pool", brow[:], 0.0, ["brow"])
    MEMSET("pool", brow_s[:], 0.0, ["brow_s"])
    flat_b = spb_d.rearrange("l g t -> (l g t)")
    MEMSET("pool", bst[0:64, :], 0.0, ["bst"])
    for p0 in (0, 32):
        S.dma(bst[p0:p0 + 1, :], flat_b.partition_broadcast(1), writes=["bst"])
    CP("dve", bstb[0:64, :], bst[0:64, :], ["bst"], ["bstb"])
    TT("dve", bst2[0:64, :], bst[0:64, :], bstb[0:64, :], ALU.subtract, ["bst", "bstb"], ["bst2"])
    CP("dve", brow[0:1].rearrange("p a t -> p (a t)"), bstb[0:1, :], ["bstb", "brow"], ["brow"])
    CP("dve", brow[32:33].rearrange("p a t -> p (a t)"), bst2[32:33, :], ["bst2", "brow"], ["brow"])
    for sq_ in range(NSEQ):
        CP("dve", brow_s[0:1, :, 4 * sq_:4 * sq_ + 4], bstb[0:1, :].rearrange("p (a t) -> p a t", t=128)[:, :, 0:4], ["bstb", "brow_s"], ["brow_s"])
        CP("dve", brow_s[32:33, :, 4 * sq_:4 * sq_ + 4], bst2[32:33, :].rearrange("p (a t) -> p a t", t=128)[:, :, 0:4], ["bst2", "brow_s"], ["brow_s"])
    MEMSET("pool", kT[:], 0.0, [("kT", 0), ("kT", 1)])
    MEMSET("pool", vtok[:], 0.0, [("vtok", l, s) for l in range(L) for s in range(5)])
    MEMSET("pool", prev_p[:], 0.0, [("prev_p", l, c) for l in range(L) for c in range(44)])
    MEMSET("pool", zc_p[:], 0.0, [("zc_p", 0), ("zc_p", 1)])

    S.barrier()
    def w_in_src(l, c0, n):
        return w_in_d[l].rearrange("(kc p) n -> p kc n", p=128)[:, :, c0:c0 + n]

    def groups(l):
        gs = []
        gs.append(dict(K=8, cols=512, pieces=[(0, 8, 0, w_in_src(l, 0, 256)), (0, 8, 256, w_in_src(l, 512, 256))]))
        gs.append(dict(K=8, cols=512, pieces=[(0, 8, 0, w_in_src(l, 256, 256)), (0, 8, 256, w_in_src(l, 1536, 256))]))
        gs.append(dict(K=8, cols=512, pieces=[(0, 8, 0, w_in_src(l, 768, 512))], qperm=True))
        gs.append(dict(K=8, cols=512, pieces=[(0, 8, 0, w_in_src(l, 1280, 256)), (0, 8, 256, w_in_src(l, 1792, 256))]))
        for br, wd, kb in ((0, wba_d, 2), (2, wbc_d, 2), (1, wbb_d, 4)):
            gs.append(dict(K=8, cols=512, pieces=[(0, 8, 0, w_in_src(l, 2048 + br * 1024, 512))]))
            if br == 1:
                pb = []
                for X in range(4):
                    for hk in range(2):
                        hd = hk * 4 + X
                        pb.append((X, 1, 0, wd[l][hd * 64:(hd + 1) * 64, :].rearrange("p (k n) -> p k n", k=1), (hk * 64, 64)))
                gs.append(dict(K=kb, cols=1024, pieces=pb))
            else:
                gs.append(dict(K=kb, cols=1024, pieces=[(0, kb, 0, wd[l].rearrange("(kc p) n -> p kc n", p=128))]))
            gs.append(dict(K=8, cols=512, pieces=[(0, 8, 0, w_in_src(l, 2048 + br * 1024 + 512, 512))]))
        for hf in range(2):
            gs.append(dict(K=8, cols=512, pieces=[(0, 8, 0, wo_d[l].rearrange("(kc p) n -> p kc n", p=128)[:, :, hf * 512:(hf + 1) * 512])]))
        wu = wup_d[l].rearrange("(kc p) n -> p kc n", p=128)
        for gi in range(11):
            pcs = [(0, 8, 0, wu[:, :, gi * 256:(gi + 1) * 256]), (0, 8, 256, wu[:, :, (22 + 2 * gi) * 128:(24 + 2 * gi) * 128])]
            gs.append(dict(K=8, cols=512, pieces=pcs))
        wdn = wdn_d[l].rearrange("(kc p) n -> p kc n", p=128)
        for hf in range(2):
            for (k0, nk) in ((0, 8), (8, 8), (16, 6)):
                gs.append(dict(K=nk, cols=512, pieces=[(0, nk, 0, wdn[:, k0:k0 + nk, hf * 512:(hf + 1) * 512])]))
        assert len(gs) == NG
        return gs

    GROUPS = [groups(l) for l in range(L)]

    segs = [Seg(i, False, b0, nb) for i, (b0, nb) in enumerate(TILES)] + [Seg(len(TILES), True, 0, 0)]
    items = [(l, gi) for _ in segs for l in range(L) for gi in range(NG)]
    wstate = {"next_pref": 0, "next_get": 0}

    cvstate = {"n": 0}

    def convert_into_slot(l, gi, slot):
        g = GROUPS[l][gi]
        K, cols = g["K"], g["cols"]
        base, rem = divmod(K, 4)
        bounds = []
        k_ = 0
        for q_ in range(4):
            n_ = base + (1 if q_ < rem else 0)
            bounds.append((k_, k_ + n_))
            k_ += n_
        for q_, (ka, kb) in enumerate(bounds):
            wtok = [("ring", slot, t_) for t_ in range(q_, 4)]
            if kb > ka:
                b = cvstate["n"] % 4
                cvstate["n"] += 1
                n_el = (kb - ka) * cols
                stv = cvs[:, b, 0:n_el].rearrange("p (k n) -> p k n", k=kb - ka)
                npc = 0
                for pc in g["pieces"]:
                    k0, nk, c0, src = pc[:4]
                    p0, np_ = pc[4] if len(pc) > 4 else (0, 128)
                    lo, hi = max(k0, ka), min(k0 + nk, kb)
                    if lo >= hi:
                        continue
                    ncol = src.shape[-1]
                    S.dma(stv[p0:p0 + np_, lo - ka:hi - ka, c0:c0 + ncol], src[:, lo - k0:hi - k0, :], writes=[("cst", b, npc)])
                    npc += 1
                rtok = [("cst", b, pi) for pi in range(8)]
                if g.get("qperm"):
                    for kk in range(kb - ka):
                        src_v = cvs[:, b, kk * cols:(kk + 1) * cols].rearrange("p (hk g d) -> p g hk d", hk=2, g=4)
                        dst_v = ring[:, slot, (ka + kk) * cols:(ka + kk + 1) * cols].rearrange("p (g hk d) -> p g hk d", g=4, hk=2)
                        CP("act", dst_v, src_v, rtok, wtok)
                else:
                    CP("dve", ring[:, slot, ka * cols:kb * cols], cvs[:, b, 0:n_el], rtok, wtok)
            if q_ in (1, 3):
                sa, sb_ = bounds[q_ - 1][0], bounds[q_][1]
                if sb_ > sa:
                    S.dma(wbf_d[l * NG + gi][:, sa * cols:sb_ * cols], ring[:, slot, sa * cols:sb_ * cols],
                          reads=[("ring", slot, q_ - 1), ("ring", slot, q_)], writes=[("wbf", l, gi, q_ // 2)], q="pool")

    def prefetch_upto(k):
        while wstate["next_pref"] <= k and wstate["next_pref"] < len(items):
            i = wstate["next_pref"]
            l, gi = items[i]
            g = GROUPS[l][gi]
            used = g["K"] * g["cols"]
            slot = i % NSLOT
            if i < L * NG:
                convert_into_slot(l, gi, slot)
            else:
                S.dma(ring[:, slot, 0:used], wbf_d[l * NG + gi][:, 0:used], reads=[("wbf", l, gi, 0), ("wbf", l, gi, 1)],
                      writes=[("ring", slot, t_) for t_ in range(4)])
            wstate["next_pref"] += 1

    def wget(l_expect, gi_expect, hold=0):
        i = wstate["next_get"]
        l, gi = items[i]
        assert (l, gi) == (l_expect, gi_expect), (l, gi, l_expect, gi_expect)
        prefetch_upto(i + NSLOT - 1 - hold)
        wstate["next_get"] += 1
        g = GROUPS[l][gi]
        slot = i % NSLOT
        view = ring[:, slot, 0:g["K"] * g["cols"]].rearrange("p (k n) -> p k n", k=g["K"])
        return view, [("ring", slot, t_) for t_ in range(4)]

    def xtok(seg, c):
        return [("x", c, j) for j in range(seg.nbt)]

    STG = [(xin[:, 0, :], [("xin", 0)]),
           (tmp[:, :, :].rearrange("p a b -> p (a b)"), [("tmp", 0), ("tmp", 1)]),
           (ct[:, 0:2, :].rearrange("p a b -> p (a b)"), [("ct", 0), ("ct", 1)]),
           (ct[:, 2:4, :].rearrange("p a b -> p (a b)"), [("ct", 2), ("ct", 3)])]
    LOAD_STG = [0, 2, 3, 0]
    STORE_STG = [1, 2, 3, 1]

    def x_src(seg, j):
        return xs_d if seg.sample else xp_d[(seg.b0 + j) * 128:(seg.b0 + j + 1) * 128, :]

    def preload_x0(seg):
        R = 64 if seg.sample else 128
        buf, toks = STG[LOAD_STG[0]]
        S.dma(buf[0:R, :], x_src(seg, 0), writes=toks)

    def load_x(seg, skip_first_dma=False):
        for j in range(seg.nbt):
            R = 64 if seg.sample else 128
            buf, toks = STG[LOAD_STG[j]]
            if not (j == 0 and skip_first_dma):
                S.dma(buf[0:R, :], x_src(seg, j), writes=toks)
            for hf in range(2):
                bk = at_next()
                for cc in range(4):
                    c = hf * 4 + cc
                    tr(ps[bk][:, cc * R:(cc + 1) * R], buf[0:R, c * 128:(c + 1) * 128], ident_f[0:R, 0:R], toks + ["ident_f"], [("ps", bk)])
                CP(cp_eng(), xT[:, hf * 4:hf * 4 + 4, j * 128:j * 128 + R], ps[bk][:, 0:4 * R].rearrange("p (c t) -> p c t", c=4),
                   [("ps", bk)], [("x", hf * 4 + cc, j) for cc in range(4)])

    def store_x(seg):
        for j in range(seg.nbt):
            R = 64 if seg.sample else 128
            buf, toks = STG[STORE_STG[j]]
            for hf in range(2):
                bk = at_next()
                for cc in range(4):
                    c = hf * 4 + cc
                    tr(ps[bk][0:R, cc * 128:(cc + 1) * 128], xT[:, c, j * 128:j * 128 + R], ident_f[:, :], [("x", c, j), "ident_f"], [("ps", bk)])
                CP(cp_eng(), buf[0:R, hf * 512:(hf + 1) * 512], ps[bk][0:R, 0:512], [("ps", bk)], toks)
            dst = ys_d if seg.sample else yp_d[(seg.b0 + j) * 128:(seg.b0 + j + 1) * 128, :]
            S.dma(dst, buf[0:R, :], reads=toks)

    def norm_stats(seg, eps=EPS):
        N = seg.N
        for c in range(8):
            mm(ps[4][:, :N], ones_b[:, :], h[:, c, :N], c == 0, c == 7, [("h", c), "ones"], [("ps", 4)])
        ACT(rstd[:, :N], ps[4][:, :N], AF.Sqrt, [("ps", 4)], ["rstd"], scale=1.0 / D, bias=eps)
        S.add("dve", lambda e: e.reciprocal(out=rstd[:, :N], in_=rstd[:, :N]), ["rstd"], ["rstd"])

    def rmsnorm_to_h(seg, l, kind):
        N = seg.N
        for c in range(8):
            ACT(h[:, c, :N], xT[:, c, :N], AF.Square, xtok(seg, c), [("h", c)])
        norm_stats(seg)
        for c in range(8):
            STT(h[:, c, :N], xT[:, c, :N], gall[:, c, kind * L + l:kind * L + l + 1], rstd[:, :N], ALU.mult, ALU.mult,
                xtok(seg, c) + ["rstd", "gall"], [("h", c)])

    def post_norm_add(seg, l, kind):
        N = seg.N
        norm_stats(seg, 4.0 * EPS if kind == 1 else EPS)
        for m in range(8):
            STT(tmp[:, m % 2, :N], mo[:, m, :N], gall[:, m, kind * L + l:kind * L + l + 1], rstd[:, :N], ALU.mult, ALU.mult,
                [("mo", m), "rstd", "gall"], [("tmp", m % 2)])
            TT(ADD_ENG, xT[:, m, :N], xT[:, m, :N], tmp[:, m % 2, :N], ALU.add, xtok(seg, m) + [("tmp", m % 2)], xtok(seg, m))

    def v3(ap2d, seg, off=0):
        return ap2d[:, 0:seg.nseq * (seg.T + off)].rearrange("p (s t) -> p s t", s=seg.nseq)

    def out_proj_evac(seg, bk, m):
        N = seg.N
        ACT(h[:, m, :N], ps[bk][:, :N], AF.Square, [("ps", bk)], [("h", m)])
        CP("act", mo[:, m, :N], ps[bk][:, :N], [("ps", bk)], [("mo", m)])

    def mixer(seg, l):
        N, T, nseq = seg.N, seg.T, seg.nseq
        rmsnorm_to_h(seg, l, 0)
        _chk("norm")
        hr = lambda kc: [("h", kc)]

        def proj_chunk(wv, wt, K, col0, rhs, rreads, evac):
            bk = mm_next()
            for kc in range(K):
                mm(ps[bk][:, :N], wv[:, kc, col0:col0 + 128], rhs(kc), kc == 0, kc == K - 1, wt + rreads(kc), [("ps", bk)])
            evac(bk)

        hrhs = lambda kc: h[:, kc, :N]
        wv, wt = wget(l, 0)
        for c in range(2):
            proj_chunk(wv, wt, 8, c * 128, hrhs, hr, lambda bk, c=c: CP("act", a_in[:, c, :N], ps[bk][:, :N], [("ps", bk)], [("tmp", c)]))
        for c in range(2):
            def ev(bk, c=c):
                zv = v3(zfull[:, c, :], seg, 2)
                TT("dve", zv[:, :, 2:2 + T], v3(ps[bk], seg), v3(a_in[:, c, :], seg), ALU.mult, [("ps", bk), ("tmp", c)], [("z", c)])
            proj_chunk(wv, wt, 8, 256 + c * 128, hrhs, hr, ev)
        wv, wt = wget(l, 1)
        for c in range(2):
            proj_chunk(wv, wt, 8, c * 128, hrhs, hr, lambda bk, c=c: CP("act", a_b[:, c, :N], ps[bk][:, :N], [("ps", bk)], [("a_b", c)]))
        for c in range(2):
            proj_chunk(wv, wt, 8, 256 + c * 128, hrhs, hr, lambda bk, c=c: CP("act", c_u[:, c, :N], ps[bk][:, :N], [("ps", bk)], [("c_u", c)]))
        if seg.sample:
            attn_sample_prep(seg, l)
        wv, wt = wget(l, 2)
        for X in range(4):
            proj_chunk(wv, wt, 8, X * 128, hrhs, hr, lambda bk, X=X: ACT(qT[:, X, :N], ps[bk][:, :N], AF.Copy, [("ps", bk)], [("q", X)], scale=0.125))
        wv, wt = wget(l, 3)
        if seg.sample:
            proj_chunk(wv, wt, 8, 0, hrhs, hr, lambda bk: CP("act", KT_all[:, :, 128:132], v3(ps[bk], seg), [("ps", bk)], ["KT_new"]))
        else:
            proj_chunk(wv, wt, 8, 0, hrhs, hr, lambda bk: CP("act", kT[:, l, 128:128 + N], ps[bk][:, :N], [("ps", bk)], [("kT", l)]))
        for j in range(seg.nbt):
            R = 64 if seg.sample else 128
            bk = at_next()
            c_lo = 0 if (seg.sample or (seg.last and j == seg.nb - 1)) else 128
            for kc in range(8):
                mm(ps[bk][0:R, c_lo:512], h[:, kc, j * 128:j * 128 + R], wv[:, kc, c_lo:512], kc == 0, kc == 7, wt + [("h", kc)], [("ps", bk)])
            if not seg.sample:
                CP("act", vtok[:, l, 1 + j, :], ps[bk][:, 128:256], [("ps", bk)], [("vtok", l, 1 + j)])
            ACT(junk[0:R, :], ps[bk][0:R, 256:512], AF.Square, [("ps", bk)], ["junk", ("st4", 0)], accum_out=st4[0:R, 0, 0:1])
            ACT(st4[0:R, 0, 0:1], st4[0:R, 0, 0:1], AF.Sqrt, [("st4", 0)], [("st4", 0)], scale=1.0 / 256, bias=EPS)
            S.add("dve", lambda e, R=R: e.reciprocal(out=st4[0:R, 0, 0:1], in_=st4[0:R, 0, 0:1]), [("st4", 0)], [("st4", 0)])
            if seg.sample:
                STT(vno[0:R, :], ps[bk][0:R, 256:512], st4[0:R, 0, 0:1], gvb[0:R, l, :], ALU.mult, ALU.mult,
                    [("ps", bk), ("st4", 0), ("gvb", l)], ["vno"])
                CP("dve", vn[0:R, 0, :], vno[0:R, :], ["vno"], [("vn", 0)])
                S.dma(cvs_d[l].rearrange("s t f -> (s t) f"), vno[0:R, :], reads=["vno"])
                CP("act", kvo[0:R, :], ps[bk][0:R, 0:256], [("ps", bk)], ["kvo"])
                S.dma(kvscr_d[l], kvo[0:R, :], reads=["kvo"], writes=[("kvscr", l)])
                S.dma(wks_d[l, :, 124:128, :], kvscr_d[l][:, 0:128].rearrange("(s t) f -> s t f", t=4), reads=[("kvscr", l)])
                S.dma(wvs_d[l, :, 124:128, :], kvscr_d[l][:, 128:256].rearrange("(s t) f -> s t f", t=4), reads=[("kvscr", l)])
                S.dma(vnt_f, kvscr_d[l][:, 128:256].rearrange("(s t) f -> t s f", t=4), reads=[("kvscr", l)], writes=["vnt_f"] + [("ct", c) for c in range(4)])
                CP("dve", vnt, vnt_f, ["vnt_f"] + [("ct", c) for c in range(4)] + [("mbf", m) for m in range(4)], ["vnt"] + [("mbf", m) for m in range(4)])
                S.dma(wks_d[l, :, 0:124, :], ck_d[l, :, 4:128, :])
                S.dma(wvs_d[l, :, 0:124, :], cv_d[l, :, 4:128, :])
            else:
                STT(vn[:, j, :], ps[bk][:, 256:512], st4[:, 0, 0:1], gvb[:, l, :], ALU.mult, ALU.mult,
                    [("ps", bk), ("st4", 0), ("gvb", l)], [("vn", j)])
                if seg.last and j == seg.nb - 1:
                    CP("act", kvo[:, :], ps[bk][:, 0:256], [("ps", bk)], ["kvo"])
                    S.dma(wkp_d[l], kvo[:, 0:128], reads=["kvo"])
                    S.dma(wvp_d[l], kvo[:, 128:256], reads=["kvo"])

        _chk("proj")
        for c in range(2):
            zv = v3(zfull[:, c, :], seg, 2)
            if seg.sample:
                CP("pool", zv[:, :, 0:2], zst_s[:, l, c, :].rearrange("p (s j) -> p s j", j=2), [("zst_s", l)], [("z", c)])
            else:
                CP("pool", zv[:, :, 0:2], zc_p[:, l, c:c + 1, :], [("zc_p", l)], [("z", c)])
            c1 = v3(ct[:, c, :], seg)
            w = lambda i, c=c: caw[:, c, l * 3 + i:l * 3 + i + 1]
            TSC("dve", c1, zv[:, :, 0:T], w(0), None, ALU.mult, None, [("z", c), "caw"], [("ct", c)])
            STT(c1, zv[:, :, 1:T + 1], w(1), c1, ALU.mult, ALU.add, [("z", c), "caw", ("ct", c)], [("ct", c)])
            STT(c1, zv[:, :, 2:T + 2], w(2), c1, ALU.mult, ALU.add, [("z", c), "caw", ("ct", c)], [("ct", c)])
            TT("dve", ya[:, c, :N], ct[:, c, :N], a_b[:, c, :N], ALU.mult, [("ct", c), ("a_b", c)], [("ya", c)])
            if seg.sample:
                CP("pool", zout_s[:, c, :].rearrange("p (s j) -> p s j", j=2), zv[:, :, T:T + 2], [("z", c)], ["zout_s"])
            else:
                CP("pool", zc_p[:, l, c:c + 1, :], zv[:, :, T:T + 2], [("z", c)], [("zc_p", l)])
        if seg.sample:
            cols_to_rows(lambda c: zout_s[:, c, :], 32, 256, [(0, 32, cas_d[l].rearrange("s j f -> (s j) f"))], ["zout_s"])
        elif seg.last:
            cols_to_rows(lambda c: zc_p[:, l, c, :], 2, 256, [(0, 2, cap_d[l])], [("zc_p", l)])

        _chk("mixA")
        def mixer_c(seg, l):
            for j in range(seg.nbt):
                bk = at_next()
                if seg.sample:
                    for g in range(4):
                        o = ps[bk][(g % 2) * 64:(g % 2) * 64 + 64, (g // 2) * 64:(g // 2) * 64 + 64]
                        mm(o, vn[0:64, 0, g * 64:(g + 1) * 64], Wbd[0:64, l * 4 + g, :], True, False, [("vn", 0), "Wbd"], [("ps", bk)])
                        mm(o, ones_b[0:64, 0:64], brow_s[0:64, l * 4 + g, :], False, True, ["ones", "brow_s"], [("ps", bk)])
                    TT("dve", yc[:, :, 0:64], ps[bk][:, 0:128].rearrange("p (g t) -> p g t", g=2), c_u[:, :, 0:64], ALU.mult,
                       [("ps", bk), ("c_u", 0), ("c_u", 1)], [("yc", 0, 0), ("yc", 1, 0)])
                else:
                    for g in range(4):
                        o = ps[bk][(g % 2) * 64:(g % 2) * 64 + 64, (g // 2) * 128:(g // 2) * 128 + 128]
                        mm(o, vn[:, j, g * 64:(g + 1) * 64], WsT[:, l, g, :], True, False, [("vn", j), ("WsT", l)], [("ps", bk)])
                        mm(o, ones_b[:, 0:64], brow[:, l * 4 + g, :], False, True, ["ones", "brow"], [("ps", bk)])
                    TT("dve", yc[:, :, j * 128:(j + 1) * 128], ps[bk][:, 0:256].rearrange("p (g t) -> p g t", g=2), c_u[:, :, j * 128:(j + 1) * 128],
                       ALU.mult, [("ps", bk), ("c_u", 0), ("c_u", 1)], [("yc", 0, j), ("yc", 1, j)])


        def merge_branches(seg, l, bis, gi):
            order = ((0, ya, 2, lambda kc: [("ya", kc)]),
                     (2, yc, 2, lambda kc: [("yc", kc, j) for j in range(seg.nbt)]),
                     (1, yb, 4, lambda kc: [("yb", kc, j) for j in range(seg.nbt)] if not seg.sample else [("yb", kc, 0)]))
            for bi in bis:
                br, ybuf, kb, yreads = order[bi]
                wbr = wbrt = None
                for hf in range(2):
                    if hf == 0:
                        wg, wgt = wget(l, gi)
                    else:
                        wg, wgt = wget(l, gi + 2, hold=1)
                    for mi in range(4):
                        proj_chunk(wg, wgt, 8, mi * 128, hrhs, hr,
                                   lambda bk, mi=mi, hf=hf: ACT(gate[:, 0, mi, :N], ps[bk][:, :N], AF.Tanh, [("ps", bk)], [("gate", mi)], scale=0.5))
                        yield
                    if hf == 0:
                        wbr, wbrt = wget(l, gi + 1)
                    for mi in range(4):
                        m = hf * 4 + mi

                        def ev(bk, m=m, mi=mi, hf=hf, bi=bi):
                            if bi == 0:
                                STT(mo[:, m, :N], gate[:, 0, mi, :N], 1.0, ps[bk][:, :N], ALU.add, ALU.mult, [("ps", bk), ("gate", mi)], [("mo", m)])
                            else:
                                STT(tmp[:, m % 2, :N], gate[:, 0, mi, :N], 1.0, ps[bk][:, :N], ALU.add, ALU.mult, [("ps", bk), ("gate", mi)],
                                    [("tmp", m % 2)])
                                if bi == 1:
                                    TT(ADD_ENG, mo[:, m, :N], mo[:, m, :N], tmp[:, m % 2, :N], ALU.add, [("mo", m), ("tmp", m % 2)], [("mo", m)])
                                else:
                                    TT(ADD_ENG, mbf[:, m, :N], mo[:, m, :N], tmp[:, m % 2, :N], ALU.add, [("mo", m), ("tmp", m % 2)], [("mbf", m)])
                        proj_chunk(wbr, wbrt, kb, m * 128, lambda kc, ybuf=ybuf: ybuf[:, kc, :N], yreads, ev)
                        yield
                gi += 3

        WARM(AF.Exp)
        mixer_c(seg, l)
        filler = merge_branches(seg, l, (0, 1), 4)

        def fill(n=1):
            for _ in range(n):
                try:
                    next(filler)
                except StopIteration:
                    return

        if seg.sample:
            attn_sample(seg, l)
        else:
            PbV = [Pb, ct[:, 3, 0:264].bitcast(BF16).rearrange("p (a b) -> p a b", a=2)]
            PTsV = [PTs, ct[:, 2, 0:256].bitcast(BF16).rearrange("p (a b) -> p a b", a=2)]
            COLS = [(1, 3), (5, 6)]
            SBK = [4, 5, 6, 3]
            fine1 = [(nm_, 1, hh) for nm_ in ("Pb", "PTs") for hh in (0, 1)]
            S.add("dve", lambda e: e.memset(st4[:, 0, 2:3], 0.0), [], [("ct", 2), ("ct", 3)] + fine1)
            rr["mmn"] = 3
            rr["mm"] = rr["mm"] % 3
            sls = [slice(hh * 64, (hh + 1) * 64) for hh in (0, 1)]
            for j in range(seg.nb):
                mi_ = 1 if (seg.first and j == 0) else 0
                for Xp in range(2):
                    H4 = [(2 * Xp + g, hh) for g in range(2) for hh in range(2)]
                    for h4, (X, hh) in enumerate(H4):
                        hidx = hh * 4 + X
                        sb_ = ps[SBK[h4]]
                        tk = [("ps", SBK[h4])]
                        mm(sb_[:, 0:256], qT[sls[hh], X, j * 128:(j + 1) * 128], kT[sls[hh], l, j * 128:j * 128 + 256], True, False,
                           [("q", X), ("kT", l)], tk)
                        mm(sb_[:, 0:256], ident_b[:, :], maskb[:, mi_, :], False, False, ["ident_b", "maskb"], tk)
                        mm(sb_[:, 256:257], ident_b[:, :], sinkb[:, l, hidx:hidx + 1], False, True, ["ident_b", ("sinkb", l)], tk)
                    for h4, (X, hh) in enumerate(H4):
                        g = h4 // 2
                        cm, cs = COLS[g]
                        S.add("dve", lambda e, hh=hh, cm=cm, bkk=SBK[h4]: e.tensor_reduce(out=st4[:, hh, cm:cm + 1], in_=ps[bkk][:, 0:257], axis=AX.X,
                                                                                       op=ALU.max, negate=True), [("ps", SBK[h4])], [("st4m", g, hh)])
                    for h4, (X, hh) in enumerate(H4):
                        g = h4 // 2
                        cm, cs = COLS[g]
                        ACT(PbV[g][:, hh, 0:257], ps[SBK[h4]][:, 0:257], AF.Exp, [("ps", SBK[h4]), ("st4m", g, hh)], [("Pb", g, hh), ("rs4", h4)],
                            bias=st4[:, hh, cm:cm + 1], accum_out=rs4[:, h4:h4 + 1])
                    S.add("dve", lambda e: e.reciprocal(out=rs4[:, 0:4], in_=rs4[:, 0:4]), [("rs4", k) for k in range(4)], [("rs4", k) for k in range(4)])
                    for h4, (X, hh) in enumerate(H4):
                        g = h4 // 2
                        TSC("dve", PbV[g][:, hh, 0:256], PbV[g][:, hh, 0:256], rs4[:, h4:h4 + 1], None, ALU.mult, None,
                            [("Pb", g, hh), ("rs4", h4)], [("Pb", g, hh)])
                    fill(2)
                    for h4, (X, hh) in enumerate(H4):
                        g = h4 // 2
                        for kb in range(2):
                            tr(psT[:, h4 * 256 + kb * 128:h4 * 256 + (kb + 1) * 128], PbV[g][:, hh, kb * 128:(kb + 1) * 128], ident_b[:, :],
                               [("Pb", g, hh), "ident_b"], [("ps", 7)])
                    for g in range(2):
                        CP("act", PTsV[g][:, :, :].rearrange("p a b -> p (a b)"), psT[:, g * 512:(g + 1) * 512], [("ps", 7)], [("PTs", g, 0), ("PTs", g, 1)])
                    fill(2)
                    for g in range(2):
                        X = 2 * Xp + g
                        obk = mm_next()
                        for hh in range(2):
                            for kb in range(2):
                                mm(ps[obk][sls[hh], 0:128], vtok[:, l, j + kb, sls[hh]], PTsV[g][:, hh, kb * 128:(kb + 1) * 128], kb == 0, kb == 1,
                                   [("vtok", l, j + kb), ("PTs", g, hh)], [("ps", obk)])
                        CP("act", yb[:, X, j * 128:(j + 1) * 128], ps[obk][:, 0:128], [("ps", obk)], [("yb", X, j)])
            rr["mmn"] = 4
            S.add("dve", lambda e: e.memset(st4[:, 1, 2:3], 0.0), fine1, [("ct", 2), ("ct", 3)])
            CP("pool", kT[:, l, 0:128], kT[:, l, N:N + 128], [("kT", l)], [("kT", l)])
            CP("pool", vtok[:, l, 0, :], vtok[:, l, seg.nb, :], [("vtok", l, seg.nb)], [("vtok", l, 0)])

        _chk("attn")
        fill(100)
        for _ in merge_branches(seg, l, (2,), 10):
            pass
        WARM(AF.Sqrt)
        _chk("merge")
        for hf in range(2):
            wv, wt = wget(l, 13 + hf)
            for mi in range(4):
                m = hf * 4 + mi
                proj_chunk(wv, wt, 8, mi * 128, lambda kc: mbf[:, kc, :N], lambda kc: [("mbf", kc)], lambda bk, m=m: out_proj_evac(seg, bk, m))
        _chk("wout0")
        post_norm_add(seg, l, 1)
        _chk("wout")

    def attn_sample_prep(seg, l):
        for hfi in range(2):
            for which, src_d in ((0, ck_d), (1, cv_d)):
                stg = sstage[:, which]
                S.dma(stg, src_d[l, hfi * 8:(hfi + 1) * 8].rearrange("s k f -> k s f"), writes=[("sstage", which)])
                if which == 0:
                    for q4 in range(2):
                        bk = at_next()
                        for i in range(4):
                            tr(ps[bk][:, i * 128:(i + 1) * 128], stg[:, q4 * 4 + i, :], ident_f[:, :], [("sstage", 0), "ident_f"], [("ps", bk)])
                        s0 = hfi * 8 + q4 * 4
                        CP(cp_eng(), KT_all[:, s0:s0 + 4, 0:128], ps[bk][:, :].rearrange("p (s k) -> p s k", s=4), [("ps", bk)], ["KT_c"])
                else:
                    CP("pool", Vc_bf[:, hfi * 8:(hfi + 1) * 8, :], stg, [("sstage", 1)], ["Vc_bf"])
        MEMSET("pool", qbd[:], 0.0, ["qbd"])
        MEMSET("pool", PTn[:, :], 0.0, ["PTn"])

    def attn_sample(seg, l):
        for X in range(4):
            for hk in range(2):
                sl = slice(hk * 64, (hk + 1) * 64)
                CP("dve", qbd[sl, :, X * 8 + hk * 4:X * 8 + hk * 4 + 4], qT[sl, X, 0:64].rearrange("p (s t) -> p s t", t=4), [("q", X), "qbd"], ["qbd"])
        for i in range(4):
            bk = 5 + (i % 2)
            u = i % 2
            for s4 in range(4):
                s = i * 4 + s4
                mm(ps[bk][32 * s4:32 * s4 + 32, 0:132], qbd[:, s, :], KT_all[:, s, :], True, True, ["qbd", "KT_c", "KT_new"], [("ps", bk)],
                   tile_position=(0, 32 * s4))
            TT("dve", sm[:, u, 0:132], ps[bk][:, 0:132], mask_s[:, :], ALU.add, [("ps", bk), "mask_s"], [("sm", u)])
            S.add("dve", lambda e, u=u: e.tensor_reduce(out=st4[:, u, 1:2], in_=sm[:, u, 0:132], axis=AX.X, op=ALU.max), [("sm", u)], [("st4m", u)])
            TSC("dve", st4[:, u, 2:3], st4[:, u, 1:2], -1.0, nsinkc[:, l:l + 1], ALU.mult, ALU.min, [("st4m", u), ("nsinkc", l)], [("st4n", u)])
            ACT(Pb[:, u, 0:132], sm[:, u, 0:132], AF.Exp, [("sm", u), ("st4n", u)], [("Pb", u), ("st4r", u)], scale=1.0, bias=st4[:, u, 2:3],
                accum_out=st4[:, u, 3:4])
            ACT(st4[:, u, 4:5], sinkc[:, l:l + 1], AF.Exp, [("sinkc", l), ("st4n", u)], [("st4e", u)], bias=st4[:, u, 2:3])
            TT("dve", st4[:, u, 3:4], st4[:, u, 3:4], st4[:, u, 4:5], ALU.add, [("st4r", u), ("st4e", u)], [("st4r", u)])
            S.add("dve", lambda e, u=u: e.reciprocal(out=st4[:, u, 3:4], in_=st4[:, u, 3:4]), [("st4r", u)], [("st4r", u)])
            TSC("dve", Pb[:, u, 0:132], Pb[:, u, 0:132], st4[:, u, 3:4], None, ALU.mult, None, [("Pb", u), ("st4r", u)], [("Pb", u)])
            tr(psT[:, i * 128:(i + 1) * 128], Pb[:, u, 0:128], ident_b[:, :], [("Pb", u), "ident_b"], PT_ALL)
            tr(psT[0:4, 512 + i * 128:512 + (i + 1) * 128], Pb[:, u, 128:132], ident_b[:, :], [("Pb", u), "ident_b"], PT_ALL)
        CP("act", PT_all[:, :], psT[:, 0:512], PT_ALL, ["PT_all"])
        CP("dve", PTn[0:4, :], psT[0:4, 512:1024], PT_ALL + ["PTn"], ["PTn"])
        for s in range(NSEQ):
            mm(ps[5][:, 32 * s:32 * s + 32], Vc_bf[:, s, :], PT_all[:, 32 * s:32 * s + 32], True, False, ["Vc_bf", "PT_all"], [("ps", 5)])
            mm(ps[5][:, 32 * s:32 * s + 32], vnt[0:4, s, :], PTn[0:4, 32 * s:32 * s + 32], False, True, ["vnt", "PTn"], [("ps", 5)])
        ov = ps[5][:, :].rearrange("p (s k) -> p s k", s=16)
        for X in range(4):
            for hk in range(2):
                sl = slice(hk * 64, (hk + 1) * 64)
                CP(cp_eng(), yb[sl, X, 0:64].rearrange("p (s t) -> p s t", t=4), ov[sl, :, X * 8 + hk * 4:X * 8 + hk * 4 + 4], [("ps", 5)], [("yb", X, 0)])

    def ffn(seg, l):
        N, T, nseq = seg.N, seg.T, seg.nseq
        actv = act_s if seg.sample else act_p
        rmsnorm_to_h(seg, l, 2)
        if seg.sample:
            load_prev_s(l)
        pend_gate = []
        for gi in range(11):
            wv, wt = wget(l, 15 + gi)
            for pi in range(2):
                ca = gi * 2 + pi
                AB = (0, 1)
                cidx = [ca, ca + 22]
                ui = [(ca % 2) * 2, (ca % 2) * 2 + 1]
                uv = [v3(ust[:, ui[ab], :], seg, 2) for ab in AB]
                c1 = [v3(ct[:, ui[ab], :], seg) for ab in AB]
                bks = []
                for ab in AB:
                    bk = mm_next()
                    bks.append(bk)
                    for kc in range(8):
                        mm(ps[bk][:, :N], wv[:, kc, (ab * 2 + pi) * 128:(ab * 2 + pi + 1) * 128], h[:, kc, :N], kc == 0, kc == 7, wt + [("h", kc)], [("ps", bk)])
                for ab in AB:
                    if seg.sample:
                        CP("pool", uv[ab][:, :, 0:2], prev_s[:, cidx[ab], :].rearrange("p (s j) -> p s j", j=2), [("prev_s", cidx[ab])], [("ust", ui[ab])])
                    else:
                        CP("pool", uv[ab][:, :, 0:2], prev_p[:, l, cidx[ab]:cidx[ab] + 1, :], [("prev_p", l, cidx[ab])], [("ust", ui[ab])])
                for ab in AB:
                    CP("act", uv[ab][:, :, 2:2 + T], v3(ps[bks[ab]], seg), [("ps", bks[ab])], [("ust", ui[ab])])
                w = lambda i, ab: cfa[:, cidx[ab], l * 3 + i:l * 3 + i + 1]
                for ab in AB:
                    ACT(c1[ab], uv[ab][:, :, 0:T], AF.Identity, [("ust", ui[ab]), "cfa"], [("ct", ui[ab])], scale=w(0, ab), bias=cfa[:, cidx[ab], 6 + l:7 + l])
                for ab in AB:
                    STT(c1[ab], uv[ab][:, :, 1:T + 1], w(1, ab), c1[ab], ALU.mult, ALU.add, [("ust", ui[ab]), "cfa", ("ct", ui[ab])], [("ct", ui[ab])])
                for ab in AB:
                    STT(c1[ab], uv[ab][:, :, 2:T + 2], w(2, ab), c1[ab], ALU.mult, ALU.add, [("ust", ui[ab]), "cfa", ("ct", ui[ab])], [("ct", ui[ab])])
                for ab in AB:
                    if seg.sample:
                        CP("pool", prev_s[:, cidx[ab], :].rearrange("p (s j) -> p s j", j=2), uv[ab][:, :, T:T + 2], [("ust", ui[ab])], [("prev_s", cidx[ab])])
                    else:
                        CP("pool", prev_p[:, l, cidx[ab]:cidx[ab] + 1, :], uv[ab][:, :, T:T + 2], [("ust", ui[ab])], [("prev_p", l, cidx[ab])])
                if pend_gate:
                    pend_gate.pop()()

                def gate_fn(ua=ui[0], ub=ui[1], ca=ca):
                    ACT(ct[:, ua, :N], ct[:, ua, :N], AF.Silu, [("ct", ua)], [("ct", ua)])
                    TT("dve", actv[:, ca, :N], ct[:, ua, :N], ct[:, ub, :N], ALU.mult, [("ct", ua), ("ct", ub)], [("act", ca)])
                pend_gate.append(gate_fn)
        if pend_gate:
            pend_gate.pop()()
        WARM(AF.Sqrt)
        _chk("up")
        gi = 26
        for hf in range(2):
            banks = [mm_next() for _ in range(4)]
            k0 = 0
            for part, nk in enumerate((8, 8, 6)):
                wv, wt = wget(l, gi)
                gi += 1
                for mi in range(4):
                    for kc in range(nk):
                        mm(ps[banks[mi]][:, :N], wv[:, kc, mi * 128:(mi + 1) * 128], actv[:, k0 + kc, :N], (part == 0 and kc == 0),
                           (part == 2 and kc == nk - 1), wt + [("act", k0 + kc)], [("ps", banks[mi])])
                k0 += nk
            for mi in range(4):
                out_proj_evac(seg, banks[mi], hf * 4 + mi)
        _chk("down")
        post_norm_add(seg, l, 3)
        if seg.sample:
            store_prev_s(l)
        elif seg.last:
            cols_to_rows(lambda c: prev_p[:, l, c, :], 2, 2 * DFF, [(0, 2, ffp_d[l])], [("prev_p", l, c) for c in range(44)])

    rowst = mo[:, :, :].rearrange("p a b -> p (a b)")

    def load_prev_s(l):
        src = sff_d[l].rearrange("s j f -> (s j) f")
        for part, (c0, n) in enumerate(((0, 32), (32, 12))):
            S.dma(rowst[0:32, 0:n * 128], src[:, c0 * 128:(c0 + n) * 128], writes=MO_ALL)
            for cc0 in range(0, n, 16):
                bk = at_next()
                nn = min(16, n - cc0)
                for c in range(cc0, cc0 + nn):
                    tr(ps[bk][:, (c - cc0) * 32:(c - cc0 + 1) * 32], rowst[0:32, c * 128:(c + 1) * 128], ident_f[0:32, 0:32], MO_ALL + ["ident_f"], [("ps", bk)])
                CP(cp_eng(), prev_s[:, c0 + cc0:c0 + cc0 + nn, :], ps[bk][:, 0:nn * 32].rearrange("p (c r) -> p c r", r=32), [("ps", bk)],
                   [("prev_s", c) for c in range(c0 + cc0, c0 + cc0 + nn)])

    def store_prev_s(l):
        dst = ffs_d[l].rearrange("s j f -> (s j) f")
        for part, (c0, n) in enumerate(((0, 32), (32, 12))):
            for cc0 in range(0, n, 4):
                bk = at_next()
                for c in range(cc0, cc0 + 4):
                    tr(ps[bk][0:32, (c - cc0) * 128:(c - cc0 + 1) * 128], prev_s[:, c0 + c, :], ident_f[:, :], [("prev_s", c0 + c), "ident_f"], [("ps", bk)])
                CP(cp_eng(), rowst[0:32, cc0 * 128:(cc0 + 4) * 128], ps[bk][0:32, 0:512], [("ps", bk)], MO_ALL)
            S.dma(dst[:, c0 * 128:(c0 + n) * 128], rowst[0:32, 0:n * 128], reads=MO_ALL)

    try:
        for si, seg in enumerate(segs):
            if seg.sample:
                S.barrier()
            load_x(seg, skip_first_dma=(si > 0 and not seg.sample))
            _chk("load")
            for l in range(L):
                mixer(seg, l)
                _chk("mixer")
                ffn(seg, l)
                _chk("ffn")
            if si + 1 < len(segs) and not segs[si + 1].sample:
                preload_x0(segs[si + 1])
            store_x(seg)
            _chk("tile")
    except _Stop:
        pass

    stats = S.finalize()
    stats['lane_max'] = max(S.lane_cnt)
    stats['sbuf_left'] = nc.sbuf_bytes_remaining
    es.close()
    return nc, stats


_CACHE = {}


def kernel(x_prompt, x_sample, state_conv_a, cache_win_k, cache_win_v, state_ffn_conv,
           w_in, conv_a_w, attn_sinks, spatial_w, spatial_b, g_v, w_branch_a, w_branch_b,
           w_branch_c, w_out, g_pre_mix, g_post_mix, g_pre_ffn, g_post_ffn, w_up,
           conv_ffn_w, conv_ffn_b, w_down):
    f = lambda a: np.ascontiguousarray(np.asarray(a, dtype=np.float32))
    if "nc" not in _CACHE:
        _CACHE["nc"] = build_program()
    nc, stats = _CACHE["nc"]
    shared = {
        "w_in": f(w_in), "conv_a_w": f(conv_a_w), "attn_sinks": f(attn_sinks), "spatial_w": f(spatial_w),
        "spatial_b": f(spatial_b), "g_v": f(g_v), "w_branch_a": f(w_branch_a), "w_branch_b": f(w_branch_b),
        "w_branch_c": f(w_branch_c), "w_out": f(w_out), "g_pre_mix": f(g_pre_mix), "g_post_mix": f(g_post_mix),
        "g_pre_ffn": f(g_pre_ffn), "g_post_ffn": f(g_post_ffn), "w_up": f(w_up), "conv_ffn_w": f(conv_ffn_w),
        "conv_ffn_b": f(conv_ffn_b), "w_down": f(w_down),
    }
    xp = np.asarray(x_prompt, dtype=np.float32)
    xs = np.asarray(x_sample, dtype=np.float32)
    sca = np.asarray(state_conv_a, dtype=np.float32)
    ck = np.asarray(cache_win_k, dtype=np.float32)
    cv = np.asarray(cache_win_v, dtype=np.float32)
    sff = np.asarray(state_ffn_conv, dtype=np.float32)
    in_maps = []
    for c in range(8):
        b, half = c // 2, c % 2
        blk0 = 0 if half == 0 else 14
        m = dict(shared)
        m["xp"] = f(xp[b, blk0 * 128:(blk0 + NBLK) * 128, :])
        sl = slice(c * NSEQ, (c + 1) * NSEQ)
        m["xs"] = f(xs[sl].reshape(NS, D))
        m["sca"] = f(sca[:, sl])
        m["ck"] = f(ck[:, sl].reshape(L, NSEQ, 128, 128))
        m["cv"] = f(cv[:, sl].reshape(L, NSEQ, 128, 128))
        m["sff"] = f(sff[:, sl])
        in_maps.append(m)
    res = run_bass_kernel_spmd(nc, in_maps, core_ids=list(range(8)))
    R = res.results
    B = 4
    y_prompt = np.zeros((B, 4096, D), np.float32)
    ca_p = np.zeros((L, B, 2, 256), np.float32)
    wk_p = np.zeros((L, B, 128, 2, 64), np.float32)
    wv_p = np.zeros((L, B, 128, 2, 64), np.float32)
    ff_p = np.zeros((L, B, 2, 2 * DFF), np.float32)
    y_sample = np.zeros((128, TS_, D), np.float32)
    ca_s = np.zeros((L, 128, 2, 256), np.float32)
    wk_s = np.zeros((L, 128, 128, 2, 64), np.float32)
    wv_s = np.zeros((L, 128, 128, 2, 64), np.float32)
    ff_s = np.zeros((L, 128, 2, 2 * DFF), np.float32)
    cv_s = np.zeros((L, 128, TS_, 256), np.float32)
    for c in range(8):
        b, half = c // 2, c % 2
        r = R[c]
        if half == 0:
            y_prompt[b, 0:17 * 128] = r["yp"][0:17 * 128]
        else:
            y_prompt[b, 17 * 128:] = r["yp"][3 * 128:]
            ca_p[:, b] = r["ca_p"]
            wk_p[:, b] = r["wk_p"].reshape(L, 128, 2, 64)
            wv_p[:, b] = r["wv_p"].reshape(L, 128, 2, 64)
            ff_p[:, b] = r["ff_p"]
        sl = slice(c * NSEQ, (c + 1) * NSEQ)
        y_sample[sl] = r["ys"].reshape(NSEQ, TS_, D)
        ca_s[:, sl] = r["ca_s"]
        wk_s[:, sl] = r["wk_s"].reshape(L, NSEQ, 128, 2, 64)
        wv_s[:, sl] = r["wv_s"].reshape(L, NSEQ, 128, 2, 64)
        ff_s[:, sl] = r["ff_s"]
        cv_s[:, sl] = r["cv_s"]
    return (y_prompt, y_sample, ca_p, wk_p, wv_p, ff_p, ca_s, wk_s, wv_s, ff_s, cv_s)
```

```python
import numpy as np
from contextlib import ExitStack
import concourse.bass as bass
import concourse.mybir as mybir
from concourse.bass_utils import run_bass_kernel_spmd

F32 = mybir.dt.float32
BF16 = mybir.dt.bfloat16
I32 = mybir.dt.int32
ALU = mybir.AluOpType
AF = mybir.ActivationFunctionType
AX = mybir.AxisListType

ENGS = ("pe", "act", "dve", "pool", "sp")
N_LANES = 24
N_PLANES = 4
SAME_ENGINE_SYNC = True
FUSE_WAIT = True

L = 2
D = 1024
NBLK = 18
NS = 64
NSEQ = 16
TS_ = 4
DFF = 2816
NG = 32
NSLOT = 4
EPS = 1e-6
NEG = -1e30
TILES = [(0, 4), (4, 4), (8, 4), (12, 4), (16, 2)]


class Op:
    __slots__ = ("eng", "fn", "deps", "is_dma", "sem", "val", "milestone")

    def __init__(self, eng, fn, deps, is_dma=False):
        self.eng = eng
        self.fn = fn
        self.deps = deps
        self.is_dma = is_dma
        self.sem = None
        self.val = None
        self.milestone = False


class Res:
    __slots__ = ("w", "r")

    def __init__(self):
        self.w = None
        self.r = []


class Sched:
    def __init__(self, nc):
        self.nc = nc
        self.ops = {e: [] for e in ENGS}
        self.esem = {e: nc.alloc_semaphore("s_" + e) for e in ENGS}
        self.lanes = [nc.alloc_semaphore("s_dma%d" % i) for i in range(N_LANES + N_PLANES)]
        self.lane_cnt = [0] * (N_LANES + N_PLANES)
        self.lane_last = [None] * (N_LANES + N_PLANES)
        self.lane_rr = 0
        self.plane_rr = 0
        self.res = {}

    def _res(self, t):
        r = self.res.get(t)
        if r is None:
            r = self.res[t] = Res()
        return r

    def _collect(self, reads, writes):
        deps = []
        for t in reads:
            r = self._res(t)
            if r.w is not None:
                deps.append(r.w)
        for t in writes:
            r = self._res(t)
            if r.w is not None:
                deps.append(r.w)
            deps.extend(r.r)
        return deps

    def _update(self, op, reads, writes):
        for t in reads:
            self._res(t).r.append(op)
        for t in writes:
            r = self._res(t)
            r.w = op
            r.r = []

    @staticmethod
    def _excl(reads, writes):
        ex = [t for t in reads if isinstance(t, tuple) and t[0] == "ps"]
        if ex:
            reads = [t for t in reads if not (isinstance(t, tuple) and t[0] == "ps")]
            writes = list(writes) + ex
        return reads, writes

    def add(self, eng, fn, reads=(), writes=()):
        reads, writes = self._excl(reads, writes)
        deps = self._collect(reads, writes)
        op = Op(eng, fn, deps)
        self.ops[eng].append(op)
        self._update(op, reads, writes)
        return op

    def dma(self, out, in_, reads=(), writes=(), q="sp", **kw):
        reads, writes = self._excl(reads, writes)
        deps = self._collect(reads, writes)
        if q == "pool":
            lane = N_LANES + self.plane_rr
            self.plane_rr = (self.plane_rr + 1) % N_PLANES
        else:
            lane = self.lane_rr
            self.lane_rr = (self.lane_rr + 1) % N_LANES
        if self.lane_last[lane] is not None:
            deps.append(self.lane_last[lane])
        self.lane_cnt[lane] += 16

        def fn(e, out=out, in_=in_, kw=kw):
            return e.dma_start(out=out, in_=in_, **kw)

        op = Op(q, fn, deps, is_dma=True)
        op.sem = self.lanes[lane]
        op.val = self.lane_cnt[lane]
        self.lane_last[lane] = op
        self.ops[q].append(op)
        self._update(op, reads, writes)
        return op

    def barrier(self):
        deps = [op for op in self.lane_last if op is not None]
        for e in ENGS:
            for op in reversed(self.ops[e]):
                if not op.is_dma and op.fn is not None:
                    deps.append(op)
                    break
        for e in ENGS:
            self.ops[e].append(Op(e, None, list(deps)))

    def finalize(self, final_eng="sp"):
        nc = self.nc
        last_deps = [op for op in self.lane_last if op is not None]
        for e in ENGS:
            if e != final_eng:
                for op in reversed(self.ops[e]):
                    if not op.is_dma and op.fn is not None:
                        last_deps.append(op)
                        break
        self.ops[final_eng].append(Op(final_eng, None, last_deps))
        for e in ENGS:
            for op in self.ops[e]:
                for d in op.deps:
                    if d.is_dma:
                        continue
                    if d.eng != op.eng or (SAME_ENGINE_SYNC and d.eng != "pe"):
                        d.milestone = True
        for e in ENGS:
            c = 0
            for op in self.ops[e]:
                if op.is_dma:
                    continue
                if op.milestone:
                    c += 1
                    op.val = c
                    op.sem = self.esem[e]
        stats = {}
        with nc.Block() as block:
            def emit(e, eo):
                seen = {}
                nw = 0
                for op in self.ops[e]:
                    need = {}
                    for d in op.deps:
                        if not d.is_dma and d.eng == e and (not SAME_ENGINE_SYNC or e == "pe"):
                            continue
                        k = id(d.sem)
                        if seen.get(k, 0) >= d.val:
                            continue
                        if k not in need or need[k][1] < d.val:
                            need[k] = (d.sem, d.val)
                    waits = list(need.values())
                    for k, (sem, val) in need.items():
                        seen[k] = val
                    fused = None
                    if FUSE_WAIT and waits and op.fn is not None and not op.is_dma:
                        fused = waits.pop()
                    for (sem, val) in waits:
                        eo.wait_ge(sem, val)
                        nw += 1
                    if op.fn is None:
                        continue
                    ins = op.fn(eo)
                    if fused is not None:
                        ins._wait_ge(fused[0], fused[1])
                    if op.is_dma:
                        ins.then_inc(op.sem, 16)
                    elif op.milestone:
                        ins.then_inc(op.sem, 1)
                stats[e] = (len(self.ops[e]), nw)

            @block.tensor
            def _(eo):
                emit("pe", eo)

            @block.scalar
            def _(eo):
                emit("act", eo)

            @block.vector
            def _(eo):
                emit("dve", eo)

            @block.gpsimd
            def _(eo):
                emit("pool", eo)

            @block.sync
            def _(eo):
                emit("sp", eo)
        return stats


class Seg:
    def __init__(self, ti, sample, b0, nb):
        self.ti = ti
        self.sample = sample
        self.b0 = b0
        self.nb = nb
        self.N = NS if sample else nb * 128
        self.nseq = NSEQ if sample else 1
        self.T = TS_ if sample else self.N
        self.nbt = 1 if sample else nb
        self.first = (not sample) and b0 == 0
        self.last = (not sample) and (b0 + nb == NBLK)


class _Stop(Exception):
    pass


DEBUG_STOP = None
NOCONV = False
ADD_ENG = 'dve'
NCONV = None
LOADMODE = 0


def _chk(stage):
    if DEBUG_STOP == stage:
        raise _Stop()


def build_program():
    nc = bass.Bass("TRN2", target_bir_lowering=False)

    def din(name, shape):
        return nc.dram_tensor(name, list(shape), F32, kind="ExternalInput").ap()

    def dout(name, shape):
        return nc.dram_tensor(name, list(shape), F32, kind="ExternalOutput").ap()

    xp_d = din("xp", [NBLK * 128, D])
    xs_d = din("xs", [NS, D])
    sca_d = din("sca", [L, NSEQ, 2, 256])
    ck_d = din("ck", [L, NSEQ, 128, 128])
    cv_d = din("cv", [L, NSEQ, 128, 128])
    sff_d = din("sff", [L, NSEQ, 2, 2 * DFF])
    w_in_d = din("w_in", [L, D, 5120])
    caw_d = din("conv_a_w", [L, 3, 256])
    sinks_d = din("attn_sinks", [L, 2, 4])
    spw_d = din("spatial_w", [L, 4, 128, 128])
    spb_d = din("spatial_b", [L, 4, 128])
    gv_d = din("g_v", [L, 256])
    wba_d = din("w_branch_a", [L, 256, D])
    wbb_d = din("w_branch_b", [L, 512, D])
    wbc_d = din("w_branch_c", [L, 256, D])
    wo_d = din("w_out", [L, D, D])
    g_d = [din(n, [L, D]) for n in ("g_pre_mix", "g_post_mix", "g_pre_ffn", "g_post_ffn")]
    wup_d = din("w_up", [L, D, 2 * DFF])
    cfw_d = din("conv_ffn_w", [L, 3, 2 * DFF])
    cfb_d = din("conv_ffn_b", [L, 2 * DFF])
    wdn_d = din("w_down", [L, DFF, D])

    yp_d = dout("yp", [NBLK * 128, D])
    ys_d = dout("ys", [NS, D])
    cap_d = dout("ca_p", [L, 2, 256])
    wkp_d = dout("wk_p", [L, 128, 128])
    wvp_d = dout("wv_p", [L, 128, 128])
    ffp_d = dout("ff_p", [L, 2, 2 * DFF])
    cas_d = dout("ca_s", [L, NSEQ, 2, 256])
    wks_d = dout("wk_s", [L, NSEQ, 128, 128])
    wvs_d = dout("wv_s", [L, NSEQ, 128, 128])
    ffs_d = dout("ff_s", [L, NSEQ, 2, 2 * DFF])
    cvs_d = dout("cv_s", [L, NSEQ, TS_, 256])

    wbf_d = nc.dram_tensor("wbf", [L * NG, 128, 4096], BF16).ap()
    kvscr_d = nc.dram_tensor("kvscr", [L, NS, 256], F32).ap()

    es = ExitStack()

    def sb(name, shape, dt):
        return es.enter_context(nc.sbuf_tensor(name, list(shape), dt))

    def psum(name, shape, dt):
        return es.enter_context(nc.psum_tensor(name, list(shape), dt))

    S = Sched(nc)

    ring = sb("ring", [128, NSLOT, 4096], BF16)
    xT = sb("xT", [128, 8, 512], F32)
    xin = sb("xin", [128, 1, 1024], F32)
    xout = xin
    h = sb("h", [128, 8, 512], BF16)
    rstd = sb("rstd", [128, 512], F32)
    zfull = sb("zfull", [128, 2, 520], F32)
    a_b = sb("a_b", [128, 2, 512], F32)
    c_u = sb("c_u", [128, 2, 512], F32)
    qT = sb("qT", [128, 4, 512], BF16)
    kT = sb("kT", [128, L, 640], BF16)
    vtok = sb("vtok", [128, L, 5, 128], BF16)
    vn = sb("vn", [128, 4, 256], BF16)
    kvo = sb("kvo", [128, 256], F32)
    vno = sb("vno", [128, 256], F32)
    junk = sb("junk", [128, 256], F32)
    ya = sb("ya", [128, 2, 512], BF16)
    yb = sb("yb", [128, 4, 512], BF16)
    yc = sb("yc", [128, 2, 512], BF16)
    ct = sb("ct", [128, 4, 512], F32)
    gate = sb("gate", [128, 1, 4, 512], BF16)
    mo = sb("mo", [128, 8, 512], F32)
    mbf = sb("mbf", [128, 8, 512], BF16)
    tmp = sb("tmp", [128, 2, 512], F32)
    a_in = tmp
    act_raw = sb("act_raw", [128, 5632], F32)
    ust = sb("ust", [128, 4, 520], F32)
    prev_p = sb("prev_p", [128, L, 44, 2], F32)
    prev_s = sb("prev_s", [128, 44, 32], F32)
    zc_p = sb("zc_p", [128, L, 2, 2], F32)
    zst_s = sb("zst_s", [128, L, 2, 32], F32)
    zout_s = sb("zout_s", [128, 2, 32], F32)
    sm = sb("sm", [128, 2, 256], F32)
    Pb = sb("Pb", [128, 2, 264], BF16)
    PTs = sb("PTs", [128, 2, 256], BF16)
    st4 = sb("st4", [128, 2, 8], F32)
    rs4 = sb("rs4", [128, 4], F32)
    dmy = sb("dmy", [128, 4], F32)
    ident_f = sb("ident_f", [128, 128], F32)
    ident_b = sb("ident_b", [128, 128], BF16)
    ones_b = sb("ones_b", [128, 128], BF16)
    maskb = sb("maskb", [128, 2, 256], BF16)
    sinkb = sb("sinkb", [128, L, 8], BF16)
    mask_s = sb("mask_s", [128, 132], F32)
    gall = sb("gall", [128, 8, 8], F32)
    caw = sb("caw", [128, 2, 6], F32)
    cfa = sb("cfa", [128, 44, 8], F32)
    sink = sb("sink", [128, L, 8], F32)
    nsink = sb("nsink", [128, L, 8], F32)
    sinkc = sb("sinkc", [128, L], F32)
    nsinkc = sb("nsinkc", [128, L], F32)
    gvb = sb("gvb", [128, L, 256], F32)
    WsT = sb("WsT", [128, L, 4, 128], BF16)
    WsT_f = tmp[:, 0, :]
    WbdF = tmp[0:64, 1, :].rearrange("p (a b) -> p a b", a=L * 4)
    Wbd = sb("Wbd", [64, L * 4, 64], BF16)
    brow = sb("brow", [128, L * 4, 128], BF16)
    brow_s = sb("brow_s", [64, L * 4, 64], BF16)
    bst = ct[:, 0:2, :].rearrange("p a b -> p (a b)")
    bst2 = ct[:, 2:4, :].rearrange("p a b -> p (a b)")
    bstb = mbf[:, 0:2, :].rearrange("p a b -> p (a b)")
    ipi = sb("ipi", [128, 2], I32)
    ipf = sb("ipf", [128, 4], F32)
    iotaj = sb("iotaj", [128, 132], F32)
    cvs = sb("cvs", [128, 4, 1024], F32)

    ps = [psum("ps%d" % i, [128, 512], F32) for i in range(7)]
    psT = psum("psT", [128, 1024], BF16)
    PT_ALL = [("ps", 7)]

    act_bf = act_raw[:, :].bitcast(BF16)
    act_p = act_bf.rearrange("p (c n) -> p c n", c=22)
    act_s = act_raw[:, 0:704].bitcast(BF16).rearrange("p (c n) -> p c n", c=22)
    o0 = 704
    sstage = act_raw[:, o0:o0 + 2048].rearrange("p (a s f) -> p a s f", a=2, s=8)
    KT_all = act_raw[:, o0 + 2048:o0 + 3104].bitcast(BF16).rearrange("p (s k) -> p s k", s=16)
    Vc_bf = act_raw[:, o0 + 3104:o0 + 4128].bitcast(BF16).rearrange("p (s k) -> p s k", s=16)
    qbd = act_raw[:, o0 + 4128:o0 + 4384].bitcast(BF16).rearrange("p (s k) -> p s k", s=16)
    PT_all = act_raw[:, o0 + 4384:o0 + 4640].bitcast(BF16)
    PTn = act_raw[:, o0 + 4640:o0 + 4896].bitcast(BF16)
    vnt_f = ct[0:4, :, :].rearrange("p a (b f) -> p (a b) f", f=128)
    vnt = mbf[0:4, 0:4, :].rearrange("p a (b f) -> p (a b) f", f=128)

    def mm(out, lhsT, rhs, start, stop, reads, writes, **kw):
        S.add("pe", lambda e: e.matmul(out, lhsT=lhsT, rhs=rhs, start=start, stop=stop, **kw), reads, writes)

    def tr(out, in_, ident, reads, writes):
        S.add("pe", lambda e: e.transpose(out, in_, ident), reads, writes)

    def ACT(out, in_, func, reads, writes, **kw):
        S.add("act", lambda e: e.activation(out=out, in_=in_, func=func, **kw), reads, writes)

    def CP(eng, out, in_, reads, writes):
        if eng == "act":
            S.add("act", lambda e: e.copy(out=out, in_=in_), reads, writes)
        else:
            S.add(eng, lambda e: e.tensor_copy(out=out, in_=in_), reads, writes)

    def TSC(eng, out, in0, s1, s2, op0, op1, reads, writes, **kw):
        if op1 is None:
            S.add(eng, lambda e: e.tensor_scalar(out=out, in0=in0, scalar1=s1, scalar2=None, op0=op0, **kw), reads, writes)
        else:
            S.add(eng, lambda e: e.tensor_scalar(out=out, in0=in0, scalar1=s1, scalar2=s2, op0=op0, op1=op1, **kw), reads, writes)

    def TT(eng, out, in0, in1, op, reads, writes):
        S.add(eng, lambda e: e.tensor_tensor(out=out, in0=in0, in1=in1, op=op), reads, writes)

    def STT(out, in0, scalar, in1, op0, op1, reads, writes):
        S.add("dve", lambda e: e.scalar_tensor_tensor(out=out, in0=in0, scalar=scalar, in1=in1, op0=op0, op1=op1), reads, writes)

    def WARM(func):
        ACT(dmy[:, 1:2], dmy[:, 0:1], func, ["dmy0"], ["dmy1"])

    def MEMSET(eng, ap, val, writes):
        S.add(eng, lambda e: e.memset(ap, val), (), writes)

    rr = {"mm": 0, "at": 0, "cp": 0, "mmn": 4}

    def mm_next():
        b = rr["mm"] % rr["mmn"]
        rr["mm"] = (b + 1) % rr["mmn"]
        return b

    def at_next():
        b = 5 + rr["at"]
        rr["at"] = (rr["at"] + 1) % 2
        return b

    def cp_eng():
        rr["cp"] = (rr["cp"] + 1) % 2
        return ("act", "dve")[rr["cp"]]

    ACT_ALL = [("act", c) for c in range(22)]
    MO_ALL = [("mo", m) for m in range(8)]

    MEMSET("pool", ident_f[:], 1.0, ["ident_f"])
    S.add("pool", lambda e: e.affine_select(out=ident_f[:], in_=ident_f[:], pattern=[[1, 128]], base=0, channel_multiplier=-1,
                                             compare_op=ALU.is_equal, fill=0.0), ["ident_f"], ["ident_f"])
    CP("dve", ident_b[:], ident_f[:], ["ident_f"], ["ident_b"])
    MEMSET("dve", ones_b[:], 1.0, ["ones"])
    MEMSET("dve", dmy[:], 1.0, ["dmy0", "dmy1"])
    MEMSET("pool", maskb[:, 0, :], 0.0, ["maskb"])
    S.add("pool", lambda e: e.affine_select(out=maskb[:, 0, :], in_=maskb[:, 0, :], pattern=[[1, 256]], base=-1, channel_multiplier=-1,
                                             compare_op=ALU.is_ge, fill=NEG), ["maskb"], ["maskb"])
    S.add("pool", lambda e: e.affine_select(out=maskb[:, 0, :], in_=maskb[:, 0, :], pattern=[[-1, 256]], base=128, channel_multiplier=1,
                                             compare_op=ALU.is_ge, fill=NEG), ["maskb"], ["maskb"])
    CP("pool", maskb[:, 1, :], maskb[:, 0, :], ["maskb"], ["maskb"])
    MEMSET("pool", maskb[:, 1, 0:128], NEG, ["maskb"])
    S.add("pool", lambda e: e.iota(ipi[:, 0:1], pattern=[[0, 1]], base=0, channel_multiplier=1), (), ["ipi"])
    S.add("dve", lambda e: e.tensor_single_scalar(out=ipi[:, 1:2], in_=ipi[:, 0:1], scalar=3, op=ALU.bitwise_and), ["ipi"], ["ipi1"])
    CP("dve", ipf[:, 0:1], ipi[:, 1:2], ["ipi1"], ["ipf0"])
    S.add("dve", lambda e: e.tensor_scalar(out=ipi[:, 1:2], in0=ipi[:, 0:1], scalar1=2, scalar2=7, op0=ALU.logical_shift_right,
                                            op1=ALU.bitwise_and), ["ipi", "ipf0"], ["ipi1"])
    CP("dve", ipf[:, 1:2], ipi[:, 1:2], ["ipi1"], ["ipf1"])
    S.add("pool", lambda e: e.iota(iotaj[:], pattern=[[1, 132]], base=0, channel_multiplier=0, allow_small_or_imprecise_dtypes=True), (), ["iotaj"])
    TSC("dve", iotaj[:], iotaj[:], ipf[:, 0:1], None, ALU.subtract, None, ["iotaj", "ipf0"], ["iotaj"])
    TSC("dve", mask_s[:], iotaj[:], 1.0, None, ALU.is_ge, None, ["iotaj"], ["mask_s"])
    TSC("dve", iotaj[:], iotaj[:], 128.0, None, ALU.is_le, None, ["mask_s"], ["iotaj"])
    TT("dve", mask_s[:], mask_s[:], iotaj[:], ALU.mult, ["iotaj", "mask_s"], ["mask_s"])
    TSC("dve", mask_s[:], mask_s[:], -1.0, -NEG, ALU.add, ALU.mult, ["mask_s"], ["mask_s"])
    for l in range(L):
        S.dma(sink[:, l, :], sinks_d[l].rearrange("k g -> (k g)").partition_broadcast(128), writes=[("sink", l)])
        TSC("dve", nsink[:, l, :], sink[:, l, :], -1.0, None, ALU.mult, None, [("sink", l)], [("nsink", l)])
        CP("dve", sinkb[:, l, :], sink[:, l, :], [("sink", l)], [("sinkb", l)])
        MEMSET("dve", sinkc[:, l:l + 1], 0.0, [("sinkc", l)])
        for X in range(4):
            for hk in range(2):
                TSC("dve", junk[:, 0:1], ipf[:, 1:2], float(X * 2 + hk), None, ALU.is_equal, None, ["ipf1", ("sinkc", l)], ["junk"])
                STT(sinkc[:, l:l + 1], junk[:, 0:1], sink[:, l, hk * 4 + X:hk * 4 + X + 1], sinkc[:, l:l + 1], ALU.mult, ALU.add,
                    ["junk", ("sink", l), ("sinkc", l)], [("sinkc", l)])
        TSC("dve", nsinkc[:, l:l + 1], sinkc[:, l:l + 1], -1.0, None, ALU.mult, None, [("sinkc", l)], [("nsinkc", l)])
        S.dma(gvb[:, l, :], gv_d[l].partition_broadcast(128), writes=[("gvb", l)])

    stage_rows = act_raw

    r2c_pending = []

    def rows_to_cols(row_srcs, R, W, dst, dst_tok, pbase=0, cbase=0):
        tok = [("stg", pbase, cbase)]
        for (r0, n, ap) in row_srcs:
            S.dma(stage_rows[pbase + r0:pbase + r0 + n, cbase:cbase + W], ap, writes=tok)

        def compute():
            nch = W // 128
            per = 512 // R
            for c0 in range(0, nch, per):
                bk = at_next()
                n = min(per, nch - c0)
                for c in range(c0, c0 + n):
                    tr(ps[bk][:, (c - c0) * R:(c - c0 + 1) * R], stage_rows[pbase:pbase + R, cbase + c * 128:cbase + (c + 1) * 128],
                       ident_f[pbase:pbase + R, pbase:pbase + R], tok + ["ident_f"], [("ps", bk)])
                CP(cp_eng(), dst[:, c0:c0 + n, :], ps[bk][:, 0:n * R].rearrange("p (c r) -> p c r", r=R), [("ps", bk)], [dst_tok])
        r2c_pending.append(compute)

    def r2c_flush():
        for f_ in r2c_pending:
            f_()
        del r2c_pending[:]

    def cols_to_rows(src_fn, R, W, dst_rows, src_toks):
        nch = W // 128
        for c0 in range(0, nch, 4):
            bk = at_next()
            n = min(4, nch - c0)
            for c in range(c0, c0 + n):
                tr(ps[bk][0:R, (c - c0) * 128:(c - c0 + 1) * 128], src_fn(c), ident_f[:, :], list(src_toks) + ["ident_f"], [("ps", bk)])
            CP(cp_eng(), stage_rows[0:R, c0 * 128:(c0 + n) * 128], ps[bk][0:R, 0:n * 128], [("ps", bk)], ACT_ALL)
        for (r0, n, ap) in dst_rows:
            S.dma(ap, stage_rows[r0:r0 + n, 0:W], reads=ACT_ALL)

    rows_to_cols([(l * 3, 3, cfw_d[l]) for l in range(L)] + [(6, 2, cfb_d)], 8, 2 * DFF, cfa[:], "cfa", pbase=0, cbase=0)
    rows_to_cols([(k * L, L, g_d[k]) for k in range(4)], 8, D, gall[:], "gall", pbase=32, cbase=0)
    rows_to_cols([(l * 3, 3, caw_d[l]) for l in range(L)], 6, 256, caw[:], "caw", pbase=32, cbase=1024)
    for l in range(L):
        rows_to_cols([(0, 32, sca_d[l].rearrange("s j f -> (s j) f"))], 32, 256, zst_s[:, l], ("zst_s", l), pbase=64, cbase=256 * l)
    r2c_flush()

    for l in range(L):
        S.dma(WsT_f.rearrange("p (g s) -> p g s", g=4), spw_d[l].rearrange("g t s -> t g s"), writes=["WsT_f"])
        bk = at_next()
        for g in range(4):
            tr(ps[bk][:, g * 128:(g + 1) * 128], WsT_f[:, g * 128:(g + 1) * 128], ident_f[:], ["WsT_f", "ident_f"], [("ps", bk)])
        CP("dve", WsT_f, ps[bk][:, :], [("ps", bk)], ["WsT_f"])
        S.add("pool", lambda e: e.affine_select(out=WsT_f.rearrange("p (g t) -> p g t", g=4), in_=WsT_f.rearrange("p (g t) -> p g t", g=4),
                                                 pattern=[[0, 4], [1, 128]], base=0, channel_multiplier=-1, compare_op=ALU.is_ge, fill=0.0),
              ["WsT_f"], ["WsT_f"])
        CP("dve", WsT[:, l, :, :], WsT_f.rearrange("p (g t) -> p g t", g=4), ["WsT_f"], [("WsT", l)])
    MEMSET("pool", WbdF, 0.0, ["WbdF"])
    for i in range(NSEQ):
        S.dma(WbdF[4 * i:4 * i + 4, :, 4 * i:4 * i + 4], spw_d[:, :, 0:4, 0:4].rearrange("l g t s -> t (l g) s"), writes=["WbdF"])
    bk = at_next()
    for a_ in range(L * 4):
        tr(ps[bk][0:64, a_ * 64:(a_ + 1) * 64], WbdF[:, a_, :], ident_f[0:64, 0:64], ["WbdF", "ident_f"], [("ps", bk)])
    CP("dve", WbdF, ps[bk][0:64, :].rearrange("p (a b) -> p a b", a=L * 4), [("ps", bk)], ["WbdF"])
    S.add("pool", lambda e: e.affine_select(out=WbdF, in_=WbdF, pattern=[[0, L * 4], [1, 64]], base=0, channel_multiplier=-1,
                                             compare_op=ALU.is_ge, fill=0.0), ["WbdF"], ["WbdF"])
    CP("dve", Wbd[:], WbdF, ["WbdF"], ["Wbd"])
    MEMSET("pool", brow[:], 0.0, ["brow"])
    MEMSET("pool", brow_s[:], 0.0, ["brow_s"])
    flat_b = spb_d.rearrange("l g t -> (l g t)")
    MEMSET("pool", bst[0:64, :], 0.0, ["bst"])
    for p0 in (0, 32):
        S.dma(bst[p0:p0 + 1, :], flat_b.partition_broadcast(1), writes=["bst"])
    CP("dve", bstb[0:64, :], bst[0:64, :], ["bst"], ["bstb"])
    TT("dve", bst2[0:64, :], bst[0:64, :], bstb[0:64, :], ALU.subtract, ["bst", "bstb"], ["bst2"])
    CP("dve", brow[0:1].rearrange("p a t -> p (a t)"), bstb[0:1, :], ["bstb", "brow"], ["brow"])
    CP("dve", brow[32:33].rearrange("p a t -> p (a t)"), bst2[32:33, :], ["bst2", "brow"], ["brow"])
    for sq_ in range(NSEQ):
        CP("dve", brow_s[0:1, :, 4 * sq_:4 * sq_ + 4], bstb[0:1, :].rearrange("p (a t) -> p a t", t=128)[:, :, 0:4], ["bstb", "brow_s"], ["brow_s"])
        CP("dve", brow_s[32:33, :, 4 * sq_:4 * sq_ + 4], bst2[32:33, :].rearrange("p (a t) -> p a t", t=128)[:, :, 0:4], ["bst2", "brow_s"], ["brow_s"])
    MEMSET("pool", kT[:], 0.0, [("kT", 0), ("kT", 1)])
    MEMSET("pool", vtok[:], 0.0, [("vtok", l, s) for l in range(L) for s in range(5)])
    MEMSET("pool", prev_p[:], 0.0, [("prev_p", l, c) for l in range(L) for c in range(44)])
    MEMSET("pool", zc_p[:], 0.0, [("zc_p", 0), ("zc_p", 1)])

    S.barrier()
    def w_in_src(l, c0, n):
        return w_in_d[l].rearrange("(kc p) n -> p kc n", p=128)[:, :, c0:c0 + n]

    def groups(l):
        gs = []
        gs.append(dict(K=8, cols=512, pieces=[(0, 8, 0, w_in_src(l, 0, 256)), (0, 8, 256, w_in_src(l, 512, 256))]))
        gs.append(dict(K=8, cols=512, pieces=[(0, 8, 0, w_in_src(l, 256, 256)), (0, 8, 256, w_in_src(l, 1536, 256))]))
        gs.append(dict(K=8, cols=512, pieces=[(0, 8, 0, w_in_src(l, 768, 512))], qperm=True))
        gs.append(dict(K=8, cols=512, pieces=[(0, 8, 0, w_in_src(l, 1280, 256)), (0, 8, 256, w_in_src(l, 1792, 256))]))
        for br, wd, kb in ((0, wba_d, 2), (2, wbc_d, 2), (1, wbb_d, 4)):
            gs.append(dict(K=8, cols=512, pieces=[(0, 8, 0, w_in_src(l, 2048 + br * 1024, 512))]))
            if br == 1:
                pb = []
                for X in range(4):
                    for hk in range(2):
                        hd = hk * 4 + X
                        pb.append((X, 1, 0, wd[l][hd * 64:(hd + 1) * 64, :].rearrange("p (k n) -> p k n", k=1), (hk * 64, 64)))
                gs.append(dict(K=kb, cols=1024, pieces=pb))
            else:
                gs.append(dict(K=kb, cols=1024, pieces=[(0, kb, 0, wd[l].rearrange("(kc p) n -> p kc n", p=128))]))
            gs.append(dict(K=8, cols=512, pieces=[(0, 8, 0, w_in_src(l, 2048 + br * 1024 + 512, 512))]))
        for hf in range(2):
            gs.append(dict(K=8, cols=512, pieces=[(0, 8, 0, wo_d[l].rearrange("(kc p) n -> p kc n", p=128)[:, :, hf * 512:(hf + 1) * 512])]))
        wu = wup_d[l].rearrange("(kc p) n -> p kc n", p=128)
        for gi in range(11):
            pcs = [(0, 8, 0, wu[:, :, gi * 256:(gi + 1) * 256]), (0, 8, 256, wu[:, :, (22 + 2 * gi) * 128:(24 + 2 * gi) * 128])]
            gs.append(dict(K=8, cols=512, pieces=pcs))
        wdn = wdn_d[l].rearrange("(kc p) n -> p kc n", p=128)
        for hf in range(2):
            for (k0, nk) in ((0, 8), (8, 8), (16, 6)):
                gs.append(dict(K=nk, cols=512, pieces=[(0, nk, 0, wdn[:, k0:k0 + nk, hf * 512:(hf + 1) * 512])]))
        assert len(gs) == NG
        return gs

    GROUPS = [groups(l) for l in range(L)]

    segs = [Seg(i, False, b0, nb) for i, (b0, nb) in enumerate(TILES)] + [Seg(len(TILES), True, 0, 0)]
    items = [(l, gi) for _ in segs for l in range(L) for gi in range(NG)]
    wstate = {"next_pref": 0, "next_get": 0}

    cvstate = {"n": 0}

    def convert_into_slot(l, gi, slot):
        g = GROUPS[l][gi]
        K, cols = g["K"], g["cols"]
        base, rem = divmod(K, 4)
        bounds = []
        k_ = 0
        for q_ in range(4):
            n_ = base + (1 if q_ < rem else 0)
            bounds.append((k_, k_ + n_))
            k_ += n_
        for q_, (ka, kb) in enumerate(bounds):
            wtok = [("ring", slot, t_) for t_ in range(q_, 4)]
            if kb > ka:
                b = cvstate["n"] % 4
                cvstate["n"] += 1
                n_el = (kb - ka) * cols
                stv = cvs[:, b, 0:n_el].rearrange("p (k n) -> p k n", k=kb - ka)
                npc = 0
                for pc in g["pieces"]:
                    k0, nk, c0, src = pc[:4]
                    p0, np_ = pc[4] if len(pc) > 4 else (0, 128)
                    lo, hi = max(k0, ka), min(k0 + nk, kb)
                    if lo >= hi:
                        continue
                    ncol = src.shape[-1]
                    S.dma(stv[p0:p0 + np_, lo - ka:hi - ka, c0:c0 + ncol], src[:, lo - k0:hi - k0, :], writes=[("cst", b, npc)])
                    npc += 1
                rtok = [("cst", b, pi) for pi in range(8)]
                if g.get("qperm"):
                    for kk in range(kb - ka):
                        src_v = cvs[:, b, kk * cols:(kk + 1) * cols].rearrange("p (hk g d) -> p g hk d", hk=2, g=4)
                        dst_v = ring[:, slot, (ka + kk) * cols:(ka + kk + 1) * cols].rearrange("p (g hk d) -> p g hk d", g=4, hk=2)
                        CP("act", dst_v, src_v, rtok, wtok)
                else:
                    CP("dve", ring[:, slot, ka * cols:kb * cols], cvs[:, b, 0:n_el], rtok, wtok)
            if q_ in (1, 3):
                sa, sb_ = bounds[q_ - 1][0], bounds[q_][1]
                if sb_ > sa:
                    S.dma(wbf_d[l * NG + gi][:, sa * cols:sb_ * cols], ring[:, slot, sa * cols:sb_ * cols],
                          reads=[("ring", slot, q_ - 1), ("ring", slot, q_)], writes=[("wbf", l, gi, q_ // 2)], q="pool")

    def prefetch_upto(k):
        while wstate["next_pref"] <= k and wstate["next_pref"] < len(items):
            i = wstate["next_pref"]
            l, gi = items[i]
            g = GROUPS[l][gi]
            used = g["K"] * g["cols"]
            slot = i % NSLOT
            if i < L * NG:
                convert_into_slot(l, gi, slot)
            else:
                S.dma(ring[:, slot, 0:used], wbf_d[l * NG + gi][:, 0:used], reads=[("wbf", l, gi, 0), ("wbf", l, gi, 1)],
                      writes=[("ring", slot, t_) for t_ in range(4)])
            wstate["next_pref"] += 1

    def wget(l_expect, gi_expect, hold=0):
        i = wstate["next_get"]
        l, gi = items[i]
        assert (l, gi) == (l_expect, gi_expect), (l, gi, l_expect, gi_expect)
        prefetch_upto(i + NSLOT - 1 - hold)
        wstate["next_get"] += 1
        g = GROUPS[l][gi]
        slot = i % NSLOT
        view = ring[:, slot, 0:g["K"] * g["cols"]].rearrange("p (k n) -> p k n", k=g["K"])
        return view, [("ring", slot, t_) for t_ in range(4)]

    def xtok(seg, c):
        return [("x", c, j) for j in range(seg.nbt)]

    STG = [(xin[:, 0, :], [("xin", 0)]),
           (tmp[:, :, :].rearrange("p a b -> p (a b)"), [("tmp", 0), ("tmp", 1)]),
           (ct[:, 0:2, :].rearrange("p a b -> p (a b)"), [("ct", 0), ("ct", 1)]),
           (ct[:, 2:4, :].rearrange("p a b -> p (a b)"), [("ct", 2), ("ct", 3)])]
    LOAD_STG = [0, 2, 3, 0]
    STORE_STG = [1, 2, 3, 1]

    def x_src(seg, j):
        return xs_d if seg.sample else xp_d[(seg.b0 + j) * 128:(seg.b0 + j + 1) * 128, :]

    def preload_x0(seg):
        R = 64 if seg.sample else 128
        buf, toks = STG[LOAD_STG[0]]
        S.dma(buf[0:R, :], x_src(seg, 0), writes=toks)

    def load_x(seg, skip_first_dma=False):
        for j in range(seg.nbt):
            R = 64 if seg.sample else 128
            buf, toks = STG[LOAD_STG[j]]
            if not (j == 0 and skip_first_dma):
                S.dma(buf[0:R, :], x_src(seg, j), writes=toks)
            for hf in range(2):
                bk = at_next()
                for cc in range(4):
                    c = hf * 4 + cc
                    tr(ps[bk][:, cc * R:(cc + 1) * R], buf[0:R, c * 128:(c + 1) * 128], ident_f[0:R, 0:R], toks + ["ident_f"], [("ps", bk)])
                CP(cp_eng(), xT[:, hf * 4:hf * 4 + 4, j * 128:j * 128 + R], ps[bk][:, 0:4 * R].rearrange("p (c t) -> p c t", c=4),
                   [("ps", bk)], [("x", hf * 4 + cc, j) for cc in range(4)])

    def store_x(seg):
        for j in range(seg.nbt):
            R = 64 if seg.sample else 128
            buf, toks = STG[STORE_STG[j]]
            for hf in range(2):
                bk = at_next()
                for cc in range(4):
                    c = hf * 4 + cc
                    tr(ps[bk][0:R, cc * 128:(cc + 1) * 128], xT[:, c, j * 128:j * 128 + R], ident_f[:, :], [("x", c, j), "ident_f"], [("ps", bk)])
                CP(cp_eng(), buf[0:R, hf * 512:(hf + 1) * 512], ps[bk][0:R, 0:512], [("ps", bk)], toks)
            dst = ys_d if seg.sample else yp_d[(seg.b0 + j) * 128:(seg.b0 + j + 1) * 128, :]
            S.dma(dst, buf[0:R, :], reads=toks)

    def norm_stats(seg, eps=EPS):
        N = seg.N
        for c in range(8):
            mm(ps[4][:, :N], ones_b[:, :], h[:, c, :N], c == 0, c == 7, [("h", c), "ones"], [("ps", 4)])
        ACT(rstd[:, :N], ps[4][:, :N], AF.Sqrt, [("ps", 4)], ["rstd"], scale=1.0 / D, bias=eps)
        S.add("dve", lambda e: e.reciprocal(out=rstd[:, :N], in_=rstd[:, :N]), ["rstd"], ["rstd"])

    def rmsnorm_to_h(seg, l, kind):
        N = seg.N
        for c in range(8):
            ACT(h[:, c, :N], xT[:, c, :N], AF.Square, xtok(seg, c), [("h", c)])
        norm_stats(seg)
        for c in range(8):
            STT(h[:, c, :N], xT[:, c, :N], gall[:, c, kind * L + l:kind * L + l + 1], rstd[:, :N], ALU.mult, ALU.mult,
                xtok(seg, c) + ["rstd", "gall"], [("h", c)])

    def post_norm_add(seg, l, kind):
        N = seg.N
        norm_stats(seg, 4.0 * EPS if kind == 1 else EPS)
        for m in range(8):
            STT(tmp[:, m % 2, :N], mo[:, m, :N], gall[:, m, kind * L + l:kind * L + l + 1], rstd[:, :N], ALU.mult, ALU.mult,
                [("mo", m), "rstd", "gall"], [("tmp", m % 2)])
            TT(ADD_ENG, xT[:, m, :N], xT[:, m, :N], tmp[:, m % 2, :N], ALU.add, xtok(seg, m) + [("tmp", m % 2)], xtok(seg, m))

    def v3(ap2d, seg, off=0):
        return ap2d[:, 0:seg.nseq * (seg.T + off)].rearrange("p (s t) -> p s t", s=seg.nseq)

    def out_proj_evac(seg, bk, m):
        N = seg.N
        ACT(h[:, m, :N], ps[bk][:, :N], AF.Square, [("ps", bk)], [("h", m)])
        CP("act", mo[:, m, :N], ps[bk][:, :N], [("ps", bk)], [("mo", m)])

    def mixer(seg, l):
        N, T, nseq = seg.N, seg.T, seg.nseq
        rmsnorm_to_h(seg, l, 0)
        _chk("norm")
        hr = lambda kc: [("h", kc)]

        def proj_chunk(wv, wt, K, col0, rhs, rreads, evac):
            bk = mm_next()
            for kc in range(K):
                mm(ps[bk][:, :N], wv[:, kc, col0:col0 + 128], rhs(kc), kc == 0, kc == K - 1, wt + rreads(kc), [("ps", bk)])
            evac(bk)

        hrhs = lambda kc: h[:, kc, :N]
        wv, wt = wget(l, 0)
        for c in range(2):
            proj_chunk(wv, wt, 8, c * 128, hrhs, hr, lambda bk, c=c: CP("act", a_in[:, c, :N], ps[bk][:, :N], [("ps", bk)], [("tmp", c)]))
        for c in range(2):
            def ev(bk, c=c):
                zv = v3(zfull[:, c, :], seg, 2)
                TT("dve", zv[:, :, 2:2 + T], v3(ps[bk], seg), v3(a_in[:, c, :], seg), ALU.mult, [("ps", bk), ("tmp", c)], [("z", c)])
            proj_chunk(wv, wt, 8, 256 + c * 128, hrhs, hr, ev)
        wv, wt = wget(l, 1)
        for c in range(2):
            proj_chunk(wv, wt, 8, c * 128, hrhs, hr, lambda bk, c=c: CP("act", a_b[:, c, :N], ps[bk][:, :N], [("ps", bk)], [("a_b", c)]))
        for c in range(2):
            proj_chunk(wv, wt, 8, 256 + c * 128, hrhs, hr, lambda bk, c=c: CP("act", c_u[:, c, :N], ps[bk][:, :N], [("ps", bk)], [("c_u", c)]))
        if seg.sample:
            attn_sample_prep(seg, l)
        wv, wt = wget(l, 2)
        for X in range(4):
            proj_chunk(wv, wt, 8, X * 128, hrhs, hr, lambda bk, X=X: ACT(qT[:, X, :N], ps[bk][:, :N], AF.Copy, [("ps", bk)], [("q", X)], scale=0.125))
        wv, wt = wget(l, 3)
        if seg.sample:
            proj_chunk(wv, wt, 8, 0, hrhs, hr, lambda bk: CP("act", KT_all[:, :, 128:132], v3(ps[bk], seg), [("ps", bk)], ["KT_new"]))
        else:
            proj_chunk(wv, wt, 8, 0, hrhs, hr, lambda bk: CP("act", kT[:, l, 128:128 + N], ps[bk][:, :N], [("ps", bk)], [("kT", l)]))
        for j in range(seg.nbt):
            R = 64 if seg.sample else 128
            bk = at_next()
            c_lo = 0 if (seg.sample or (seg.last and j == seg.nb - 1)) else 128
            for kc in range(8):
                mm(ps[bk][0:R, c_lo:512], h[:, kc, j * 128:j * 128 + R], wv[:, kc, c_lo:512], kc == 0, kc == 7, wt + [("h", kc)], [("ps", bk)])
            if not seg.sample:
                CP("act", vtok[:, l, 1 + j, :], ps[bk][:, 128:256], [("ps", bk)], [("vtok", l, 1 + j)])
            ACT(junk[0:R, :], ps[bk][0:R, 256:512], AF.Square, [("ps", bk)], ["junk", ("st4", 0)], accum_out=st4[0:R, 0, 0:1])
            ACT(st4[0:R, 0, 0:1], st4[0:R, 0, 0:1], AF.Sqrt, [("st4", 0)], [("st4", 0)], scale=1.0 / 256, bias=EPS)
            S.add("dve", lambda e, R=R: e.reciprocal(out=st4[0:R, 0, 0:1], in_=st4[0:R, 0, 0:1]), [("st4", 0)], [("st4", 0)])
            if seg.sample:
                STT(vno[0:R, :], ps[bk][0:R, 256:512], st4[0:R, 0, 0:1], gvb[0:R, l, :], ALU.mult, ALU.mult,
                    [("ps", bk), ("st4", 0), ("gvb", l)], ["vno"])
                CP("dve", vn[0:R, 0, :], vno[0:R, :], ["vno"], [("vn", 0)])
                S.dma(cvs_d[l].rearrange("s t f -> (s t) f"), vno[0:R, :], reads=["vno"])
                CP("act", kvo[0:R, :], ps[bk][0:R, 0:256], [("ps", bk)], ["kvo"])
                S.dma(kvscr_d[l], kvo[0:R, :], reads=["kvo"], writes=[("kvscr", l)])
                S.dma(wks_d[l, :, 124:128, :], kvscr_d[l][:, 0:128].rearrange("(s t) f -> s t f", t=4), reads=[("kvscr", l)])
                S.dma(wvs_d[l, :, 124:128, :], kvscr_d[l][:, 128:256].rearrange("(s t) f -> s t f", t=4), reads=[("kvscr", l)])
                S.dma(vnt_f, kvscr_d[l][:, 128:256].rearrange("(s t) f -> t s f", t=4), reads=[("kvscr", l)], writes=["vnt_f"] + [("ct", c) for c in range(4)])
                CP("dve", vnt, vnt_f, ["vnt_f"] + [("ct", c) for c in range(4)] + [("mbf", m) for m in range(4)], ["vnt"] + [("mbf", m) for m in range(4)])
                S.dma(wks_d[l, :, 0:124, :], ck_d[l, :, 4:128, :])
                S.dma(wvs_d[l, :, 0:124, :], cv_d[l, :, 4:128, :])
            else:
                STT(vn[:, j, :], ps[bk][:, 256:512], st4[:, 0, 0:1], gvb[:, l, :], ALU.mult, ALU.mult,
                    [("ps", bk), ("st4", 0), ("gvb", l)], [("vn", j)])
                if seg.last and j == seg.nb - 1:
                    CP("act", kvo[:, :], ps[bk][:, 0:256], [("ps", bk)], ["kvo"])
                    S.dma(wkp_d[l], kvo[:, 0:128], reads=["kvo"])
                    S.dma(wvp_d[l], kvo[:, 128:256], reads=["kvo"])

        _chk("proj")
        for c in range(2):
            zv = v3(zfull[:, c, :], seg, 2)
            if seg.sample:
                CP("pool", zv[:, :, 0:2], zst_s[:, l, c, :].rearrange("p (s j) -> p s j", j=2), [("zst_s", l)], [("z", c)])
            else:
                CP("pool", zv[:, :, 0:2], zc_p[:, l, c:c + 1, :], [("zc_p", l)], [("z", c)])
            c1 = v3(ct[:, c, :], seg)
            w = lambda i, c=c: caw[:, c, l * 3 + i:l * 3 + i + 1]
            TSC("dve", c1, zv[:, :, 0:T], w(0), None, ALU.mult, None, [("z", c), "caw"], [("ct", c)])
            STT(c1, zv[:, :, 1:T + 1], w(1), c1, ALU.mult, ALU.add, [("z", c), "caw", ("ct", c)], [("ct", c)])
            STT(c1, zv[:, :, 2:T + 2], w(2), c1, ALU.mult, ALU.add, [("z", c), "caw", ("ct", c)], [("ct", c)])
            TT("dve", ya[:, c, :N], ct[:, c, :N], a_b[:, c, :N], ALU.mult, [("ct", c), ("a_b", c)], [("ya", c)])
            if seg.sample:
                CP("pool", zout_s[:, c, :].rearrange("p (s j) -> p s j", j=2), zv[:, :, T:T + 2], [("z", c)], ["zout_s"])
            else:
                CP("pool", zc_p[:, l, c:c + 1, :], zv[:, :, T:T + 2], [("z", c)], [("zc_p", l)])
        if seg.sample:
            cols_to_rows(lambda c: zout_s[:, c, :], 32, 256, [(0, 32, cas_d[l].rearrange("s j f -> (s j) f"))], ["zout_s"])
        elif seg.last:
            cols_to_rows(lambda c: zc_p[:, l, c, :], 2, 256, [(0, 2, cap_d[l])], [("zc_p", l)])

        _chk("mixA")
        def mixer_c(seg, l):
            for j in range(seg.nbt):
                bk = at_next()
                if seg.sample:
                    for g in range(4):
                        o = ps[bk][(g % 2) * 64:(g % 2) * 64 + 64, (g // 2) * 64:(g // 2) * 64 + 64]
                        mm(o, vn[0:64, 0, g * 64:(g + 1) * 64], Wbd[0:64, l * 4 + g, :], True, False, [("vn", 0), "Wbd"], [("ps", bk)])
                        mm(o, ones_b[0:64, 0:64], brow_s[0:64, l * 4 + g, :], False, True, ["ones", "brow_s"], [("ps", bk)])
                    TT("dve", yc[:, :, 0:64], ps[bk][:, 0:128].rearrange("p (g t) -> p g t", g=2), c_u[:, :, 0:64], ALU.mult,
                       [("ps", bk), ("c_u", 0), ("c_u", 1)], [("yc", 0, 0), ("yc", 1, 0)])
                else:
                    for g in range(4):
                        o = ps[bk][(g % 2) * 64:(g % 2) * 64 + 64, (g // 2) * 128:(g // 2) * 128 + 128]
                        mm(o, vn[:, j, g * 64:(g + 1) * 64], WsT[:, l, g, :], True, False, [("vn", j), ("WsT", l)], [("ps", bk)])
                        mm(o, ones_b[:, 0:64], brow[:, l * 4 + g, :], False, True, ["ones", "brow"], [("ps", bk)])
                    TT("dve", yc[:, :, j * 128:(j + 1) * 128], ps[bk][:, 0:256].rearrange("p (g t) -> p g t", g=2), c_u[:, :, j * 128:(j + 1) * 128],
                       ALU.mult, [("ps", bk), ("c_u", 0), ("c_u", 1)], [("yc", 0, j), ("yc", 1, j)])


        def merge_branches(seg, l, bis, gi):
            order = ((0, ya, 2, lambda kc: [("ya", kc)]),
                     (2, yc, 2, lambda kc: [("yc", kc, j) for j in range(seg.nbt)]),
                     (1, yb, 4, lambda kc: [("yb", kc, j) for j in range(seg.nbt)] if not seg.sample else [("yb", kc, 0)]))
            for bi in bis:
                br, ybuf, kb, yreads = order[bi]
                wbr = wbrt = None
                for hf in range(2):
                    if hf == 0:
                        wg, wgt = wget(l, gi)
                    else:
                        wg, wgt = wget(l, gi + 2, hold=1)
                    for mi in range(4):
                        proj_chunk(wg, wgt, 8, mi * 128, hrhs, hr,
                                   lambda bk, mi=mi, hf=hf: ACT(gate[:, 0, mi, :N], ps[bk][:, :N], AF.Tanh, [("ps", bk)], [("gate", mi)], scale=0.5))
                        yield
                    if hf == 0:
                        wbr, wbrt = wget(l, gi + 1)
                    for mi in range(4):
                        m = hf * 4 + mi

                        def ev(bk, m=m, mi=mi, hf=hf, bi=bi):
                            if bi == 0:
                                STT(mo[:, m, :N], gate[:, 0, mi, :N], 1.0, ps[bk][:, :N], ALU.add, ALU.mult, [("ps", bk), ("gate", mi)], [("mo", m)])
                            else:
                                STT(tmp[:, m % 2, :N], gate[:, 0, mi, :N], 1.0, ps[bk][:, :N], ALU.add, ALU.mult, [("ps", bk), ("gate", mi)],
                                    [("tmp", m % 2)])
                                if bi == 1:
                                    TT(ADD_ENG, mo[:, m, :N], mo[:, m, :N], tmp[:, m % 2, :N], ALU.add, [("mo", m), ("tmp", m % 2)], [("mo", m)])
                                else:
                                    TT(ADD_ENG, mbf[:, m, :N], mo[:, m, :N], tmp[:, m % 2, :N], ALU.add, [("mo", m), ("tmp", m % 2)], [("mbf", m)])
                        proj_chunk(wbr, wbrt, kb, m * 128, lambda kc, ybuf=ybuf: ybuf[:, kc, :N], yreads, ev)
                        yield
                gi += 3

        WARM(AF.Exp)
        mixer_c(seg, l)
        filler = merge_branches(seg, l, (0, 1), 4)

        def fill(n=1):
            for _ in range(n):
                try:
                    next(filler)
                except StopIteration:
                    return

        if seg.sample:
            attn_sample(seg, l)
        else:
            PbV = [Pb, ct[:, 3, 0:264].bitcast(BF16).rearrange("p (a b) -> p a b", a=2)]
            PTsV = [PTs, ct[:, 2, 0:256].bitcast(BF16).rearrange("p (a b) -> p a b", a=2)]
            COLS = [(1, 3), (5, 6)]
            SBK = [4, 5, 6, 3]
            fine1 = [(nm_, 1, hh) for nm_ in ("Pb", "PTs") for hh in (0, 1)]
            S.add("dve", lambda e: e.memset(st4[:, 0, 2:3], 0.0), [], [("ct", 2), ("ct", 3)] + fine1)
            rr["mmn"] = 3
            rr["mm"] = rr["mm"] % 3
            sls = [slice(hh * 64, (hh + 1) * 64) for hh in (0, 1)]
            for j in range(seg.nb):
                mi_ = 1 if (seg.first and j == 0) else 0
                for Xp in range(2):
                    H4 = [(2 * Xp + g, hh) for g in range(2) for hh in range(2)]
                    for h4, (X, hh) in enumerate(H4):
                        hidx = hh * 4 + X
                        sb_ = ps[SBK[h4]]
                        tk = [("ps", SBK[h4])]
                        mm(sb_[:, 0:256], qT[sls[hh], X, j * 128:(j + 1) * 128], kT[sls[hh], l, j * 128:j * 128 + 256], True, False,
                           [("q", X), ("kT", l)], tk)
                        mm(sb_[:, 0:256], ident_b[:, :], maskb[:, mi_, :], False, False, ["ident_b", "maskb"], tk)
                        mm(sb_[:, 256:257], ident_b[:, :], sinkb[:, l, hidx:hidx + 1], False, True, ["ident_b", ("sinkb", l)], tk)
                    for h4, (X, hh) in enumerate(H4):
                        g = h4 // 2
                        cm, cs = COLS[g]
                        S.add("dve", lambda e, hh=hh, cm=cm, bkk=SBK[h4]: e.tensor_reduce(out=st4[:, hh, cm:cm + 1], in_=ps[bkk][:, 0:257], axis=AX.X,
                                                                                       op=ALU.max, negate=True), [("ps", SBK[h4])], [("st4m", g, hh)])
                    for h4, (X, hh) in enumerate(H4):
                        g = h4 // 2
                        cm, cs = COLS[g]
                        ACT(PbV[g][:, hh, 0:257], ps[SBK[h4]][:, 0:257], AF.Exp, [("ps", SBK[h4]), ("st4m", g, hh)], [("Pb", g, hh), ("rs4", h4)],
                            bias=st4[:, hh, cm:cm + 1], accum_out=rs4[:, h4:h4 + 1])
                    S.add("dve", lambda e: e.reciprocal(out=rs4[:, 0:4], in_=rs4[:, 0:4]), [("rs4", k) for k in range(4)], [("rs4", k) for k in range(4)])
                    for h4, (X, hh) in enumerate(H4):
                        g = h4 // 2
                        TSC("dve", PbV[g][:, hh, 0:256], PbV[g][:, hh, 0:256], rs4[:, h4:h4 + 1], None, ALU.mult, None,
                            [("Pb", g, hh), ("rs4", h4)], [("Pb", g, hh)])
                    fill(2)
                    for h4, (X, hh) in enumerate(H4):
                        g = h4 // 2
                        for kb in range(2):
                            tr(psT[:, h4 * 256 + kb * 128:h4 * 256 + (kb + 1) * 128], PbV[g][:, hh, kb * 128:(kb + 1) * 128], ident_b[:, :],
                               [("Pb", g, hh), "ident_b"], [("ps", 7)])
                    for g in range(2):
                        CP("act", PTsV[g][:, :, :].rearrange("p a b -> p (a b)"), psT[:, g * 512:(g + 1) * 512], [("ps", 7)], [("PTs", g, 0), ("PTs", g, 1)])
                    fill(2)
                    for g in range(2):
                        X = 2 * Xp + g
                        obk = mm_next()
                        for hh in range(2):
                            for kb in range(2):
                                mm(ps[obk][sls[hh], 0:128], vtok[:, l, j + kb, sls[hh]], PTsV[g][:, hh, kb * 128:(kb + 1) * 128], kb == 0, kb == 1,
                                   [("vtok", l, j + kb), ("PTs", g, hh)], [("ps", obk)])
                        CP("act", yb[:, X, j * 128:(j + 1) * 128], ps[obk][:, 0:128], [("ps", obk)], [("yb", X, j)])
            rr["mmn"] = 4
            S.add("dve", lambda e: e.memset(st4[:, 1, 2:3], 0.0), fine1, [("ct", 2), ("ct", 3)])
            CP("pool", kT[:, l, 0:128], kT[:, l, N:N + 128], [("kT", l)], [("kT", l)])
            CP("pool", vtok[:, l, 0, :], vtok[:, l, seg.nb, :], [("vtok", l, seg.nb)], [("vtok", l, 0)])

        _chk("attn")
        fill(100)
        for _ in merge_branches(seg, l, (2,), 10):
            pass
        WARM(AF.Sqrt)
        _chk("merge")
        for hf in range(2):
            wv, wt = wget(l, 13 + hf)
            for mi in range(4):
                m = hf * 4 + mi
                proj_chunk(wv, wt, 8, mi * 128, lambda kc: mbf[:, kc, :N], lambda kc: [("mbf", kc)], lambda bk, m=m: out_proj_evac(seg, bk, m))
        _chk("wout0")
        post_norm_add(seg, l, 1)
        _chk("wout")

    def attn_sample_prep(seg, l):
        for hfi in range(2):
            for which, src_d in ((0, ck_d), (1, cv_d)):
                stg = sstage[:, which]
                S.dma(stg, src_d[l, hfi * 8:(hfi + 1) * 8].rearrange("s k f -> k s f"), writes=[("sstage", which)])
                if which == 0:
                    for q4 in range(2):
                        bk = at_next()
                        for i in range(4):
                            tr(ps[bk][:, i * 128:(i + 1) * 128], stg[:, q4 * 4 + i, :], ident_f[:, :], [("sstage", 0), "ident_f"], [("ps", bk)])
                        s0 = hfi * 8 + q4 * 4
                        CP(cp_eng(), KT_all[:, s0:s0 + 4, 0:128], ps[bk][:, :].rearrange("p (s k) -> p s k", s=4), [("ps", bk)], ["KT_c"])
                else:
                    CP("pool", Vc_bf[:, hfi * 8:(hfi + 1) * 8, :], stg, [("sstage", 1)], ["Vc_bf"])
        MEMSET("pool", qbd[:], 0.0, ["qbd"])
        MEMSET("pool", PTn[:, :], 0.0, ["PTn"])

    def attn_sample(seg, l):
        for X in range(4):
            for hk in range(2):
                sl = slice(hk * 64, (hk + 1) * 64)
                CP("dve", qbd[sl, :, X * 8 + hk * 4:X * 8 + hk * 4 + 4], qT[sl, X, 0:64].rearrange("p (s t) -> p s t", t=4), [("q", X), "qbd"], ["qbd"])
        for i in range(4):
            bk = 5 + (i % 2)
            u = i % 2
            for s4 in range(4):
                s = i * 4 + s4
                mm(ps[bk][32 * s4:32 * s4 + 32, 0:132], qbd[:, s, :], KT_all[:, s, :], True, True, ["qbd", "KT_c", "KT_new"], [("ps", bk)],
                   tile_position=(0, 32 * s4))
            TT("dve", sm[:, u, 0:132], ps[bk][:, 0:132], mask_s[:, :], ALU.add, [("ps", bk), "mask_s"], [("sm", u)])
            S.add("dve", lambda e, u=u: e.tensor_reduce(out=st4[:, u, 1:2], in_=sm[:, u, 0:132], axis=AX.X, op=ALU.max), [("sm", u)], [("st4m", u)])
            TSC("dve", st4[:, u, 2:3], st4[:, u, 1:2], -1.0, nsinkc[:, l:l + 1], ALU.mult, ALU.min, [("st4m", u), ("nsinkc", l)], [("st4n", u)])
            ACT(Pb[:, u, 0:132], sm[:, u, 0:132], AF.Exp, [("sm", u), ("st4n", u)], [("Pb", u), ("st4r", u)], scale=1.0, bias=st4[:, u, 2:3],
                accum_out=st4[:, u, 3:4])
            ACT(st4[:, u, 4:5], sinkc[:, l:l + 1], AF.Exp, [("sinkc", l), ("st4n", u)], [("st4e", u)], bias=st4[:, u, 2:3])
            TT("dve", st4[:, u, 3:4], st4[:, u, 3:4], st4[:, u, 4:5], ALU.add, [("st4r", u), ("st4e", u)], [("st4r", u)])
            S.add("dve", lambda e, u=u: e.reciprocal(out=st4[:, u, 3:4], in_=st4[:, u, 3:4]), [("st4r", u)], [("st4r", u)])
            TSC("dve", Pb[:, u, 0:132], Pb[:, u, 0:132], st4[:, u, 3:4], None, ALU.mult, None, [("Pb", u), ("st4r", u)], [("Pb", u)])
            tr(psT[:, i * 128:(i + 1) * 128], Pb[:, u, 0:128], ident_b[:, :], [("Pb", u), "ident_b"], PT_ALL)
            tr(psT[0:4, 512 + i * 128:512 + (i + 1) * 128], Pb[:, u, 128:132], ident_b[:, :], [("Pb", u), "ident_b"], PT_ALL)
        CP("act", PT_all[:, :], psT[:, 0:512], PT_ALL, ["PT_all"])
        CP("dve", PTn[0:4, :], psT[0:4, 512:1024], PT_ALL + ["PTn"], ["PTn"])
        for s in range(NSEQ):
            mm(ps[5][:, 32 * s:32 * s + 32], Vc_bf[:, s, :], PT_all[:, 32 * s:32 * s + 32], True, False, ["Vc_bf", "PT_all"], [("ps", 5)])
            mm(ps[5][:, 32 * s:32 * s + 32], vnt[0:4, s, :], PTn[0:4, 32 * s:32 * s + 32], False, True, ["vnt", "PTn"], [("ps", 5)])
        ov = ps[5][:, :].rearrange("p (s k) -> p s k", s=16)
        for X in range(4):
            for hk in range(2):
                sl = slice(hk * 64, (hk + 1) * 64)
                CP(cp_eng(), yb[sl, X, 0:64].rearrange("p (s t) -> p s t", t=4), ov[sl, :, X * 8 + hk * 4:X * 8 + hk * 4 + 4], [("ps", 5)], [("yb", X, 0)])

    def ffn(seg, l):
        N, T, nseq = seg.N, seg.T, seg.nseq
        actv = act_s if seg.sample else act_p
        rmsnorm_to_h(seg, l, 2)
        WARM(AF.Silu)
        if seg.sample:
            load_prev_s(l)
        pend_gate = []
        for gi in range(11):
            wv, wt = wget(l, 15 + gi)
            for pi in range(2):
                ca = gi * 2 + pi
                AB = (0, 1)
                cidx = [ca, ca + 22]
                ui = [(ca % 2) * 2, (ca % 2) * 2 + 1]
                uv = [v3(ust[:, ui[ab], :], seg, 2) for ab in AB]
                c1 = [v3(ct[:, ui[ab], :], seg) for ab in AB]
                bks = []
                for ab in AB:
                    bk = mm_next()
                    bks.append(bk)
                    for kc in range(8):
                        mm(ps[bk][:, :N], wv[:, kc, (ab * 2 + pi) * 128:(ab * 2 + pi + 1) * 128], h[:, kc, :N], kc == 0, kc == 7, wt + [("h", kc)], [("ps", bk)])
                for ab in AB:
                    if seg.sample:
                        CP("pool", uv[ab][:, :, 0:2], prev_s[:, cidx[ab], :].rearrange("p (s j) -> p s j", j=2), [("prev_s", cidx[ab])], [("ust", ui[ab])])
                    else:
                        CP("pool", uv[ab][:, :, 0:2], prev_p[:, l, cidx[ab]:cidx[ab] + 1, :], [("prev_p", l, cidx[ab])], [("ust", ui[ab])])
                for ab in AB:
                    CP("act", uv[ab][:, :, 2:2 + T], v3(ps[bks[ab]], seg), [("ps", bks[ab])], [("ust", ui[ab])])
                w = lambda i, ab: cfa[:, cidx[ab], l * 3 + i:l * 3 + i + 1]
                for ab in AB:
                    ACT(c1[ab], uv[ab][:, :, 0:T], AF.Identity, [("ust", ui[ab]), "cfa"], [("ct", ui[ab])], scale=w(0, ab), bias=cfa[:, cidx[ab], 6 + l:7 + l])
                for ab in AB:
                    STT(c1[ab], uv[ab][:, :, 1:T + 1], w(1, ab), c1[ab], ALU.mult, ALU.add, [("ust", ui[ab]), "cfa", ("ct", ui[ab])], [("ct", ui[ab])])
                for ab in AB:
                    STT(c1[ab], uv[ab][:, :, 2:T + 2], w(2, ab), c1[ab], ALU.mult, ALU.add, [("ust", ui[ab]), "cfa", ("ct", ui[ab])], [("ct", ui[ab])])
                for ab in AB:
                    if seg.sample:
                        CP("pool", prev_s[:, cidx[ab], :].rearrange("p (s j) -> p s j", j=2), uv[ab][:, :, T:T + 2], [("ust", ui[ab])], [("prev_s", cidx[ab])])
                    else:
                        CP("pool", prev_p[:, l, cidx[ab]:cidx[ab] + 1, :], uv[ab][:, :, T:T + 2], [("ust", ui[ab])], [("prev_p", l, cidx[ab])])
                if pend_gate:
                    pend_gate.pop()()

                def gate_fn(ua=ui[0], ub=ui[1], ca=ca):
                    ACT(ct[:, ua, :N], ct[:, ua, :N], AF.Silu, [("ct", ua)], [("ct", ua)])
                    TT("dve", actv[:, ca, :N], ct[:, ua, :N], ct[:, ub, :N], ALU.mult, [("ct", ua), ("ct", ub)], [("act", ca)])
                pend_gate.append(gate_fn)
        if pend_gate:
            pend_gate.pop()()
        WARM(AF.Sqrt)
        _chk("up")
        gi = 26
        for hf in range(2):
            banks = [mm_next() for _ in range(4)]
            k0 = 0
            for part, nk in enumerate((8, 8, 6)):
                wv, wt = wget(l, gi)
                gi += 1
                for mi in range(4):
                    for kc in range(nk):
                        mm(ps[banks[mi]][:, :N], wv[:, kc, mi * 128:(mi + 1) * 128], actv[:, k0 + kc, :N], (part == 0 and kc == 0),
                           (part == 2 and kc == nk - 1), wt + [("act", k0 + kc)], [("ps", banks[mi])])
                k0 += nk
            for mi in range(4):
                out_proj_evac(seg, banks[mi], hf * 4 + mi)
        _chk("down")
        post_norm_add(seg, l, 3)
        if seg.sample:
            store_prev_s(l)
        elif seg.last:
            cols_to_rows(lambda c: prev_p[:, l, c, :], 2, 2 * DFF, [(0, 2, ffp_d[l])], [("prev_p", l, c) for c in range(44)])

    rowst = mo[:, :, :].rearrange("p a b -> p (a b)")

    def load_prev_s(l):
        src = sff_d[l].rearrange("s j f -> (s j) f")
        for part, (c0, n) in enumerate(((0, 32), (32, 12))):
            S.dma(rowst[0:32, 0:n * 128], src[:, c0 * 128:(c0 + n) * 128], writes=MO_ALL)
            for cc0 in range(0, n, 16):
                bk = at_next()
                nn = min(16, n - cc0)
                for c in range(cc0, cc0 + nn):
                    tr(ps[bk][:, (c - cc0) * 32:(c - cc0 + 1) * 32], rowst[0:32, c * 128:(c + 1) * 128], ident_f[0:32, 0:32], MO_ALL + ["ident_f"], [("ps", bk)])
                CP(cp_eng(), prev_s[:, c0 + cc0:c0 + cc0 + nn, :], ps[bk][:, 0:nn * 32].rearrange("p (c r) -> p c r", r=32), [("ps", bk)],
                   [("prev_s", c) for c in range(c0 + cc0, c0 + cc0 + nn)])

    def store_prev_s(l):
        dst = ffs_d[l].rearrange("s j f -> (s j) f")
        for part, (c0, n) in enumerate(((0, 32), (32, 12))):
            for cc0 in range(0, n, 4):
                bk = at_next()
                for c in range(cc0, cc0 + 4):
                    tr(ps[bk][0:32, (c - cc0) * 128:(c - cc0 + 1) * 128], prev_s[:, c0 + c, :], ident_f[:, :], [("prev_s", c0 + c), "ident_f"], [("ps", bk)])
                CP(cp_eng(), rowst[0:32, cc0 * 128:(cc0 + 4) * 128], ps[bk][0:32, 0:512], [("ps", bk)], MO_ALL)
            S.dma(dst[:, c0 * 128:(c0 + n) * 128], rowst[0:32, 0:n * 128], reads=MO_ALL)

    try:
        for si, seg in enumerate(segs):
            if seg.sample:
                S.barrier()
            load_x(seg, skip_first_dma=(si > 0 and not seg.sample))
            _chk("load")
            for l in range(L):
                mixer(seg, l)
                _chk("mixer")
                ffn(seg, l)
                _chk("ffn")
            if si + 1 < len(segs) and not segs[si + 1].sample:
                preload_x0(segs[si + 1])
            store_x(seg)
            _chk("tile")
    except _Stop:
        pass

    stats = S.finalize()
    stats['lane_max'] = max(S.lane_cnt)
    stats['sbuf_left'] = nc.sbuf_bytes_remaining
    es.close()
    return nc, stats


_CACHE = {}


def kernel(x_prompt, x_sample, state_conv_a, cache_win_k, cache_win_v, state_ffn_conv,
           w_in, conv_a_w, attn_sinks, spatial_w, spatial_b, g_v, w_branch_a, w_branch_b,
           w_branch_c, w_out, g_pre_mix, g_post_mix, g_pre_ffn, g_post_ffn, w_up,
           conv_ffn_w, conv_ffn_b, w_down):
    f = lambda a: np.ascontiguousarray(np.asarray(a, dtype=np.float32))
    if "nc" not in _CACHE:
        _CACHE["nc"] = build_program()
    nc, stats = _CACHE["nc"]
    shared = {
        "w_in": f(w_in), "conv_a_w": f(conv_a_w), "attn_sinks": f(attn_sinks), "spatial_w": f(spatial_w),
        "spatial_b": f(spatial_b), "g_v": f(g_v), "w_branch_a": f(w_branch_a), "w_branch_b": f(w_branch_b),
        "w_branch_c": f(w_branch_c), "w_out": f(w_out), "g_pre_mix": f(g_pre_mix), "g_post_mix": f(g_post_mix),
        "g_pre_ffn": f(g_pre_ffn), "g_post_ffn": f(g_post_ffn), "w_up": f(w_up), "conv_ffn_w": f(conv_ffn_w),
        "conv_ffn_b": f(conv_ffn_b), "w_down": f(w_down),
    }
    xp = np.asarray(x_prompt, dtype=np.float32)
    xs = np.asarray(x_sample, dtype=np.float32)
    sca = np.asarray(state_conv_a, dtype=np.float32)
    ck = np.asarray(cache_win_k, dtype=np.float32)
    cv = np.asarray(cache_win_v, dtype=np.float32)
    sff = np.asarray(state_ffn_conv, dtype=np.float32)
    in_maps = []
    for c in range(8):
        b, half = c // 2, c % 2
        blk0 = 0 if half == 0 else 14
        m = dict(shared)
        m["xp"] = f(xp[b, blk0 * 128:(blk0 + NBLK) * 128, :])
        sl = slice(c * NSEQ, (c + 1) * NSEQ)
        m["xs"] = f(xs[sl].reshape(NS, D))
        m["sca"] = f(sca[:, sl])
        m["ck"] = f(ck[:, sl].reshape(L, NSEQ, 128, 128))
        m["cv"] = f(cv[:, sl].reshape(L, NSEQ, 128, 128))
        m["sff"] = f(sff[:, sl])
        in_maps.append(m)
    res = run_bass_kernel_spmd(nc, in_maps, core_ids=list(range(8)))
    R = res.results
    B = 4
    y_prompt = np.zeros((B, 4096, D), np.float32)
    ca_p = np.zeros((L, B, 2, 256), np.float32)
    wk_p = np.zeros((L, B, 128, 2, 64), np.float32)
    wv_p = np.zeros((L, B, 128, 2, 64), np.float32)
    ff_p = np.zeros((L, B, 2, 2 * DFF), np.float32)
    y_sample = np.zeros((128, TS_, D), np.float32)
    ca_s = np.zeros((L, 128, 2, 256), np.float32)
    wk_s = np.zeros((L, 128, 128, 2, 64), np.float32)
    wv_s = np.zeros((L, 128, 128, 2, 64), np.float32)
    ff_s = np.zeros((L, 128, 2, 2 * DFF), np.float32)
    cv_s = np.zeros((L, 128, TS_, 256), np.float32)
    for c in range(8):
        b, half = c // 2, c % 2
        r = R[c]
        if half == 0:
            y_prompt[b, 0:17 * 128] = r["yp"][0:17 * 128]
        else:
            y_prompt[b, 17 * 128:] = r["yp"][3 * 128:]
            ca_p[:, b] = r["ca_p"]
            wk_p[:, b] = r["wk_p"].reshape(L, 128, 2, 64)
            wv_p[:, b] = r["wv_p"].reshape(L, 128, 2, 64)
            ff_p[:, b] = r["ff_p"]
        sl = slice(c * NSEQ, (c + 1) * NSEQ)
        y_sample[sl] = r["ys"].reshape(NSEQ, TS_, D)
        ca_s[:, sl] = r["ca_s"]
        wk_s[:, sl] = r["wk_s"].reshape(L, NSEQ, 128, 2, 64)
        wv_s[:, sl] = r["wv_s"].reshape(L, NSEQ, 128, 2, 64)
        ff_s[:, sl] = r["ff_s"]
        cv_s[:, sl] = r["cv_s"]
    return (y_prompt, y_sample, ca_p, wk_p, wv_p, ff_p, ca_s, wk_s, wv_s, ff_s, cv_s)
```
